# Optimizing a Trainium2 kernel written in Bass

```python
import math
import jax, jax.numpy as jnp
from jax import lax
import numpy as np

D_MODEL = 2048
BATCH = 4
SEQ = 2048
DEPTH = 2
DEC_BATCH = 2
DEC_SEQ = 4096
PAST_LEN = 128

HEAD_DIM = 128
RET_HEADS = 4
RET_WIDTH = RET_HEADS * HEAD_DIM
RET_CHUNK = 128
MLA_HEADS = 4
MLA_Q_RANK = 384
MLA_KV_RANK = 256
MLA_NOPE = 128
MLA_ROPE = 64
MLA_QK = MLA_NOPE + MLA_ROPE
MLA_V = 128
GQA_HEADS = 8
GQA_KV_HEADS = 2
MIX_WIDTH = RET_WIDTH + MLA_HEADS * MLA_V + GQA_HEADS * HEAD_DIM
IN_WIDTH = 4 * RET_WIDTH + MLA_Q_RANK + MLA_KV_RANK + MLA_ROPE + (GQA_HEADS + 2 * GQA_KV_HEADS) * HEAD_DIM
D_FF = 4 * D_MODEL
GRID_W = 64
Q_BLOCK = 128
ROPE_THETA = 10000.0
EPS = 1e-6

kernel_name = 'hybrid_retention_mla_axial_gqa_encoder'


def rmsnorm(x, w):
    xf = x.astype(jnp.float32)
    y = xf * lax.rsqrt(jnp.mean(xf * xf, axis=-1, keepdims=True) + EPS)
    return (y * w.astype(jnp.float32)).astype(x.dtype)


def rope_1d(S, dim):
    inv = 1.0 / (ROPE_THETA ** (jnp.arange(0, dim, 2, dtype=jnp.float32) / dim))
    ang = jnp.arange(S, dtype=jnp.float32)[:, None] * inv[None, :]
    return jnp.cos(ang), jnp.sin(ang)


def rope_axial(S, dim):
    rows = S // GRID_W
    t = jnp.arange(S)
    row = jnp.repeat(jnp.arange(rows), GRID_W, total_repeat_length=S).astype(jnp.float32)
    col = (t % GRID_W).astype(jnp.float32)
    half = dim // 2
    inv = 1.0 / (ROPE_THETA ** (jnp.arange(0, half, 2, dtype=jnp.float32) / half))
    ang = jnp.concatenate([row[:, None] * inv[None, :], col[:, None] * inv[None, :]], axis=-1)
    return jnp.cos(ang), jnp.sin(ang)


def apply_rope(x, cos, sin):
    d2 = x.shape[-1] // 2
    xf = x.astype(jnp.float32)
    x1, x2 = xf[..., :d2], xf[..., d2:]
    c, s = cos[None, :, None, :], sin[None, :, None, :]
    return jnp.concatenate([x1 * c - x2 * s, x1 * s + x2 * c], axis=-1).astype(x.dtype)


def retention_one_direction(q, k, v, log_gamma, inclusive):
    B, H, S, D = q.shape
    E = v.shape[-1]
    C = RET_CHUNK
    N = S // C
    f32 = jnp.float32
    qc = q.reshape(B, H, N, C, D).astype(f32)
    kc = k.reshape(B, H, N, C, D).astype(f32)
    vc = v.reshape(B, H, N, C, E).astype(f32)
    pos = jnp.arange(C, dtype=f32)
    diff = pos[:, None] - pos[None, :]
    mask = diff >= 0 if inclusive else diff > 0
    decay_intra = jnp.where(mask[None], jnp.exp(jnp.where(mask, diff, 0.0)[None] * log_gamma[:, None, None]), 0.0)
    scores = jnp.einsum('bhncd,bhnmd->bhncm', qc, kc) * decay_intra[None, :, None]
    intra = jnp.einsum('bhncm,bhnme->bhnce', scores, vc)
    zeta = jnp.exp((C - 1 - pos)[None, :] * log_gamma[:, None])
    chunk_kv = jnp.einsum('bhncd,hc,bhnce->nbhde', kc, zeta, vc)
    chunk_decay = jnp.exp(C * log_gamma)[None, :, None, None]

    def step(state, kv_n):
        return chunk_decay * state + kv_n, state

    _, prev = lax.scan(step, jnp.zeros((B, H, D, E), f32), chunk_kv)
    inner = jnp.exp((pos + 1.0)[None, :] * log_gamma[:, None])
    cross = jnp.einsum('bhncd,hc,nbhde->bhnce', qc, inner, prev)
    return (intra + cross).reshape(B, H, S, E)


def blocked_attention(q, k, v, scale):
    B, S, Hq, D = q.shape
    Hkv = k.shape[2]
    G = Hq // Hkv
    E = v.shape[-1]
    nb = S // Q_BLOCK
    qb = q.reshape(B, nb, Q_BLOCK, Hkv, G, D).transpose(1, 0, 2, 3, 4, 5)

    def one_block(q_blk):
        s = jnp.einsum('bqhgd,bkhd->bhgqk', q_blk, k, preferred_element_type=jnp.float32) * scale
        p = jax.nn.softmax(s, axis=-1).astype(v.dtype)
        return jnp.einsum('bhgqk,bkhe->bqhge', p, v)

    o = lax.map(one_block, qb)
    return o.transpose(1, 0, 2, 3, 4, 5).reshape(B, S, Hq * E)


def encoder_layer(x, ret_cs, ax128_cs, ax64_cs, ln1_w, w_in, ret_decay_fwd, ret_decay_bwd, ret_gn_w,
                  mla_q_a_norm, mla_w_uq, mla_kv_a_norm, mla_w_ukv, mla_q_norm, mla_k_norm,
                  gqa_q_norm, gqa_k_norm, w_out, ln2_w, w_up, w_down):
    B, S, _ = x.shape
    f32 = jnp.float32
    h = rmsnorm(x, ln1_w)
    proj = h @ w_in
    sizes = (RET_WIDTH, RET_WIDTH, RET_WIDTH, RET_WIDTH, MLA_Q_RANK, MLA_KV_RANK, MLA_ROPE,
             GQA_HEADS * HEAD_DIM, GQA_KV_HEADS * HEAD_DIM, GQA_KV_HEADS * HEAD_DIM)
    offsets = np.cumsum(sizes)[:-1].tolist()
    rq, rk, rv, rg, cq, ckv, krope, gq, gk, gv = jnp.split(proj, offsets, axis=-1)

    cos_r, sin_r = ret_cs
    rq = apply_rope(rq.reshape(B, S, RET_HEADS, HEAD_DIM), cos_r, sin_r).transpose(0, 2, 1, 3)
    rk = (apply_rope(rk.reshape(B, S, RET_HEADS, HEAD_DIM), cos_r, sin_r) * (HEAD_DIM ** -0.5)).transpose(0, 2, 1, 3)
    rv = rv.reshape(B, S, RET_HEADS, HEAD_DIM).transpose(0, 2, 1, 3)
    log_gf = jax.nn.log_sigmoid(ret_decay_fwd.astype(f32))
    log_gb = jax.nn.log_sigmoid(ret_decay_bwd.astype(f32))
    y_f = retention_one_direction(rq, rk, rv, log_gf, True)
    y_b = jnp.flip(retention_one_direction(jnp.flip(rq, 2), jnp.flip(rk, 2), jnp.flip(rv, 2), log_gb, False), 2)
    y = (y_f + y_b).transpose(0, 2, 1, 3)
    mu = jnp.mean(y, axis=-1, keepdims=True)
    var = jnp.mean(jnp.square(y - mu), axis=-1, keepdims=True)
    y = ((y - mu) * lax.rsqrt(var + EPS)).reshape(B, S, RET_WIDTH)
    ret_out = (y * ret_gn_w.astype(f32) * jax.nn.silu(rg.astype(f32))).astype(x.dtype)

    cos64, sin64 = ax64_cs
    q_m = (rmsnorm(cq, mla_q_a_norm) @ mla_w_uq).reshape(B, S, MLA_HEADS, MLA_QK)
    kv_m = (rmsnorm(ckv, mla_kv_a_norm) @ mla_w_ukv).reshape(B, S, MLA_HEADS, MLA_NOPE + MLA_V)
    k_nope, v_m = kv_m[..., :MLA_NOPE], kv_m[..., MLA_NOPE:]
    k_rope = jnp.broadcast_to(krope[:, :, None, :], (B, S, MLA_HEADS, MLA_ROPE))
    k_m = jnp.concatenate([k_nope, k_rope], axis=-1)
    q_m = rmsnorm(q_m, mla_q_norm)
    k_m = rmsnorm(k_m, mla_k_norm)
    q_m = jnp.concatenate([q_m[..., :MLA_NOPE], apply_rope(q_m[..., MLA_NOPE:], cos64, sin64)], axis=-1)
    k_m = jnp.concatenate([k_m[..., :MLA_NOPE], apply_rope(k_m[..., MLA_NOPE:], cos64, sin64)], axis=-1)
    mla_out = blocked_attention(q_m, k_m, v_m, MLA_QK ** -0.5)

    cos128, sin128 = ax128_cs
    q_g = apply_rope(rmsnorm(gq.reshape(B, S, GQA_HEADS, HEAD_DIM), gqa_q_norm), cos128, sin128)
    k_g = apply_rope(rmsnorm(gk.reshape(B, S, GQA_KV_HEADS, HEAD_DIM), gqa_k_norm), cos128, sin128)
    v_g = gv.reshape(B, S, GQA_KV_HEADS, HEAD_DIM)
    gqa_out = blocked_attention(q_g, k_g, v_g, HEAD_DIM ** -0.5)

    mix = jnp.concatenate([ret_out, mla_out.astype(x.dtype), gqa_out.astype(x.dtype)], axis=-1)
    x = x + mix @ w_out

    h2 = rmsnorm(x, ln2_w)
    u = jnp.square(jax.nn.relu(h2 @ w_up))
    return x + u @ w_down


def setup_inputs(seed: int = 0) -> dict:
    key = jax.random.key(seed)
    ks = jax.random.split(key, 20)
    f32 = jnp.float32

    def normal(k, shape, scale):
        return jax.random.normal(k, shape, f32) * scale

    def gain(k, shape):
        return 1.0 + 0.02 * jax.random.normal(k, shape, f32)

    gamma0 = 1.0 - 2.0 ** (-5.0 - jnp.arange(RET_HEADS, dtype=f32))
    logit0 = jnp.log(gamma0) - jnp.log1p(-gamma0)
    return {
        'x_prompt': jax.random.normal(ks[0], (BATCH, SEQ, D_MODEL), f32),
        'x_sample': jax.random.normal(ks[1], (DEC_BATCH, DEC_SEQ, D_MODEL), f32),
        'ln1_w': gain(ks[2], (DEPTH, D_MODEL)),
        'w_in': normal(ks[3], (DEPTH, D_MODEL, IN_WIDTH), D_MODEL ** -0.5),
        'ret_decay_fwd': logit0[None, :] + 0.1 * jax.random.normal(ks[4], (DEPTH, RET_HEADS), f32),
        'ret_decay_bwd': logit0[None, :] + 0.1 * jax.random.normal(ks[5], (DEPTH, RET_HEADS), f32),
        'ret_gn_w': gain(ks[6], (DEPTH, RET_WIDTH)),
        'mla_q_a_norm': gain(ks[7], (DEPTH, MLA_Q_RANK)),
        'mla_w_uq': normal(ks[8], (DEPTH, MLA_Q_RANK, MLA_HEADS * MLA_QK), MLA_Q_RANK ** -0.5),
        'mla_kv_a_norm': gain(ks[9], (DEPTH, MLA_KV_RANK)),
        'mla_w_ukv': normal(ks[10], (DEPTH, MLA_KV_RANK, MLA_HEADS * (MLA_NOPE + MLA_V)), MLA_KV_RANK ** -0.5),
        'mla_q_norm': gain(ks[11], (DEPTH, MLA_QK)),
        'mla_k_norm': gain(ks[12], (DEPTH, MLA_QK)),
        'gqa_q_norm': gain(ks[13], (DEPTH, HEAD_DIM)),
        'gqa_k_norm': gain(ks[14], (DEPTH, HEAD_DIM)),
        'w_out': normal(ks[15], (DEPTH, MIX_WIDTH, D_MODEL), MIX_WIDTH ** -0.5),
        'ln2_w': gain(ks[16], (DEPTH, D_MODEL)),
        'w_up': normal(ks[17], (DEPTH, D_MODEL, D_FF), D_MODEL ** -0.5),
        'w_down': normal(ks[18], (DEPTH, D_FF, D_MODEL), D_FF ** -0.5),
    }


def reference(x_prompt, x_sample, ln1_w, w_in, ret_decay_fwd, ret_decay_bwd, ret_gn_w,
              mla_q_a_norm, mla_w_uq, mla_kv_a_norm, mla_w_ukv, mla_q_norm, mla_k_norm,
              gqa_q_norm, gqa_k_norm, w_out, ln2_w, w_up, w_down):
    def trunk(x):
        S = x.shape[1]
        ret_cs = rope_1d(S, HEAD_DIM)
        ax128_cs = rope_axial(S, HEAD_DIM)
        ax64_cs = rope_axial(S, MLA_ROPE)
        for l in range(DEPTH):
            x = encoder_layer(x, ret_cs, ax128_cs, ax64_cs, ln1_w[l], w_in[l], ret_decay_fwd[l], ret_decay_bwd[l],
                              ret_gn_w[l], mla_q_a_norm[l], mla_w_uq[l], mla_kv_a_norm[l], mla_w_ukv[l],
                              mla_q_norm[l], mla_k_norm[l], gqa_q_norm[l], gqa_k_norm[l], w_out[l],
                              ln2_w[l], w_up[l], w_down[l])
        return x

    y_prompt = trunk(x_prompt)
    y_sample = trunk(x_sample)
    return (y_prompt, y_sample)
```

```python
import numpy as np
from contextlib import ExitStack
import concourse.bass as bass
import concourse.mybir as mybir
from concourse.bass_utils import run_bass_kernel_spmd

F32 = mybir.dt.float32
BF16 = mybir.dt.bfloat16
AF = mybir.ActivationFunctionType
ALU = mybir.AluOpType
AX = mybir.AxisListType

NCORES = 8
T = 2048
NT = T // 128
DM = 2048
DEPTH = 2
INW = 4288
DFF = 8192
EPS = 1e-6
NEG = -30000.0

GROUPS = [(0, 512), (512, 512), (1024, 512), (1536, 512), (2048, 384), (2432, 320),
          (2752, 512), (3264, 512), (3776, 512)]

import os
KDBG = int(os.environ.get('KDBG', '0'))
STOP_AFTER = None
DEBUG_OUT = []
DEBUG_CORES = None


class Sem:
    def __init__(self, handle, name):
        self.h = handle
        self.name = name
        self.val = 0


class Buf:
    def __init__(self, name, t=None):
        self.name = name
        self.t = t
        self.last_w = None
        self.readers = []
        self.dsem = None
        self.is_psum = False

    def __getitem__(self, k):
        return self.t[k]


class Eng:
    def __init__(self, fw, name, eng):
        self.fw = fw
        self.name = name
        self.e = eng
        self.sem = fw.new_sem("e_" + name)
        self.waited = {}

    def _need(self, ev):
        if ev is None:
            return
        s, v = ev
        if self.waited.get(s, 0) >= v:
            return
        self.e.wait_ge(s.h, v)
        self.waited[s] = v

    def sync(self, reads, writes):
        need = {}

        def add(ev):
            if ev is None:
                return
            s_, v_ = ev
            if need.get(s_, 0) < v_:
                need[s_] = v_
        for b in reads:
            add(b.last_w)
            if b.is_psum:
                for ev in b.readers:
                    if ev[0] is not self.sem:
                        add(ev)
        for b in writes:
            add(b.last_w)
            for ev in b.readers:
                if ev[0] is self.sem:
                    continue
                add(ev)
        for s_, v_ in need.items():
            self._need((s_, v_))

    def _commit(self, ev, reads, writes):
        for b in writes:
            b.last_w = ev
            b.readers = []
        for b in reads:
            if b not in writes:
                b.readers.append(ev)
                if len(b.readers) > 16:
                    best = {}
                    for s, v in b.readers:
                        if best.get(s, 0) < v:
                            best[s] = v
                    b.readers = list(best.items())

    def op(self, fn, reads=(), writes=()):
        self.sync(reads, writes)
        ins = fn(self.e)
        ins.then_inc(self.sem.h, 1)
        self.sem.val += 1
        self._commit((self.sem, self.sem.val), reads, writes)

    def dma(self, out, in_, reads=(), writes=(), sem=None):
        self.sync(reads, writes)
        if sem is None:
            b = (list(writes) + list(reads))[0]
            if b.dsem is None:
                b.dsem = {}
            if self.name not in b.dsem:
                b.dsem[self.name] = self.fw.get_dsem(b.name, self.name)
            sem = b.dsem[self.name]
        outs = out if isinstance(out, (list, tuple)) else [out]
        ins = in_ if isinstance(in_, (list, tuple)) else [in_]
        for o, i in zip(outs, ins):
            self.e.dma_start(out=o, in_=i).then_inc(sem.h, 16)
            sem.val += 16
        self._commit((sem, sem.val), reads, writes)


class FW:
    def __init__(self, nc, stack):
        self.nc = nc
        self.stack = stack
        self.sems = []
        self.bufs = []
        self.uid = 0
        self.free_dsems = {}
        self.pe = Eng(self, "pe", nc.tensor)
        self.act = Eng(self, "act", nc.scalar)
        self.dve = Eng(self, "dve", nc.vector)
        self.pool = Eng(self, "pool", nc.gpsimd)
        self.sp = Eng(self, "sp", nc.sync)
        self.engs = [self.pe, self.act, self.dve, self.pool, self.sp]

    def new_sem(self, name):
        self.uid += 1
        name = f"{name}_{self.uid}"
        h = self.stack.enter_context(self.nc.semaphore(name))
        s = Sem(h, name)
        self.sems.append(s)
        return s

    def get_dsem(self, name, qname):
        fl = self.free_dsems.setdefault(qname, [])
        if fl:
            return fl.pop()
        return self.new_sem("d_" + qname + "_" + name)

    def buf(self, name, t=None):
        b = Buf(name, t)
        self.bufs.append(b)
        return b

    def sbuf(self, stack, name, shape, dtype):
        self.uid += 1
        t = stack.enter_context(self.nc.sbuf_tensor(f"{name}_{self.uid}", list(shape), dtype))
        return self.buf(name, t)

    def psum(self, stack, name, shape, dtype):
        self.uid += 1
        t = stack.enter_context(self.nc.psum_tensor(f"{name}_{self.uid}", list(shape), dtype))
        b = self.buf(name, t)
        b.is_psum = True
        return b

    def barrier(self):
        for e in self.engs:
            for s in self.sems:
                if s.val > 0 and e.waited.get(s, 0) < s.val:
                    e.e.wait_ge(s.h, s.val)
                    e.waited[s] = s.val
        for b in self.bufs:
            b.last_w = None
            b.readers = []
        keep = []
        for b in self.bufs:
            if getattr(b, "persist", False):
                keep.append(b)
            elif b.dsem:
                for qn, sm in b.dsem.items():
                    self.free_dsems.setdefault(qn, []).append(sm)
                b.dsem = None
        self.bufs = keep


class Pipe:
    def __init__(self):
        self.items = []

    def tile(self, nstages):
        st = [[] for _ in range(nstages)]
        self.items.append(st)
        return st

    def run(self):
        n = len(self.items)
        if n == 0:
            return
        K = max(len(s) for s in self.items)
        for it in range(n + K - 1):
            for s in range(K - 1, -1, -1):
                j = it - s
                if 0 <= j < n and s < len(self.items[j]):
                    for th in self.items[j][s]:
                        th()


class Rot:
    def __init__(self, items):
        self.items = items
        self.i = 0

    def next(self):
        r = self.items[self.i % len(self.items)]
        self.i += 1
        return r


def build_program():
    nc = bass.Bass("TRN2", target_bir_lowering=False)
    dbg = set(DEBUG_OUT)

    def din(name, shape, dt=F32):
        return nc.dram_tensor(name, list(shape), dt, kind="ExternalInput")

    def dscr(name, shape, dt):
        if name in dbg:
            return nc.dram_tensor(name, list(shape), dt, kind="ExternalOutput")
        return nc.dram_tensor(name, list(shape), dt)

    x_in = din("x", [T, DM])
    w_in = din("w_in", [DEPTH, DM, INW])
    w_out = din("w_out", [DEPTH, DM, DM])
    early = STOP_AFTER is not None and STOP_AFTER.endswith("0") and not STOP_AFTER.startswith("mlp")
    w_up = None if early else din("w_up", [DEPTH, DM, DFF])
    w_down = None if early else din("w_down", [DEPTH, DFF, DM])
    w_uq = din("w_uq", [DEPTH, 384, 768])
    w_ukv = din("w_ukv", [DEPTH, 256, 1024])
    ln1 = din("ln1_w", [DEPTH, DM])
    ln2 = din("ln2_w", [DEPTH, DM])
    dec_f = din("dec_f", [DEPTH, 4])
    dec_b = din("dec_b", [DEPTH, 4])
    gn_w = din("gn_w", [DEPTH, 512])
    qa_n = din("qa_n", [DEPTH, 384])
    kva_n = din("kva_n", [DEPTH, 256])
    mq_n = din("mq_n", [DEPTH, 192])
    mk_n = din("mk_n", [DEPTH, 192])
    gq_n = din("gq_n", [DEPTH, 128])
    gk_n = din("gk_n", [DEPTH, 128])
    rt_r = din("rt_r", [T, 256])
    rt_rk = din("rt_rk", [T, 256])
    rt_g = din("rt_g", [T, 256])
    rt_m = din("rt_m", [T, 128])
    kbias = din("kbias", [1, 2])
    sscale = din("sscale", [1, 4])
    c_ident = din("c_ident", [128, 128])
    c_ret = din("c_ret", [128, 6, 128])
    c_zc = din("c_zc", [128, 2])
    y_out = nc.dram_tensor("y", [T, DM], F32, kind="ExternalOutput")

    XA = dscr("XA", [T, DM], F32)
    XB = dscr("XB", [T, DM], F32)
    RQT = dscr("RQT", [4, 128, T], BF16)
    RKT = dscr("RKT", [4, 128, T], BF16)
    RK = dscr("RK", [T, 512], BF16)
    RV = dscr("RV", [T, 512], BF16)
    RG = dscr("RG", [T, 512], F32)
    MQT = dscr("MQT", [4, 192, T], BF16)
    GQT = dscr("GQT", [8, 128, T], BF16)
    KROWS = [384, 384, 256]
    KGi_l = [[nc.dram_tensor(f"KGi{l}_{p}", [KROWS[p], T], BF16) for p in range(3)] for l in range(DEPTH)]
    KGo_l = [[nc.dram_tensor(f"KGo{l}_{p}", [2 * KROWS[p], T], BF16) for p in range(3)] for l in range(DEPTH)]
    VGi_l = [[nc.dram_tensor(f"VGi{l}_{p}", [T // 2, 768], BF16) for p in range(2)] for l in range(DEPTH)]
    VGo_l = [[nc.dram_tensor(f"VGo{l}_{p}", [T, 768], BF16) for p in range(2)] for l in range(DEPTH)]
    SG_in_l = [nc.dram_tensor(f"SG_in{l}", [256, 512], F32) for l in range(DEPTH)]
    SG_out_l = [nc.dram_tensor(f"SG_out{l}", [512, 512], F32) for l in range(DEPTH)]

    ncr = NCORES if DEBUG_CORES is None else len(DEBUG_CORES)
    RG_PAIRS = [[2 * i, 2 * i + 1] for i in range(ncr // 2)]
    KVD = nc.dram_tensor("KVD", [NT, 2, 128, 512], F32)

    with ExitStack() as top:
        fw = FW(nc, top)
        pe, act, dve, pool, sp = fw.pe, fw.act, fw.dve, fw.pool, fw.sp
        cc_sem = fw.new_sem("cc")

        def P(b):
            b.persist = True
            return b

        ident = P(fw.sbuf(top, "ident", [128, 128], BF16))
        ones = P(fw.sbuf(top, "ones", [128, 128], BF16))
        czc = P(fw.sbuf(top, "czc", [128, 2], F32))
        kb = P(fw.sbuf(top, "kb", [128, 2], F32))
        ssc = P(fw.sbuf(top, "ssc", [128, 4], F32))
        lg = P(fw.sbuf(top, "lg", [128, 8], F32))
        zfb = P(fw.sbuf(top, "zfb", [128, 8], F32))
        pool.dma(ident[:], c_ident.ap(), writes=[ident])
        sp.dma(czc[:], c_zc.ap(), writes=[czc])
        sp.dma(kb[:], kbias.ap()[0, :].partition_broadcast(128), writes=[kb])
        sp.dma(ssc[:], sscale.ap()[0, :].partition_broadcast(128), writes=[ssc])
        dve.op(lambda e: e.memset(ones[:], 1.0), writes=[ones])

        def stop_here(name):
            return STOP_AFTER == name

        def finish():
            fw.barrier()

        def layer_consts(L):
            with ExitStack() as ps:
                tmp = fw.sbuf(ps, "lc_tmp", [128, 8], F32)
                sp.dma([tmp[:, 0:4], tmp[:, 4:8]],
                       [dec_f.ap()[L, :].partition_broadcast(128), dec_b.ap()[L, :].partition_broadcast(128)],
                       writes=[tmp])
                act.op(lambda e: e.activation(out=tmp[:], in_=tmp[:], func=AF.Exp, scale=-1.0), reads=[tmp], writes=[tmp])
                act.op(lambda e: e.activation(out=tmp[:], in_=tmp[:], func=AF.Ln, bias=1.0), reads=[tmp], writes=[tmp])
                dve.op(lambda e: e.tensor_scalar(out=lg[:], in0=tmp[:], scalar1=-1.0, scalar2=None, op0=ALU.mult),
                       reads=[tmp], writes=[lg])
                fw.barrier()

        def phase_norm(src, lnvec, hT):
            with ExitStack() as ps:
                xt = Rot([fw.sbuf(ps, f"n_x{i}", [128, DM], F32) for i in range(3)])
                hn = Rot([fw.sbuf(ps, f"n_hn{i}", [128, DM], BF16) for i in range(3)])
                junk = fw.sbuf(ps, "n_junk", [128, DM], BF16)
                ss = Rot([fw.sbuf(ps, f"n_ss{i}", [128, 2], F32) for i in range(3)])
                pT = Rot([fw.psum(ps, f"n_pT{i}", [128, 1024], BF16) for i in range(6)])
                lnw = fw.sbuf(ps, "n_lnw", [128, DM], F32)
                sp.dma(lnw[:], lnvec.partition_broadcast(128), writes=[lnw])
                pipe = Pipe()

                def tile(t):
                    S_ = pipe.tile(5)
                    X = xt.next(); H = hn.next(); S = ss.next()
                    S_[0].append(lambda: sp.dma(X[:], src[t * 128:(t + 1) * 128, :], writes=[X]))
                    S_[1].append(lambda: act.op(lambda e: e.activation(out=junk[:], in_=X[:], func=AF.Square, accum_out=S[:, 0:1]),
                                                reads=[X], writes=[junk, S]))
                    S_[1].append(lambda: act.op(lambda e: e.activation(out=S[:, 1:2], in_=S[:, 0:1], func=AF.Sqrt, scale=1.0 / DM, bias=EPS),
                                                reads=[S], writes=[S]))
                    S_[2].append(lambda: dve.op(lambda e: e.reciprocal(out=S[:, 1:2], in_=S[:, 1:2]), reads=[S], writes=[S]))
                    S_[2].append(lambda: dve.op(lambda e: e.scalar_tensor_tensor(out=H[:], in0=X[:], scalar=S[:, 1:2], in1=lnw[:],
                                                                                 op0=ALU.mult, op1=ALU.mult),
                                                reads=[X, S, lnw], writes=[H]))
                    for half in range(2):
                        Pt = pT.next()

                        def trf(half=half, Pt=Pt):
                            def tr(e):
                                for j in range(8):
                                    k = half * 8 + j
                                    i = e.transpose(out=Pt[:, j * 128:(j + 1) * 128], in_=H[:, k * 128:(k + 1) * 128], identity=ident[:])
                                return i
                            pe.op(tr, reads=[H, ident], writes=[Pt])
                        S_[3].append(trf)
                        dst = hT.t[:, half * 8:(half + 1) * 8, t * 128:(t + 1) * 128]
                        src_ps = Pt[:, :].rearrange("p (k c) -> p k c", k=8)
                        if half == 0:
                            S_[4].append(lambda dst=dst, s_=src_ps, Pt=Pt: act.op(lambda e: e.activation(out=dst, in_=s_, func=AF.Copy), reads=[Pt], writes=[hT.tiles[t]]))
                        else:
                            S_[4].append(lambda dst=dst, s_=src_ps, Pt=Pt: dve.op(lambda e: e.tensor_copy(out=dst, in_=s_), reads=[Pt], writes=[hT.tiles[t]]))
                for t in range(NT):
                    tile(t)
                pipe.run()
                fw.barrier()

        def phase_inproj(L, hT):
            KGi, VGi = KGi_l[L], VGi_l[L]
            with ExitStack() as ps:
                wb = Rot([fw.sbuf(ps, f"b_w{i}", [128, 16, 512], BF16) for i in range(2)])
                tabb = fw.sbuf(ps, "b_tab", [128, NT, 256], F32)
                Xs = Rot([fw.sbuf(ps, f"b_X{i}", [128, 768], F32) for i in range(3)])
                tAs = Rot([fw.sbuf(ps, f"b_tA{i}", [128, 512], F32) for i in range(3)])
                tBs = Rot([fw.sbuf(ps, f"b_tB{i}", [128, 512], F32) for i in range(3)])
                Os = Rot([fw.sbuf(ps, f"b_O{i}", [128, 768], BF16) for i in range(3)])
                sts = Rot([fw.sbuf(ps, f"b_st{i}", [128, 16], F32) for i in range(6)])
                cT_all = fw.sbuf(ps, "b_cTall", [128, 5, T], BF16)
                kr_all = fw.sbuf(ps, "b_krall", [128, NT, 64], F32)
                cTq = [fw.buf(f"cTq{i}") for i in range(NT)]
                cTk = [fw.buf(f"cTk{i}") for i in range(NT)]
                kr_tiles = [fw.buf(f"krt{i}") for i in range(NT)]
                STa = Rot([fw.sbuf(ps, f"b_STa{i}", [128, 4, 512], BF16) for i in range(2)])
                STb = Rot([fw.sbuf(ps, f"b_STb{i}", [128, 4, 512], BF16) for i in range(2)])
                Vb = Rot([fw.sbuf(ps, f"b_V{i}", [128, 512], BF16) for i in range(3)])
                Gb = Rot([fw.sbuf(ps, f"b_G{i}", [128, 512], F32) for i in range(2)])
                junk = fw.sbuf(ps, "b_junk", [128, 384], BF16)
                wqa = fw.sbuf(ps, "wqa", [128, 384], F32)
                wkva = fw.sbuf(ps, "wkva", [128, 256], F32)
                wmq = fw.sbuf(ps, "wmq", [128, 192], F32)
                wmk = fw.sbuf(ps, "wmk", [128, 192], F32)
                wgq = fw.sbuf(ps, "wgq", [128, 128], F32)
                wgk = fw.sbuf(ps, "wgk", [128, 128], F32)
                wuq = fw.sbuf(ps, "wuq", [128, 3, 768], BF16)
                wukv = fw.sbuf(ps, "wukv", [128, 2, 1024], BF16)
                sp.dma(wqa[:], qa_n.ap()[L, :].partition_broadcast(128), writes=[wqa])
                sp.dma(wkva[:], kva_n.ap()[L, :].partition_broadcast(128), writes=[wkva])
                sp.dma(wmq[:], mq_n.ap()[L, :].partition_broadcast(128), writes=[wmq])
                sp.dma(wmk[:], mk_n.ap()[L, :].partition_broadcast(128), writes=[wmk])
                sp.dma(wgq[:], gq_n.ap()[L, :].partition_broadcast(128), writes=[wgq])
                sp.dma(wgk[:], gk_n.ap()[L, :].partition_broadcast(128), writes=[wgk])
                pool.dma(wuq[:], w_uq.ap()[L].rearrange("(k p) c -> p k c", p=128), writes=[wuq])
                pool.dma(wukv[:], w_ukv.ap()[L].rearrange("(k p) c -> p k c", p=128), writes=[wukv])
                wsrc = w_in.ap()[L].rearrange("(k p) c -> p k c", p=128)
                Wt_of = {}

                def load_w(g):
                    c0, ncol = GROUPS[g]
                    Wt = wb.next()
                    Wt_of[g] = Wt
                    pool.dma([Wt[:, 4 * q:4 * q + 4, 0:ncol] for q in range(4)],
                             [wsrc[:, 4 * q:4 * q + 4, c0:c0 + ncol] for q in range(4)], writes=[Wt])

                def load_tab(src, width):
                    sp.dma(tabb[:, :, 0:width], src.ap().rearrange("(n p) c -> p n c", p=128), writes=[tabb])

                def sq_stats(stg, src_of_h, H, D, st, reads):
                    for h in range(H):
                        stg.append(lambda h=h: act.op(lambda e: e.activation(out=junk[:, 0:D], in_=src_of_h(h), func=AF.Square, accum_out=st[:, h:h + 1]),
                                                      reads=reads, writes=[junk, st]))
                    stg.append(lambda: act.op(lambda e: e.activation(out=st[:, 8:8 + H], in_=st[:, 0:H], func=AF.Sqrt, scale=1.0 / D, bias=EPS),
                                              reads=[st], writes=[st]))

                def sq_stats_sb(stg, X2, X2v, sqA, sqB, st):
                    for half, sq in enumerate((sqA, sqB)):
                        sqv = sq[:, 0:384].rearrange("p (h d) -> p h d", h=2)
                        stg.append(lambda half=half, sq=sq, sqv=sqv: pool.op(
                            lambda e: e.tensor_tensor(out=sqv, in0=X2v[:, 2 * half:2 * half + 2, :], in1=X2v[:, 2 * half:2 * half + 2, :], op=ALU.mult),
                            reads=[X2], writes=[sq]))
                        stg.append(lambda half=half, sq=sq, sqv=sqv: dve.op(
                            lambda e: e.tensor_reduce(out=st[:, 2 * half:2 * half + 2], in_=sqv, axis=AX.X, op=ALU.add), reads=[sq], writes=[st]))
                    stg.append(lambda: act.op(lambda e: e.activation(out=st[:, 8:12], in_=st[:, 0:4], func=AF.Sqrt, scale=1.0 / 192, bias=EPS),
                                              reads=[st], writes=[st]))

                def norm_rope(stg_m, stg_a, X, H, D, st, normw, tap, roff, Dr, O, tA, tB, w_eng=None):
                    w_eng = w_eng or pool
                    Xv = X[:, 0:H * D].rearrange("p (h d) -> p h d", h=H)
                    Ov = O[:, 0:H * D].rearrange("p (h d) -> p h d", h=H)
                    Av = tA[:, 0:H * Dr].rearrange("p (h d) -> p h d", h=H)
                    Bv = tB[:, 0:H * Dr].rearrange("p (h d) -> p h d", h=H)
                    if normw is not None:
                        stg_m.append(lambda: dve.op(lambda e: e.reciprocal(out=st[:, 8:8 + H], in_=st[:, 8:8 + H]), reads=[st], writes=[st]))
                        stg_m.append(lambda: dve.op(lambda e: e.tensor_tensor(out=Xv, in0=Xv, in1=st[:, 8:8 + H].unsqueeze(2).to_broadcast([128, H, D]), op=ALU.mult),
                                                    reads=[X, st], writes=[X]))
                        stg_m.append(lambda: w_eng.op(lambda e: e.tensor_tensor(out=Xv, in0=Xv, in1=normw[:, 0:D].unsqueeze(1).to_broadcast([128, H, D]), op=ALU.mult),
                                                      reads=[X, normw], writes=[X]))
                    hf = Dr // 2
                    C2 = tap[:, 0:Dr].unsqueeze(1).to_broadcast([128, H, Dr])
                    S2a = tap[:, Dr:Dr + hf].unsqueeze(1).to_broadcast([128, H, hf])
                    S2b = tap[:, Dr + hf:2 * Dr].unsqueeze(1).to_broadcast([128, H, hf])
                    stg_m.append(lambda: dve.op(lambda e: e.tensor_tensor(out=Av, in0=Xv[:, :, roff:roff + Dr], in1=C2, op=ALU.mult),
                                                reads=[X, tabb], writes=[tA]))
                    stg_m.append(lambda: pool.op(lambda e: e.tensor_tensor(out=Bv[:, :, 0:hf], in0=Xv[:, :, roff + hf:roff + Dr], in1=S2a, op=ALU.mult),
                                                 reads=[X, tabb], writes=[tB]))
                    stg_m.append(lambda: pool.op(lambda e: e.tensor_tensor(out=Bv[:, :, hf:Dr], in0=Xv[:, :, roff:roff + hf], in1=S2b, op=ALU.mult),
                                                 reads=[X, tabb, tB], writes=[tB]))
                    stg_a.append(lambda: dve.op(lambda e: e.tensor_tensor(out=Ov[:, :, roff:roff + Dr], in0=Av, in1=Bv, op=ALU.add),
                                                reads=[tA, tB], writes=[O]))
                    if roff > 0:
                        stg_a.append(lambda: act.op(lambda e: e.activation(out=Ov[:, :, 0:roff], in_=Xv[:, :, 0:roff], func=AF.Copy), reads=[X, O], writes=[O]))

                def main_mm(stg, g, t, Pm):
                    c0, ncol = GROUPS[g]

                    def f():
                        Wt = Wt_of[g]

                        def mm(e):
                            for k in range(16):
                                i = e.matmul(Pm[:, 0:ncol], lhsT=hT.t[:, k, t * 128:(t + 1) * 128], rhs=Wt[:, k, 0:ncol],
                                             start=(k == 0), stop=(k == 15))
                            return i
                        pe.op(mm, reads=[hT.tiles[t], Wt], writes=[Pm])
                    stg.append(f)

                def tr_op(stg, O, H, D, Pt, parts):
                    Ptv = Pt[:, :].rearrange("p (s c) -> p s c", s=8)

                    def f():
                        def tr(e):
                            i = None
                            for h in range(H):
                                for (d0, dn, s0) in parts:
                                    i = e.transpose(out=Ptv[0:dn, s0 + h, :], in_=O[:, h * D + d0:h * D + d0 + dn], identity=ident[:])
                            return i
                        pe.op(tr, reads=[O, ident], writes=[Pt])
                    stg.append(f)
                    return Ptv

                def block_store(g, tb, sta, stb):
                    tc = slice(tb * 512, (tb + 1) * 512)
                    if g == 0:
                        sp.dma(RQT.ap().rearrange("h d t -> d h t")[:, :, tc], sta[:], reads=[sta])
                    elif g == 1:
                        sp.dma(RKT.ap().rearrange("h d t -> d h t")[:, :, tc], sta[:], reads=[sta])
                    elif g == 4:
                        mv = MQT.ap().rearrange("h d t -> d h t")
                        sp.dma(mv[0:128, :, tc], sta[:], reads=[sta])
                        sp.dma(mv[128:192, :, tc], stb[64:128, :, :], reads=[stb])
                    elif g == 5:
                        for pc in range(2):
                            kv_ = KGi[pc].ap().rearrange("(h d) t -> d h t", d=192)
                            sp.dma(kv_[0:128, :, tc], sta[:, 2 * pc:2 * pc + 2, :], reads=[sta])
                            sp.dma(kv_[128:192, :, tc], stb[64:128, 2 * pc:2 * pc + 2, :], reads=[stb])
                    elif g in (6, 7):
                        gv_ = GQT.ap().rearrange("h d t -> d h t")
                        sp.dma(gv_[:, (g - 6) * 4:(g - 6) * 4 + 4, tc], sta[:], reads=[sta])
                    elif g == 8:
                        kg = KGi[2].ap().rearrange("(h d) t -> d h t", d=128)
                        sp.dma(kg[:, :, tc], sta[:, 0:2, :], reads=[sta])

                def simple_tile(pipe, g, t, pmm, pT, sta):
                    j = t % 4
                    tb = t // 4
                    S = pipe.tile(6)
                    rows = slice(t * 128, (t + 1) * 128)
                    cols = slice(j * 128, (j + 1) * 128)
                    Pm = pmm.next()
                    if t == 0:
                        if g + 1 < len(GROUPS):
                            S[0].append(lambda: load_w(g + 1))
                        if g == 1:
                            S[2].append(lambda: load_tab(rt_rk, 256))
                        if g == 6:
                            S[2].append(lambda: load_tab(rt_g, 256))
                    main_mm(S[0], g, t, Pm)
                    if g == 2:
                        V = Vb.next()
                        S[1].append(lambda: act.op(lambda e: e.activation(out=V[:], in_=Pm[:, 0:512], func=AF.Copy), reads=[Pm], writes=[V]))
                        S[1].append(lambda: sp.dma(RV.ap()[rows, :], V[:], reads=[V]))
                        return
                    if g == 3:
                        G = Gb.next()
                        S[1].append(lambda: act.op(lambda e: e.activation(out=G[:], in_=Pm[:, 0:512], func=AF.Silu), reads=[Pm], writes=[G]))
                        S[1].append(lambda: sp.dma(RG.ap()[rows, :], G[:], reads=[G]))
                        return
                    H = 2 if g == 8 else 4
                    X = Xs.next(); O = Os.next(); tA = tAs.next(); tB = tBs.next(); st = sts.next()
                    normw = {0: None, 1: None, 6: wgq, 7: wgq, 8: wgk}[g]
                    S[1].append(lambda: act.op(lambda e: e.activation(out=X[:, 0:H * 128], in_=Pm[:, 0:H * 128], func=AF.Copy), reads=[Pm], writes=[X]))
                    if g == 8:
                        V = Vb.next()
                        S[1].append(lambda: act.op(lambda e: e.activation(out=V[:, 0:256], in_=Pm[:, 256:512], func=AF.Copy), reads=[Pm], writes=[V]))
                        S[1].append(lambda: sp.dma(VGi[t // 8].ap()[(t % 8) * 128:(t % 8 + 1) * 128, 512:768], V[:, 0:256], reads=[V]))
                    if normw is not None:
                        sq_stats(S[1], lambda h: Pm[:, h * 128:(h + 1) * 128], H, 128, st, [Pm])
                    norm_rope(S[2], S[3], X, H, 128, st, normw, tabb[:, t, 0:256], 0, 128, O, tA, tB)
                    if g == 1:
                        S[3].append(lambda: sp.dma(RK.ap()[rows, :], O[:, 0:512], reads=[O]))
                    Pt = pT.next()
                    Ptv = tr_op(S[4], O, H, 128, Pt, [(0, 128, 0)])
                    S[5].append(lambda: dve.op(lambda e: e.tensor_copy(out=sta[:, 0:H, cols], in_=Ptv[:, 0:H, :]), reads=[Pt], writes=[sta]))
                    if j == 3:
                        S[5].append(lambda: block_store(g, tb, sta, None))

                def mla_a_tile(pipe, g, t, pmm, pT):
                    S = pipe.tile(6)
                    Dc = 384 if g == 4 else 256
                    nk = Dc // 128
                    c0 = 0 if g == 4 else 3
                    Pm = pmm.next()
                    if t == 0:
                        S[0].append(lambda: load_w(g + 1))
                    main_mm(S[0], g, t, Pm)
                    X = Xs.next(); O = Os.next(); st = sts.next()
                    S[1].append(lambda: act.op(lambda e: e.activation(out=X[:, 0:Dc], in_=Pm[:, 0:Dc], func=AF.Copy), reads=[Pm], writes=[X]))
                    if g == 5:
                        S[1].append(lambda: act.op(lambda e: e.activation(out=kr_all[:, t, :], in_=Pm[:, 256:320], func=AF.Copy), reads=[Pm], writes=[kr_tiles[t]]))
                    sq_stats(S[1], lambda h: Pm[:, 0:Dc], 1, Dc, st, [Pm])
                    wn = wqa if g == 4 else wkva
                    S[2].append(lambda: dve.op(lambda e: e.reciprocal(out=st[:, 8:9], in_=st[:, 8:9]), reads=[st], writes=[st]))
                    S[2].append(lambda: dve.op(lambda e: e.scalar_tensor_tensor(out=O[:, 0:Dc], in0=X[:, 0:Dc], scalar=st[:, 8:9], in1=wn[:, 0:Dc],
                                                                                op0=ALU.mult, op1=ALU.mult), reads=[X, st, wn], writes=[O]))
                    Pt = pT.next()
                    Ptv = Pt[:, :].rearrange("p (s c) -> p s c", s=8)

                    def trf():
                        def tr(e):
                            for k in range(nk):
                                i = e.transpose(out=Ptv[:, k, :], in_=O[:, k * 128:(k + 1) * 128], identity=ident[:])
                            return i
                        pe.op(tr, reads=[O, ident], writes=[Pt])
                    S[4].append(trf)
                    cb = cTq[t] if g == 4 else cTk[t]
                    S[5].append(lambda: dve.op(lambda e: e.tensor_copy(out=cT_all[:, c0:c0 + nk, t * 128:(t + 1) * 128], in_=Ptv[:, 0:nk, :]), reads=[Pt], writes=[cb]))

                def mla_b_tile(pipe, g, t, pmm, pT, sta, stb):
                    j = t % 4
                    tb = t // 4
                    S = pipe.tile(6)
                    cols = slice(j * 128, (j + 1) * 128)
                    nk = 3 if g == 4 else 2
                    c0 = 0 if g == 4 else 3
                    wsec = wuq if g == 4 else wukv
                    hw = 384 if g == 4 else 512
                    p2 = [pmm.next(), pmm.next()]
                    cb = cTq[t] if g == 4 else cTk[t]
                    if g == 4 and t == 0:
                        S[2].append(lambda: load_tab(rt_m, 128))

                    def mm2f():
                        def mm2(e):
                            for half in range(2):
                                for k in range(nk):
                                    i = e.matmul(p2[half][:, 0:hw], lhsT=cT_all[:, c0 + k, t * 128:(t + 1) * 128], rhs=wsec[:, k, half * hw:(half + 1) * hw],
                                                 start=(k == 0), stop=(k == nk - 1))
                            return i
                        pe.op(mm2, reads=[cb, wsec], writes=[p2[0], p2[1]])
                    S[0].append(mm2f)
                    X2 = Xs.next(); O2 = Os.next(); tA2 = tAs.next(); tB2 = tBs.next(); st2 = sts.next()
                    X2v = X2[:, 0:768].rearrange("p (h d) -> p h d", h=4)
                    if g == 4:
                        for half in range(2):
                            S[1].append(lambda half=half: dve.op(lambda e: e.tensor_copy(
                                out=X2v[:, 2 * half:2 * half + 2, :], in_=p2[half][:, 0:384].rearrange("p (h d) -> p h d", h=2)),
                                reads=[p2[half]], writes=[X2]))
                        wn2 = wmq
                    else:
                        V = Vb.next()
                        Vv = V[:, :].rearrange("p (h e) -> p h e", h=4)
                        for half in range(2):
                            pv = p2[half][:, 0:512].rearrange("p (h two e) -> p h two e", h=2, two=2)
                            S[1].append(lambda half=half, pv=pv: act.op(lambda e: e.activation(out=Vv[:, 2 * half:2 * half + 2, :], in_=pv[:, :, 1, :], func=AF.Copy),
                                                                        reads=[p2[half]], writes=[V]))
                            S[1].append(lambda half=half, pv=pv: dve.op(lambda e: e.tensor_copy(out=X2v[:, 2 * half:2 * half + 2, 0:128], in_=pv[:, :, 0, :]),
                                                                        reads=[p2[half]], writes=[X2]))
                        S[1].append(lambda: sp.dma(VGi[t // 8].ap()[(t % 8) * 128:(t % 8 + 1) * 128, 0:512], V[:], reads=[V]))
                        S[1].append(lambda: pool.op(lambda e: e.tensor_copy(out=X2v[:, :, 128:192], in_=kr_all[:, t, :].unsqueeze(1).to_broadcast([128, 4, 64])),
                                                    reads=[kr_tiles[t], X2], writes=[X2]))
                        wn2 = wmk
                    sq_stats(S[1], lambda h: X2v[:, h, :], 4, 192, st2, [X2])
                    norm_rope(S[2], S[3], X2, 4, 192, st2, wn2, tabb[:, t, 0:128], 128, 64, O2, tA2, tB2, w_eng=dve)
                    Pt2 = pT.next()
                    Ptv2 = tr_op(S[4], O2, 4, 192, Pt2, [(0, 128, 0), (64, 128, 4)])
                    S[5].append(lambda: dve.op(lambda e: e.tensor_copy(out=sta[:, :, cols], in_=Ptv2[:, 0:4, :]), reads=[Pt2], writes=[sta]))
                    S[5].append(lambda: act.op(lambda e: e.activation(out=stb[64:128, :, cols], in_=Ptv2[64:128, 4:8, :], func=AF.Copy), reads=[Pt2], writes=[stb]))
                    if j == 3:
                        S[5].append(lambda: block_store(g, tb, sta, stb))

                load_w(0)
                load_tab(rt_r, 256)
                with ExitStack() as pss:
                    pipe = Pipe()
                    pmm = Rot([fw.psum(pss, f"b_pm{i}", [128, 512], F32) for i in range(4)])
                    pT = Rot([fw.psum(pss, f"b_pT{i}", [128, 1024], BF16) for i in range(3)])
                    for g in range(len(GROUPS)):
                        for tb in range(4):
                            sta = STa.next() if g not in (2, 3, 4, 5) else None
                            for j in range(4):
                                t = tb * 4 + j
                                if g in (4, 5):
                                    mla_a_tile(pipe, g, t, pmm, pT)
                                else:
                                    simple_tile(pipe, g, t, pmm, pT, sta)
                    for g in (4, 5):
                        for tb in range(4):
                            sta = STa.next()
                            stb = STb.next()
                            for j in range(4):
                                mla_b_tile(pipe, g, tb * 4 + j, pmm, pT, sta, stb)
                    pipe.run()
                    fw.barrier()

        def gather(src, dst):
            pool.e.collective_compute("AllGather", ALU.bypass, replica_groups=RG_PAIRS,
                                      ins=[src.ap().opt()], outs=[dst.ap().opt()]).then_inc(cc_sem.h, 1)
            cc_sem.val += 1

        def wait_gathers():
            for e in fw.engs:
                e.e.wait_ge(cc_sem.h, cc_sem.val)
                e.waited[cc_sem] = cc_sem.val

        def phase_ret_kv(L, Sf, gct):
            SG_in, SG_out = SG_in_l[L], SG_out_l[L]
            with ExitStack() as ps:
                Kb = Rot([fw.sbuf(ps, f"r_K{i}", [128, 512], BF16) for i in range(8)])
                Vb = Rot([fw.sbuf(ps, f"r_V{i}", [128, 512], BF16) for i in range(8)])
                Kz = Rot([fw.sbuf(ps, f"r_Kz{i}", [128, 512], BF16) for i in range(6)])
                KVt = Rot([fw.sbuf(ps, f"r_KVt{i}", [128, 512], F32) for i in range(6)])
                pkv = Rot([fw.psum(ps, f"r_pkv{i}", [128, 512], F32) for i in range(4)])
                tmp = fw.sbuf(ps, "r_tmp", [128, 8], F32)
                act.op(lambda e: e.activation(out=tmp[:], in_=lg[:], func=AF.Exp, scale=128.0), reads=[lg], writes=[tmp])
                for d in range(2):
                    dve.op(lambda e, d=d: e.tensor_copy(
                        out=gct[:, d, :].rearrange("p (h e) -> p h e", h=4),
                        in_=tmp[:, 4 * d:4 * d + 4].unsqueeze(2).to_broadcast([128, 4, 128])),
                        reads=[tmp], writes=[gct])
                for h in range(4):
                    act.op(lambda e, h=h: e.activation(out=zfb[:, h:h + 1], in_=czc[:, 0:1], func=AF.Exp, scale=lg[:, h:h + 1]),
                           reads=[lg, czc], writes=[zfb])
                    act.op(lambda e, h=h: e.activation(out=zfb[:, 4 + h:5 + h], in_=czc[:, 1:2], func=AF.Exp, scale=lg[:, 4 + h:5 + h]),
                           reads=[lg, czc], writes=[zfb])
                dve.op(lambda e: e.memset(Sf[:], 0.0), writes=[Sf])
                pipe = Pipe()

                def kvtile(i, d):
                    n = i if d == 0 else NT - 1 - i
                    S_ = pipe.tile(5)
                    K = Kb.next(); V = Vb.next(); Z = Kz.next(); KV = KVt.next(); Pk = pkv.next()
                    rows = slice(n * 128, (n + 1) * 128)
                    S_[0].append(lambda: sp.dma(K[:], RK.ap()[rows, :], writes=[K]))
                    S_[0].append(lambda: sp.dma(V[:], RV.ap()[rows, :], writes=[V]))
                    eng = dve if d == 0 else pool
                    S_[1].append(lambda: eng.op(lambda e: e.tensor_tensor(
                        out=Z[:, :].rearrange("p (h d) -> p h d", h=4), in0=K[:, :].rearrange("p (h d) -> p h d", h=4),
                        in1=zfb[:, 4 * d:4 * d + 4].unsqueeze(2).to_broadcast([128, 4, 128]), op=ALU.mult),
                        reads=[K, zfb], writes=[Z]))

                    def mmf():
                        def mm(e):
                            for h in range(4):
                                i_ = e.matmul(Pk[:, h * 128:(h + 1) * 128], lhsT=Z[:, h * 128:(h + 1) * 128], rhs=V[:, h * 128:(h + 1) * 128],
                                              start=True, stop=True)
                            return i_
                        pe.op(mm, reads=[Z, V], writes=[Pk])
                    S_[2].append(mmf)
                    S_[3].append(lambda: act.op(lambda e: e.activation(out=KV[:], in_=Pk[:], func=AF.Copy), reads=[Pk], writes=[KV]))
                    S_[4].append(lambda: sp.dma(KVD.ap()[n, d], KV[:], reads=[KV]))
                    S_[4].append(lambda: dve.op(lambda e: e.tensor_tensor(out=Sf[:, d, :], in0=Sf[:, d, :], in1=gct[:, d, :], op=ALU.mult), reads=[Sf, gct], writes=[Sf]))
                    S_[4].append(lambda: dve.op(lambda e: e.tensor_tensor(out=Sf[:, d, :], in0=Sf[:, d, :], in1=KV[:], op=ALU.add), reads=[Sf, KV], writes=[Sf]))
                for i in range(NT):
                    for d in range(2):
                        kvtile(i, d)
                pipe.run()
                sp.dma(SG_in.ap().rearrange("(d p) c -> p d c", p=128), Sf[:], reads=[Sf])
                fw.barrier()
                gather(SG_in, SG_out)

        def phase_attn(L, mixT, PV, Sf, gct):
            KGo, VGo = KGo_l[L], VGo_l[L]
            SG_out = SG_out_l[L]
            with ExitStack() as ps:
                KTa = Rot([fw.sbuf(ps, f"a_KTa{i}", [128, 2, T], BF16) for i in range(2)])
                KTb = Rot([fw.sbuf(ps, f"a_KTb{i}", [128, 2, T], BF16) for i in range(2)])
                Vt = Rot([fw.sbuf(ps, f"a_V{i}", [128, 32, 128], BF16) for i in range(2)])
                QTa = Rot([fw.sbuf(ps, f"a_QTa{i}", [128, 512], BF16) for i in range(3)])
                QTb = Rot([fw.sbuf(ps, f"a_QTb{i}", [128, 512], BF16) for i in range(3)])
                Pb = Rot([fw.sbuf(ps, f"a_P{i}", [128, 512], BF16) for i in range(6)])
                rsb = Rot([fw.sbuf(ps, f"a_rs{i}", [128, 512], F32) for i in range(2)])
                psc = Rot([fw.psum(ps, f"a_ps{i}", [128, 512], F32) for i in range(4)])
                pob = Rot([fw.psum(ps, f"a_po{i}", [128, 512], F32) for i in range(2)])
                psb = Rot([fw.psum(ps, f"a_pz{i}", [128, 512], F32) for i in range(2)])
                kgo = [k_.ap().rearrange("(b r) t -> r b t", b=2) for k_ in KGo]

                def load_v(V, c0):
                    outs, ins = [], []
                    for b_ in range(2):
                        for i_ in range(2):
                            outs.append(V[:, b_ * 16 + i_ * 8:b_ * 16 + i_ * 8 + 8, :])
                            ins.append(VGo[i_].ap()[b_ * 1024:(b_ + 1) * 1024, c0:c0 + 128].rearrange("(n p) c -> p n c", p=128))
                    sp.dma(outs, ins, writes=[V])
                jobs = []
                for h in range(4):
                    jobs.append(dict(kind="mla", kv=h, qs=[h], scale=192 ** -0.5))
                for kvh in range(2):
                    jobs.append(dict(kind="gqa", kv=kvh, qs=[kvh * 4 + i for i in range(4)], scale=128 ** -0.5))
                def load_kv(ji):
                    jb = jobs[ji]
                    mla = jb["kind"] == "mla"
                    Ka = KTa.next(); V = Vt.next()
                    jb["Ka"], jb["V"] = Ka, V
                    if mla:
                        Kb_ = KTb.next()
                        jb["Kb"] = Kb_
                        kg_ = kgo[jb["kv"] // 2]
                        r0 = (jb["kv"] % 2) * 192
                        sp.dma(Ka[:], kg_[r0:r0 + 128, :, :], writes=[Ka])
                        sp.dma(Kb_[64:128, :, :], kg_[r0 + 128:r0 + 192, :, :], writes=[Kb_])
                        load_v(V, jb["kv"] * 128)
                    else:
                        r0 = jb["kv"] * 128
                        sp.dma(Ka[:], kgo[2][r0:r0 + 128, :, :], writes=[Ka])
                        load_v(V, 512 + jb["kv"] * 128)

                items = [(ji, qh, qb) for ji, jb in enumerate(jobs) for qh in jb["qs"] for qb in range(4)]
                Qof = {}

                def load_q(i):
                    ji, qh, qb = items[i]
                    mla = jobs[ji]["kind"] == "mla"
                    qc = slice(qb * 512, (qb + 1) * 512)
                    Qa = QTa.next()
                    Qb_ = None
                    if mla:
                        Qb_ = QTb.next()
                        sp.dma(Qa[:], MQT.ap()[qh, 0:128, qc], writes=[Qa])
                        sp.dma(Qb_[64:128, :], MQT.ap()[qh, 128:192, qc], writes=[Qb_])
                    else:
                        sp.dma(Qa[:], GQT.ap()[qh, :, qc], writes=[Qa])
                    Qof[i] = (Qa, Qb_)

                Sin = fw.sbuf(ps, "a_Sin", [128, 4, 512], F32)
                kvts = Rot([fw.sbuf(ps, f"a_kvt{i}", [128, 2, 512], F32) for i in range(3)])
                kvof = {}

                def scan_load(i_):
                    kvt = kvts.next()
                    kvof[i_] = kvt
                    sp.dma([kvt[:, 0, :], kvt[:, 1, :]], [KVD.ap()[i_, 0], KVD.ap()[NT - 1 - i_, 1]], writes=[kvt])

                def scan_init():
                    sp.dma(Sin[:], SG_out.ap().rearrange("(b p) c -> p b c", p=128), writes=[Sin])
                    for d in range(2):
                        dve.op(lambda e, d=d: e.tensor_scalar(out=Sf[:, d, :], in0=Sin[:, d, :], scalar1=ssc[:, 2 * d:2 * d + 1], scalar2=None, op0=ALU.mult),
                               reads=[Sin, ssc, Sf], writes=[Sf])
                        dve.op(lambda e, d=d: e.scalar_tensor_tensor(out=Sf[:, d, :], in0=Sin[:, 2 + d, :], scalar=ssc[:, 2 * d + 1:2 * d + 2], in1=Sf[:, d, :],
                                                                     op0=ALU.mult, op1=ALU.add),
                               reads=[Sin, ssc, Sf], writes=[Sf])
                    scan_load(0)

                def scan_step(i_):
                    if i_ + 1 < NT:
                        scan_load(i_ + 1)
                    kvt = kvof.pop(i_)
                    pool.op(lambda e: e.tensor_copy(out=PV.t[:, i_, 0, :], in_=Sf[:, 0, :]), reads=[Sf], writes=[PV.tiles[i_]])
                    pool.op(lambda e: e.tensor_copy(out=PV.t[:, NT - 1 - i_, 1, :], in_=Sf[:, 1, :]), reads=[Sf], writes=[PV.tiles[NT - 1 - i_]])
                    dve.op(lambda e: e.tensor_tensor(out=Sf[:], in0=Sf[:], in1=gct[:], op=ALU.mult), reads=[Sf, gct], writes=[Sf])
                    dve.op(lambda e: e.tensor_tensor(out=Sf[:], in0=Sf[:], in1=kvt[:], op=ALU.add), reads=[Sf, kvt], writes=[Sf])

                assert len(items) == 3 * NT
                load_kv(0)
                load_q(0)
                scan_init()
                for i, (ji, qh, qb) in enumerate(items):
                    if i % 3 == 2 and i // 3 < NT:
                        scan_step(i // 3)
                    jb = jobs[ji]
                    mla = jb["kind"] == "mla"
                    first_of_job = (i == 0) or (items[i - 1][0] != ji)
                    if first_of_job and ji + 1 < len(jobs):
                        load_kv(ji + 1)
                    if i + 1 < len(items):
                        load_q(i + 1)
                    Ka, V = jb["Ka"], jb["V"]
                    Kb_ = jb.get("Kb")
                    Qa, Qb_ = Qof.pop(i)
                    chunk = (4 + qh) if mla else (8 + qh)
                    qc = slice(qb * 512, (qb + 1) * 512)
                    po = pob.next(); pz = psb.next()

                    def qk_mm(e, Ps, kt, Ka=Ka, Kb_=Kb_, Qa=Qa, Qb_=Qb_, mla=mla):
                        blk, off = kt // 16, (kt % 16) * 128
                        i_ = e.matmul(Ps[:], lhsT=Ka[:, blk, off:off + 128], rhs=Qa[:], start=True, stop=not mla)
                        if mla:
                            i_ = e.matmul(Ps[:], lhsT=Kb_[64:128, blk, off:off + 128], rhs=Qb_[64:128, :], start=False, stop=True)
                        return i_
                    qk_reads = [Ka, Qa] + ([Kb_, Qb_] if mla else [])

                    def qk(kt):
                        Ps_ = psc.next()
                        pe.op(lambda e: qk_mm(e, Ps_, kt), reads=qk_reads, writes=[Ps_])
                        return Ps_
                    pend = [qk(0), qk(1)]
                    for kt in range(32):
                        Ps = pend.pop(0)
                        if kt + 2 < 32:
                            pend.append(qk(kt + 2))
                        Pt_ = Pb.next()
                        blk = kt // 16
                        act.op(lambda e, Ps=Ps, Pt_=Pt_, blk=blk: e.activation(out=Pt_[:], in_=Ps[:], func=AF.Exp, scale=jb["scale"], bias=kb[:, blk:blk + 1]),
                               reads=[Ps, kb], writes=[Pt_])

                        def pvf(e, kt=kt, Pt_=Pt_, V=V, po=po, pz=pz):
                            e.matmul(po[:], lhsT=V[:, kt, :], rhs=Pt_[:], start=(kt == 0), stop=(kt == 31))
                            return e.matmul(pz[:], lhsT=ones[:], rhs=Pt_[:], start=(kt == 0), stop=(kt == 31))
                        pe.op(pvf, reads=[V, Pt_, ones], writes=[po, pz])
                    rs = rsb.next()
                    dve.op(lambda e: e.reciprocal(out=rs[:], in_=pz[:]), reads=[pz], writes=[rs])
                    dve.op(lambda e: e.tensor_tensor(out=mixT.t[:, chunk, qc], in0=po[:], in1=rs[:], op=ALU.mult),
                           reads=[po, rs], writes=[mixT.chunks[chunk]])
                fw.barrier()

        def phase_ret_out(L, PV, mixT):
            with ExitStack() as ps:
                QTb = Rot([fw.sbuf(ps, f"o_QT{i}", [128, 4, 512], BF16) for i in range(2)])
                KTb = Rot([fw.sbuf(ps, f"o_KT{i}", [128, 4, 512], BF16) for i in range(2)])
                Vb = Rot([fw.sbuf(ps, f"o_V{i}", [128, 512], BF16) for i in range(5)])
                Gb = Rot([fw.sbuf(ps, f"o_G{i}", [128, 512], F32) for i in range(5)])
                Pm = Rot([fw.sbuf(ps, f"o_P{i}", [128, 512], BF16) for i in range(3)])
                Qf = Rot([fw.sbuf(ps, f"o_Qf{i}", [128, 2, 512], BF16) for i in range(3)])
                Yb = Rot([fw.sbuf(ps, f"o_Y{i}", [128, 512], F32) for i in range(4)])
                Sq = Rot([fw.sbuf(ps, f"o_Sq{i}", [128, 512], F32) for i in range(2)])
                Ro = Rot([fw.sbuf(ps, f"o_R{i}", [128, 512], BF16) for i in range(3)])
                stt = Rot([fw.sbuf(ps, f"o_st{i}", [128, 16], F32) for i in range(4)])
                psc = Rot([fw.psum(ps, f"o_ps{i}", [128, 512], F32) for i in range(2)])
                pyb = Rot([fw.psum(ps, f"o_py{i}", [128, 512], F32) for i in range(2)])
                pT = Rot([fw.psum(ps, f"o_pT{i}", [128, 1024], BF16) for i in range(2)])
                cret = fw.sbuf(ps, "o_cret", [128, 6, 128], F32)
                inn = fw.sbuf(ps, "o_inn", [128, 2, 512], F32)
                dtot = fw.sbuf(ps, "o_dtot", [128, 512], F32)
                gnw = fw.sbuf(ps, "o_gnw", [128, 512], F32)
                tm2 = fw.sbuf(ps, "o_tm2", [128, 128], F32)
                sp.dma(cret[:], c_ret.ap(), writes=[cret])
                sp.dma(gnw[:], gn_w.ap()[L, :].partition_broadcast(128), writes=[gnw])
                for h in range(4):
                    act.op(lambda e, h=h: e.activation(out=inn[:, 0, h * 128:(h + 1) * 128], in_=cret[:, 4, :], func=AF.Exp, scale=lg[:, h:h + 1]),
                           reads=[lg, cret], writes=[inn])
                    act.op(lambda e, h=h: e.activation(out=inn[:, 1, h * 128:(h + 1) * 128], in_=cret[:, 5, :], func=AF.Exp, scale=lg[:, 4 + h:5 + h]),
                           reads=[lg, cret], writes=[inn])
                for h in range(4):
                    dsl = dtot[:, h * 128:(h + 1) * 128]
                    act.op(lambda e, h=h, dsl=dsl: e.activation(out=dsl, in_=cret[:, 0, :], func=AF.Exp, scale=lg[:, h:h + 1]),
                           reads=[lg, cret], writes=[dtot])
                    dve.op(lambda e, dsl=dsl: e.tensor_tensor(out=dsl, in0=dsl, in1=cret[:, 1, :], op=ALU.mult), reads=[dtot, cret], writes=[dtot])
                    act.op(lambda e, h=h: e.activation(out=tm2[:], in_=cret[:, 2, :], func=AF.Exp, scale=lg[:, 4 + h:5 + h]),
                           reads=[lg, cret], writes=[tm2])
                    dve.op(lambda e: e.tensor_tensor(out=tm2[:], in0=tm2[:], in1=cret[:, 3, :], op=ALU.mult), reads=[tm2, cret], writes=[tm2])
                    dve.op(lambda e, dsl=dsl: e.tensor_tensor(out=dsl, in0=dsl, in1=tm2[:], op=ALU.add), reads=[dtot, tm2], writes=[dtot])
                pipe = Pipe()

                def tile(n, QT, KT, first):
                    S_ = pipe.tile(9)
                    tb, j = n // 4, n % 4
                    tc = slice(tb * 512, (tb + 1) * 512)
                    rows = slice(n * 128, (n + 1) * 128)
                    cols = slice(j * 128, (j + 1) * 128)
                    V = Vb.next(); G = Gb.next()
                    if first:
                        S_[0].append(lambda: sp.dma(QT[:], RQT.ap().rearrange("h d t -> d h t")[:, :, tc], writes=[QT]))
                        S_[0].append(lambda: sp.dma(KT[:], RKT.ap().rearrange("h d t -> d h t")[:, :, tc], writes=[KT]))
                    S_[0].append(lambda: sp.dma(V[:], RV.ap()[rows, :], writes=[V]))
                    S_[3].append(lambda: sp.dma(G[:], RG.ap()[rows, :], writes=[G]))
                    Ps = psc.next()

                    def scf():
                        def sc(e):
                            for h in range(4):
                                i = e.matmul(Ps[:, h * 128:(h + 1) * 128], lhsT=KT[:, h, cols], rhs=QT[:, h, cols], start=True, stop=True)
                            return i
                        pe.op(sc, reads=[KT, QT], writes=[Ps])
                    S_[1].append(scf)
                    Pmt = Pm.next()
                    S_[2].append(lambda: dve.op(lambda e: e.tensor_tensor(out=Pmt[:], in0=Ps[:], in1=dtot[:], op=ALU.mult), reads=[Ps, dtot], writes=[Pmt]))
                    Q2 = Qf.next()
                    for d in range(2):
                        S_[2].append(lambda d=d: pool.op(lambda e: e.tensor_tensor(
                            out=Q2[:, d, :].rearrange("p (h c) -> p h c", h=4), in0=QT[:, :, cols],
                            in1=inn[:, d, :].rearrange("p (h c) -> p h c", h=4), op=ALU.mult),
                            reads=[QT, inn], writes=[Q2]))
                    Py = pyb.next()

                    def ymf():
                        def ym(e):
                            for h in range(4):
                                hs = slice(h * 128, (h + 1) * 128)
                                e.matmul(Py[:, hs], lhsT=Pmt[:, hs], rhs=V[:, hs], start=True, stop=False)
                                e.matmul(Py[:, hs], lhsT=Q2[:, 0, hs], rhs=PV.t[:, n, 0, hs], start=False, stop=False)
                                i = e.matmul(Py[:, hs], lhsT=Q2[:, 1, hs], rhs=PV.t[:, n, 1, hs], start=False, stop=True)
                            return i
                        pe.op(ym, reads=[Pmt, V, Q2, PV.tiles[n]], writes=[Py])
                    S_[3].append(ymf)
                    Y = Yb.next(); S2 = Sq.next(); st = stt.next(); R = Ro.next()
                    Yv = Y[:, :].rearrange("p (h e) -> p h e", h=4)
                    S2v = S2[:, :].rearrange("p (h e) -> p h e", h=4)
                    S_[4].append(lambda: act.op(lambda e: e.activation(out=Y[:], in_=Py[:], func=AF.Copy), reads=[Py], writes=[Y]))
                    S_[4].append(lambda: dve.op(lambda e: e.tensor_reduce(out=st[:, 0:4], in_=Yv, axis=AX.X, op=ALU.add), reads=[Y], writes=[st]))
                    S_[4].append(lambda: dve.op(lambda e: e.tensor_scalar(out=st[:, 0:4], in0=st[:, 0:4], scalar1=-1.0 / 128, scalar2=None, op0=ALU.mult), reads=[st], writes=[st]))
                    S_[4].append(lambda: dve.op(lambda e: e.tensor_tensor(out=Yv, in0=Yv, in1=st[:, 0:4].unsqueeze(2).to_broadcast([128, 4, 128]), op=ALU.add),
                                                reads=[Y, st], writes=[Y]))
                    S_[5].append(lambda: pool.op(lambda e: e.tensor_tensor(out=S2[:], in0=Y[:], in1=Y[:], op=ALU.mult), reads=[Y], writes=[S2]))
                    S_[5].append(lambda: dve.op(lambda e: e.tensor_reduce(out=st[:, 4:8], in_=S2v, axis=AX.X, op=ALU.add), reads=[S2], writes=[st]))
                    S_[5].append(lambda: act.op(lambda e: e.activation(out=st[:, 8:12], in_=st[:, 4:8], func=AF.Sqrt, scale=1.0 / 128, bias=EPS), reads=[st], writes=[st]))
                    S_[5].append(lambda: dve.op(lambda e: e.reciprocal(out=st[:, 8:12], in_=st[:, 8:12]), reads=[st], writes=[st]))
                    S_[6].append(lambda: dve.op(lambda e: e.tensor_tensor(out=Yv, in0=Yv, in1=st[:, 8:12].unsqueeze(2).to_broadcast([128, 4, 128]), op=ALU.mult),
                                                reads=[Y, st], writes=[Y]))
                    S_[6].append(lambda: pool.op(lambda e: e.tensor_tensor(out=Y[:], in0=Y[:], in1=gnw[:], op=ALU.mult), reads=[Y, gnw], writes=[Y]))
                    S_[6].append(lambda: dve.op(lambda e: e.tensor_tensor(out=R[:], in0=Y[:], in1=G[:], op=ALU.mult), reads=[Y, G], writes=[R]))
                    Pt = pT.next()

                    def trf():
                        def tr(e):
                            for h in range(4):
                                i = e.transpose(out=Pt[:, h * 128:(h + 1) * 128], in_=R[:, h * 128:(h + 1) * 128], identity=ident[:])
                            return i
                        pe.op(tr, reads=[R, ident], writes=[Pt])
                    S_[7].append(trf)
                    S_[8].append(lambda: act.op(lambda e: e.activation(out=mixT.t[:, 0:4, n * 128:(n + 1) * 128],
                                                                       in_=Pt[:, 0:512].rearrange("p (h c) -> p h c", h=4), func=AF.Copy),
                                                reads=[Pt], writes=[mixT.chunks[0]]))
                for tb in range(4):
                    QT = QTb.next(); KT = KTb.next()
                    for j in range(4):
                        tile(tb * 4 + j, QT, KT, j == 0)
                pipe.run()
                fw.barrier()

        def phase_outproj(L, mixT, src, dst):
            with ExitStack() as ps:
                wb = Rot([fw.sbuf(ps, f"d_w{i}", [128, 16, 512], BF16) for i in range(2)])
                xs = Rot([fw.sbuf(ps, f"d_x{i}", [128, 512], F32) for i in range(4)])
                pm = Rot([fw.psum(ps, f"d_pm{i}", [128, 512], F32) for i in range(4)])
                wsrc = w_out.ap()[L].rearrange("(k p) c -> p k c", p=128)
                Wts = {}

                def load_w(cg):
                    Wt = wb.next()
                    Wts[cg] = Wt
                    cc = slice(cg * 512, (cg + 1) * 512)
                    pool.dma([Wt[:, 4 * q:4 * q + 4, :] for q in range(4)], [wsrc[:, 4 * q:4 * q + 4, cc] for q in range(4)], writes=[Wt])
                items = [(cg, t) for cg in range(4) for t in range(NT)]
                Xof = {}

                def load_x(i):
                    cg, t = items[i]
                    X = xs.next()
                    Xof[i] = X
                    sp.dma(X[:], src[t * 128:(t + 1) * 128, cg * 512:(cg + 1) * 512], writes=[X])
                load_w(0)
                load_x(0)
                load_x(1)
                for i, (cg, t) in enumerate(items):
                    if t == 0 and cg + 1 < 4:
                        load_w(cg + 1)
                    if i + 2 < len(items):
                        load_x(i + 2)
                    Wt = Wts[cg]
                    X = Xof.pop(i)
                    Pm_ = pm.next()

                    def mm(e, t=t, Pm_=Pm_, Wt=Wt):
                        for k in range(16):
                            i_ = e.matmul(Pm_[:], lhsT=mixT.t[:, k, t * 128:(t + 1) * 128], rhs=Wt[:, k, :], start=(k == 0), stop=(k == 15))
                        return i_
                    pe.op(mm, reads=list(mixT.chunks) + [Wt], writes=[Pm_])
                    dve.op(lambda e, X=X, Pm_=Pm_: e.tensor_tensor(out=X[:], in0=Pm_[:], in1=X[:], op=ALU.add), reads=[Pm_, X], writes=[X])
                    sp.dma(dst[t * 128:(t + 1) * 128, cg * 512:(cg + 1) * 512], X[:], reads=[X])
                fw.barrier()

        def phase_mlp(L, hT, uT, srcs, dsts):
            with ExitStack() as ps:
                wu = Rot([fw.sbuf(ps, f"f_wu{i}", [128, 16, 128], BF16) for i in range(4)])
                wd = Rot([fw.sbuf(ps, f"f_wd{i}", [128, 16, 512], BF16) for i in range(2)])
                xs = Rot([fw.sbuf(ps, f"f_x{i}", [128, 512], F32) for i in range(4)])
                rl = Rot([fw.sbuf(ps, f"f_rl{i}", [128, 512], F32) for i in range(4)])
                pu = Rot([[fw.psum(ps, f"f_pu{i}{h}", [128, 512], F32) for h in range(2)] for i in range(3)])
                pd = Rot([fw.psum(ps, f"f_pd{i}", [128, 512], F32) for i in range(2)])
                wus = w_up.ap()[L].rearrange("(k p) f -> p k f", p=128)
                wds = w_down.ap()[L].rearrange("(c p) d -> p c d", p=128)
                Wus, Wds = {}, {}

                def load_wu(q):
                    if q >= 64:
                        return
                    Wu = wu.next()
                    Wus[q] = Wu
                    pool.dma(Wu[:], wus[:, :, q * 128:(q + 1) * 128], writes=[Wu])

                def load_wd(q):
                    if q >= 16:
                        return
                    fq, cg = q // 4, q % 4
                    Wd = wd.next()
                    Wds[q] = Wd
                    cc = slice(cg * 512, (cg + 1) * 512)
                    pool.dma([Wd[:, 4 * r:4 * r + 4, :] for r in range(4)],
                             [wds[:, fq * 16 + 4 * r:fq * 16 + 4 * r + 4, cc] for r in range(4)], writes=[Wd])
                load_wu(0)
                load_wu(1)
                load_wu(2)
                for fq in range(4):
                    src, dst = srcs[fq], dsts[fq]
                    for fc in range(16):
                        q = fq * 16 + fc
                        load_wu(q + 3)
                        if fc == 8:
                            load_wd(fq * 4)
                        Wu = Wus.pop(q)
                        for pair in range(2):
                            Pus = pu.next()

                            def mm(e, Wu=Wu, Pus=Pus, pair=pair):
                                for k in range(16):
                                    for h in range(2):
                                        tb = 2 * pair + h
                                        i = e.matmul(Pus[h][:], lhsT=Wu[:, k, :], rhs=hT.t[:, k, tb * 512:(tb + 1) * 512], start=(k == 0), stop=(k == 15))
                                return i
                            pe.op(mm, reads=list(hT.tiles) + [Wu], writes=Pus)
                            for h in range(2):
                                tb = 2 * pair + h
                                dstu = uT.t[:, fc, tb * 512:(tb + 1) * 512]
                                R = rl.next()
                                act.op(lambda e, h=h, R=R: e.activation(out=R[:], in_=Pus[h][:], func=AF.Relu), reads=[Pus[h]], writes=[R])
                                dve.op(lambda e, h=h, R=R, dstu=dstu: e.tensor_tensor(out=dstu, in0=Pus[h][:], in1=R[:], op=ALU.mult),
                                       reads=[Pus[h], R], writes=[uT.chunks[fc]])
                    items = [(cg, t) for cg in range(4) for t in range(NT)]
                    Xof = {}

                    def load_x(i):
                        cg, t = items[i]
                        X = xs.next()
                        Xof[i] = X
                        sp.dma(X[:], src[t * 128:(t + 1) * 128, cg * 512:(cg + 1) * 512], writes=[X])
                    load_x(0)
                    load_x(1)
                    for i, (cg, t) in enumerate(items):
                        if t == 0:
                            load_wd(fq * 4 + cg + 1) if cg + 1 < 4 else None
                        if i + 2 < len(items):
                            load_x(i + 2)
                        Wd = Wds[fq * 4 + cg]
                        X = Xof.pop(i)
                        Pd = pd.next()

                        def mm2(e, t=t, Pd=Pd, Wd=Wd):
                            for c in range(16):
                                i_ = e.matmul(Pd[:], lhsT=uT.t[:, c, t * 128:(t + 1) * 128], rhs=Wd[:, c, :], start=(c == 0), stop=(c == 15))
                            return i_
                        pe.op(mm2, reads=list(uT.chunks) + [Wd], writes=[Pd])
                        dve.op(lambda e, X=X, Pd=Pd: e.tensor_tensor(out=X[:], in0=Pd[:], in1=X[:], op=ALU.add), reads=[Pd, X], writes=[X])
                        sp.dma(dst[t * 128:(t + 1) * 128, cg * 512:(cg + 1) * 512], X[:], reads=[X])
                    fw.barrier()

        def big(stack, name, shape, dtype, nsub, attr):
            b = fw.sbuf(stack, name, shape, dtype)
            setattr(b, attr, [fw.buf(f"{name}_{i}") for i in range(nsub)])
            return b

        def run():
            if stop_here("consts"):
                return
            for L in range(DEPTH):
                x_src = x_in.ap() if L == 0 else XB.ap()
                layer_consts(L)
                if stop_here(f"lconsts{L}"):
                    return
                with ExitStack() as s1:
                    hT = big(s1, "hT", [128, 16, T], BF16, NT, "tiles")
                    phase_norm(x_src, ln1.ap()[L, :], hT)
                    if stop_here(f"norm{L}"):
                        return
                    phase_inproj(L, hT)
                if stop_here(f"inproj{L}") or (STOP_AFTER or "").startswith("inprojG") or (STOP_AFTER or "").startswith("inprojO"):
                    return
                for p_ in range(3):
                    gather(KGi_l[L][p_], KGo_l[L][p_])
                for p_ in range(2):
                    gather(VGi_l[L][p_], VGo_l[L][p_])
                with ExitStack() as s_pv:
                    PV = big(s_pv, "PV", [128, NT, 2, 512], BF16, NT, "tiles")
                    Sf = fw.sbuf(s_pv, "Sf", [128, 2, 512], F32)
                    gct = fw.sbuf(s_pv, "gct", [128, 2, 512], F32)
                    mixT = big(s_pv, "mixT", [128, 16, T], BF16, 16, "chunks")
                    phase_ret_kv(L, Sf, gct)
                    if stop_here(f"retkv{L}"):
                        return
                    wait_gathers()
                    phase_attn(L, mixT, PV, Sf, gct)
                    if stop_here(f"attn{L}"):
                        return
                    phase_ret_out(L, PV, mixT)
                    if stop_here(f"retout{L}"):
                        return
                    phase_outproj(L, mixT, x_src, XA.ap())
                if stop_here(f"outproj{L}"):
                    return
                with ExitStack() as s5:
                    hT = big(s5, "h2T", [128, 16, T], BF16, NT, "tiles")
                    uT = big(s5, "uT", [128, 16, T], BF16, 16, "chunks")
                    phase_norm(XA.ap(), ln2.ap()[L, :], hT)
                    last = y_out.ap() if L == DEPTH - 1 else XB.ap()
                    phase_mlp(L, hT, uT, [XA.ap(), XB.ap(), XB.ap(), XB.ap()], [XB.ap(), XB.ap(), XB.ap(), last])
                if stop_here(f"mlp{L}"):
                    return

        run()
        finish()
    return nc


def _rope_tables(pos):
    theta = 10000.0
    pos = pos.astype(np.float32)
    inv = (1.0 / (theta ** (np.arange(0, 128, 2, dtype=np.float32) / 128))).astype(np.float32)
    ang = pos[:, None] * inv[None, :]
    c, s = np.cos(ang).astype(np.float32), np.sin(ang).astype(np.float32)
    rt_r = np.concatenate([c, c, -s, s], axis=1).astype(np.float32)
    rt_rk = (rt_r * np.float32(128 ** -0.5)).astype(np.float32)

    def axial(dim):
        half = dim // 2
        inv = (1.0 / (theta ** (np.arange(0, half, 2, dtype=np.float32) / half))).astype(np.float32)
        row = np.floor(pos / 64).astype(np.float32)
        col = (pos - row * 64).astype(np.float32)
        ang = np.concatenate([row[:, None] * inv[None, :], col[:, None] * inv[None, :]], axis=1)
        c, s = np.cos(ang).astype(np.float32), np.sin(ang).astype(np.float32)
        return np.concatenate([c, c, -s, s], axis=1).astype(np.float32)
    return rt_r, rt_rk, axial(128), axial(64)


def _consts():
    m = np.arange(128, dtype=np.float32)[:, None]
    c = np.arange(128, dtype=np.float32)[None, :]
    A1 = np.maximum(c - m, 0.0)
    M1 = (m <= c).astype(np.float32)
    A2 = np.maximum(m - c, 0.0)
    M2 = (m > c).astype(np.float32)
    io1 = np.broadcast_to(c + 1.0, (128, 128))
    io2 = np.broadcast_to(128.0 - c, (128, 128))
    cret = np.stack([A1, M1, A2, M2, io1, io2], axis=1).astype(np.float32)
    p = np.arange(128, dtype=np.float32)
    czc = np.stack([127.0 - p, p], axis=1).astype(np.float32)
    return cret, czc


_NC_CACHE = {}


def kernel(x_prompt, x_sample, ln1_w, w_in, ret_decay_fwd, ret_decay_bwd, ret_gn_w,
           mla_q_a_norm, mla_w_uq, mla_kv_a_norm, mla_w_ukv, mla_q_norm, mla_k_norm,
           gqa_q_norm, gqa_k_norm, w_out, ln2_w, w_up, w_down):
    f = lambda a: np.ascontiguousarray(np.asarray(a, dtype=np.float32))
    x_prompt, x_sample = f(x_prompt), f(x_sample)
    key = (STOP_AFTER, tuple(DEBUG_OUT), tuple(DEBUG_CORES or []))
    if key not in _NC_CACHE:
        _NC_CACHE[key] = build_program()
    nc = _NC_CACHE[key]
    cret, czc = _consts()
    shared = {
        "w_in": f(w_in), "w_out": f(w_out), "w_up": f(w_up), "w_down": f(w_down),
        "w_uq": f(mla_w_uq), "w_ukv": f(mla_w_ukv), "ln1_w": f(ln1_w), "ln2_w": f(ln2_w),
        "dec_f": f(ret_decay_fwd), "dec_b": f(ret_decay_bwd), "gn_w": f(ret_gn_w),
        "qa_n": f(mla_q_a_norm), "kva_n": f(mla_kv_a_norm), "mq_n": f(mla_q_norm), "mk_n": f(mla_k_norm),
        "gq_n": f(gqa_q_norm), "gk_n": f(gqa_k_norm),
        "c_ident": np.eye(128, dtype=np.float32), "c_ret": cret, "c_zc": czc,
    }
    early = STOP_AFTER is not None and STOP_AFTER.endswith("0") and not STOP_AFTER.startswith("mlp")
    if early:
        shared.pop("w_up"); shared.pop("w_down")
    in_maps = []
    cores = list(range(NCORES)) if DEBUG_CORES is None else list(DEBUG_CORES)
    for c in cores:
        if c < 4:
            xs = x_prompt[c]
            pos0 = 0
            rank = c % 2
            kbias = np.array([[0.0 if rank == 0 else NEG, 0.0 if rank == 1 else NEG]], dtype=np.float32)
            ssc = np.zeros((1, 4), dtype=np.float32)
        else:
            seq = (c - 4) // 2
            rank = c % 2
            pos0 = rank * T
            xs = x_sample[seq, pos0:pos0 + T]
            kbias = np.zeros((1, 2), dtype=np.float32)
            ssc = np.array([[0, 0, 0, 1]] if rank == 0 else [[1, 0, 0, 0]], dtype=np.float32)
        rt_r, rt_rk, rt_g, rt_m = _rope_tables(np.arange(pos0, pos0 + T))
        m = dict(shared)
        m.update({"x": np.ascontiguousarray(xs), "rt_r": rt_r, "rt_rk": rt_rk, "rt_g": rt_g, "rt_m": rt_m,
                  "kbias": kbias, "sscale": ssc})
        in_maps.append(m)
    res = run_bass_kernel_spmd(nc, in_maps, core_ids=list(range(len(cores))))
    kernel.last_results = res.results
    if DEBUG_CORES is not None:
        return None
    ys = [np.asarray(r["y"], dtype=np.float32) for r in res.results]
    y_prompt = np.stack(ys[0:4], axis=0)
    y_sample = np.stack([np.concatenate([ys[4], ys[5]], axis=0), np.concatenate([ys[6], ys[7]], axis=0)], axis=0)
    return (y_prompt, y_sample)
```

```python
import numpy as np
from contextlib import ExitStack
import concourse.bass as bass
import concourse.mybir as mybir
from concourse.bass_utils import run_bass_kernel_spmd

F32 = mybir.dt.float32
BF16 = mybir.dt.bfloat16
AF = mybir.ActivationFunctionType
ALU = mybir.AluOpType
AX = mybir.AxisListType

NCORES = 8
T = 2048
NT = T // 128
DM = 2048
DEPTH = 2
INW = 4288
DFF = 8192
EPS = 1e-6
NEG = -30000.0

GROUPS = [(0, 512), (512, 512), (1024, 512), (1536, 512), (2048, 384), (2432, 320),
          (2752, 512), (3264, 512), (3776, 512)]

import os
KDBG = int(os.environ.get('KDBG', '0'))
STOP_AFTER = None
DEBUG_OUT = []
DEBUG_CORES = None


class Sem:
    def __init__(self, handle, name):
        self.h = handle
        self.name = name
        self.val = 0


class Buf:
    def __init__(self, name, t=None):
        self.name = name
        self.t = t
        self.last_w = None
        self.readers = []
        self.dsem = None
        self.is_psum = False

    def __getitem__(self, k):
        return self.t[k]


class Eng:
    def __init__(self, fw, name, eng):
        self.fw = fw
        self.name = name
        self.e = eng
        self.sem = fw.new_sem("e_" + name)
        self.waited = {}

    def _need(self, ev):
        if ev is None:
            return
        s, v = ev
        if self.waited.get(s, 0) >= v:
            return
        self.e.wait_ge(s.h, v)
        self.waited[s] = v

    def sync(self, reads, writes):
        need = {}

        def add(ev):
            if ev is None:
                return
            s_, v_ = ev
            if need.get(s_, 0) < v_:
                need[s_] = v_
        for b in reads:
            add(b.last_w)
            if b.is_psum:
                for ev in b.readers:
                    if ev[0] is not self.sem:
                        add(ev)
        for b in writes:
            add(b.last_w)
            for ev in b.readers:
                if ev[0] is self.sem:
                    continue
                add(ev)
        for s_, v_ in need.items():
            self._need((s_, v_))

    def _commit(self, ev, reads, writes):
        for b in writes:
            b.last_w = ev
            b.readers = []
        for b in reads:
            if b not in writes:
                b.readers.append(ev)
                if len(b.readers) > 16:
                    best = {}
                    for s, v in b.readers:
                        if best.get(s, 0) < v:
                            best[s] = v
                    b.readers = list(best.items())

    def op(self, fn, reads=(), writes=()):
        self.sync(reads, writes)
        ins = fn(self.e)
        ins.then_inc(self.sem.h, 1)
        self.sem.val += 1
        self._commit((self.sem, self.sem.val), reads, writes)

    def dma(self, out, in_, reads=(), writes=(), sem=None):
        self.sync(reads, writes)
        if sem is None:
            b = (list(writes) + list(reads))[0]
            if b.dsem is None:
                b.dsem = {}
            if self.name not in b.dsem:
                b.dsem[self.name] = self.fw.get_dsem(b.name, self.name)
            sem = b.dsem[self.name]
        outs = out if isinstance(out, (list, tuple)) else [out]
        ins = in_ if isinstance(in_, (list, tuple)) else [in_]
        for o, i in zip(outs, ins):
            self.e.dma_start(out=o, in_=i).then_inc(sem.h, 16)
            sem.val += 16
        self._commit((sem, sem.val), reads, writes)


class FW:
    def __init__(self, nc, stack):
        self.nc = nc
        self.stack = stack
        self.sems = []
        self.bufs = []
        self.uid = 0
        self.free_dsems = {}
        self.pe = Eng(self, "pe", nc.tensor)
        self.act = Eng(self, "act", nc.scalar)
        self.dve = Eng(self, "dve", nc.vector)
        self.pool = Eng(self, "pool", nc.gpsimd)
        self.sp = Eng(self, "sp", nc.sync)
        self.engs = [self.pe, self.act, self.dve, self.pool, self.sp]

    def new_sem(self, name):
        self.uid += 1
        name = f"{name}_{self.uid}"
        h = self.stack.enter_context(self.nc.semaphore(name))
        s = Sem(h, name)
        self.sems.append(s)
        return s

    def get_dsem(self, name, qname):
        fl = self.free_dsems.setdefault(qname, [])
        if fl:
            return fl.pop()
        return self.new_sem("d_" + qname + "_" + name)

    def buf(self, name, t=None):
        b = Buf(name, t)
        self.bufs.append(b)
        return b

    def sbuf(self, stack, name, shape, dtype):
        self.uid += 1
        t = stack.enter_context(self.nc.sbuf_tensor(f"{name}_{self.uid}", list(shape), dtype))
        return self.buf(name, t)

    def psum(self, stack, name, shape, dtype):
        self.uid += 1
        t = stack.enter_context(self.nc.psum_tensor(f"{name}_{self.uid}", list(shape), dtype))
        b = self.buf(name, t)
        b.is_psum = True
        return b

    def barrier(self):
        for e in self.engs:
            for s in self.sems:
                if s.val > 0 and e.waited.get(s, 0) < s.val:
                    e.e.wait_ge(s.h, s.val)
                    e.waited[s] = s.val
        for b in self.bufs:
            b.last_w = None
            b.readers = []
        keep = []
        for b in self.bufs:
            if getattr(b, "persist", False):
                keep.append(b)
            elif b.dsem:
                for qn, sm in b.dsem.items():
                    self.free_dsems.setdefault(qn, []).append(sm)
                b.dsem = None
        self.bufs = keep


class Pipe:
    def __init__(self):
        self.items = []

    def tile(self, nstages):
        st = [[] for _ in range(nstages)]
        self.items.append(st)
        return st

    def run(self):
        n = len(self.items)
        if n == 0:
            return
        K = max(len(s) for s in self.items)
        for it in range(n + K - 1):
            for s in range(K - 1, -1, -1):
                j = it - s
                if 0 <= j < n and s < len(self.items[j]):
                    for th in self.items[j][s]:
                        th()


class Rot:
    def __init__(self, items):
        self.items = items
        self.i = 0

    def next(self):
        r = self.items[self.i % len(self.items)]
        self.i += 1
        return r


def build_program():
    nc = bass.Bass("TRN2", target_bir_lowering=False)
    dbg = set(DEBUG_OUT)

    def din(name, shape, dt=F32):
        return nc.dram_tensor(name, list(shape), dt, kind="ExternalInput")

    def dscr(name, shape, dt):
        if name in dbg:
            return nc.dram_tensor(name, list(shape), dt, kind="ExternalOutput")
        return nc.dram_tensor(name, list(shape), dt)

    x_in = din("x", [T, DM])
    w_in = din("w_in", [DEPTH, DM, INW])
    w_out = din("w_out", [DEPTH, DM, DM])
    early = STOP_AFTER is not None and STOP_AFTER.endswith("0") and not STOP_AFTER.startswith("mlp")
    w_up = None if early else din("w_up", [DEPTH, DM, DFF])
    w_down = None if early else din("w_down", [DEPTH, DFF, DM])
    w_uq = din("w_uq", [DEPTH, 384, 768])
    w_ukv = din("w_ukv", [DEPTH, 256, 1024])
    ln1 = din("ln1_w", [DEPTH, DM])
    ln2 = din("ln2_w", [DEPTH, DM])
    dec_f = din("dec_f", [DEPTH, 4])
    dec_b = din("dec_b", [DEPTH, 4])
    gn_w = din("gn_w", [DEPTH, 512])
    qa_n = din("qa_n", [DEPTH, 384])
    kva_n = din("kva_n", [DEPTH, 256])
    mq_n = din("mq_n", [DEPTH, 192])
    mk_n = din("mk_n", [DEPTH, 192])
    gq_n = din("gq_n", [DEPTH, 128])
    gk_n = din("gk_n", [DEPTH, 128])
    rt_r = din("rt_r", [T, 256])
    rt_rk = din("rt_rk", [T, 256])
    rt_g = din("rt_g", [T, 256])
    rt_m = din("rt_m", [T, 128])
    kbias = din("kbias", [1, 2])
    sscale = din("sscale", [1, 4])
    c_ident = din("c_ident", [128, 128])
    c_ret = din("c_ret", [128, 6, 128])
    c_zc = din("c_zc", [128, 2])
    y_out = nc.dram_tensor("y", [T, DM], F32, kind="ExternalOutput")

    XA = dscr("XA", [T, DM], F32)
    XB = dscr("XB", [T, DM], F32)
    RQT = dscr("RQT", [4, 128, T], BF16)
    RKT = dscr("RKT", [4, 128, T], BF16)
    RK = dscr("RK", [T, 512], BF16)
    RV = dscr("RV", [T, 512], BF16)
    RG = dscr("RG", [T, 512], F32)
    MQT = dscr("MQT", [4, 192, T], BF16)
    GQT = dscr("GQT", [8, 128, T], BF16)
    KROWS = [384, 384, 256]
    KGi_l = [[nc.dram_tensor(f"KGi{l}_{p}", [KROWS[p], T], BF16) for p in range(3)] for l in range(DEPTH)]
    KGo_l = [[nc.dram_tensor(f"KGo{l}_{p}", [2 * KROWS[p], T], BF16) for p in range(3)] for l in range(DEPTH)]
    VGi_l = [[nc.dram_tensor(f"VGi{l}_{p}", [T // 2, 768], BF16) for p in range(2)] for l in range(DEPTH)]
    VGo_l = [[nc.dram_tensor(f"VGo{l}_{p}", [T, 768], BF16) for p in range(2)] for l in range(DEPTH)]
    SG_in_l = [nc.dram_tensor(f"SG_in{l}", [256, 512], F32) for l in range(DEPTH)]
    SG_out_l = [nc.dram_tensor(f"SG_out{l}", [512, 512], F32) for l in range(DEPTH)]

    ncr = NCORES if DEBUG_CORES is None else len(DEBUG_CORES)
    RG_PAIRS = [[2 * i, 2 * i + 1] for i in range(ncr // 2)]
    KVD = nc.dram_tensor("KVD", [NT, 2, 128, 512], F32)

    with ExitStack() as top:
        fw = FW(nc, top)
        pe, act, dve, pool, sp = fw.pe, fw.act, fw.dve, fw.pool, fw.sp
        cc_sem = fw.new_sem("cc")

        def P(b):
            b.persist = True
            return b

        ident = P(fw.sbuf(top, "ident", [128, 128], BF16))
        ones = P(fw.sbuf(top, "ones", [128, 128], BF16))
        czc = P(fw.sbuf(top, "czc", [128, 2], F32))
        kb = P(fw.sbuf(top, "kb", [128, 2], F32))
        ssc = P(fw.sbuf(top, "ssc", [128, 4], F32))
        lg = P(fw.sbuf(top, "lg", [128, 8], F32))
        zfb = P(fw.sbuf(top, "zfb", [128, 8], F32))
        pool.dma(ident[:], c_ident.ap(), writes=[ident])
        sp.dma(czc[:], c_zc.ap(), writes=[czc])
        sp.dma(kb[:], kbias.ap()[0, :].partition_broadcast(128), writes=[kb])
        sp.dma(ssc[:], sscale.ap()[0, :].partition_broadcast(128), writes=[ssc])
        dve.op(lambda e: e.memset(ones[:], 1.0), writes=[ones])

        def stop_here(name):
            return STOP_AFTER == name

        def finish():
            fw.barrier()

        def layer_consts(L):
            with ExitStack() as ps:
                tmp = fw.sbuf(ps, "lc_tmp", [128, 8], F32)
                sp.dma([tmp[:, 0:4], tmp[:, 4:8]],
                       [dec_f.ap()[L, :].partition_broadcast(128), dec_b.ap()[L, :].partition_broadcast(128)],
                       writes=[tmp])
                act.op(lambda e: e.activation(out=tmp[:], in_=tmp[:], func=AF.Exp, scale=-1.0), reads=[tmp], writes=[tmp])
                act.op(lambda e: e.activation(out=tmp[:], in_=tmp[:], func=AF.Ln, bias=1.0), reads=[tmp], writes=[tmp])
                dve.op(lambda e: e.tensor_scalar(out=lg[:], in0=tmp[:], scalar1=-1.0, scalar2=None, op0=ALU.mult),
                       reads=[tmp], writes=[lg])
                fw.barrier()

        def phase_norm(src, lnvec, hT):
            with ExitStack() as ps:
                xt = Rot([fw.sbuf(ps, f"n_x{i}", [128, DM], F32) for i in range(3)])
                hn = Rot([fw.sbuf(ps, f"n_hn{i}", [128, DM], BF16) for i in range(3)])
                junk = fw.sbuf(ps, "n_junk", [128, DM], BF16)
                ss = Rot([fw.sbuf(ps, f"n_ss{i}", [128, 2], F32) for i in range(3)])
                pT = Rot([fw.psum(ps, f"n_pT{i}", [128, 1024], BF16) for i in range(6)])
                lnw = fw.sbuf(ps, "n_lnw", [128, DM], F32)
                sp.dma(lnw[:], lnvec.partition_broadcast(128), writes=[lnw])
                pipe = Pipe()

                def tile(t):
                    S_ = pipe.tile(5)
                    X = xt.next(); H = hn.next(); S = ss.next()
                    S_[0].append(lambda: sp.dma(X[:], src[t * 128:(t + 1) * 128, :], writes=[X]))
                    S_[1].append(lambda: act.op(lambda e: e.activation(out=junk[:], in_=X[:], func=AF.Square, accum_out=S[:, 0:1]),
                                                reads=[X], writes=[junk, S]))
                    S_[1].append(lambda: act.op(lambda e: e.activation(out=S[:, 1:2], in_=S[:, 0:1], func=AF.Sqrt, scale=1.0 / DM, bias=EPS),
                                                reads=[S], writes=[S]))
                    S_[2].append(lambda: dve.op(lambda e: e.reciprocal(out=S[:, 1:2], in_=S[:, 1:2]), reads=[S], writes=[S]))
                    S_[2].append(lambda: dve.op(lambda e: e.scalar_tensor_tensor(out=H[:], in0=X[:], scalar=S[:, 1:2], in1=lnw[:],
                                                                                 op0=ALU.mult, op1=ALU.mult),
                                                reads=[X, S, lnw], writes=[H]))
                    for half in range(2):
                        Pt = pT.next()

                        def trf(half=half, Pt=Pt):
                            def tr(e):
                                for j in range(8):
                                    k = half * 8 + j
                                    i = e.transpose(out=Pt[:, j * 128:(j + 1) * 128], in_=H[:, k * 128:(k + 1) * 128], identity=ident[:])
                                return i
                            pe.op(tr, reads=[H, ident], writes=[Pt])
                        S_[3].append(trf)
                        dst = hT.t[:, half * 8:(half + 1) * 8, t * 128:(t + 1) * 128]
                        src_ps = Pt[:, :].rearrange("p (k c) -> p k c", k=8)
                        if half == 0:
                            S_[4].append(lambda dst=dst, s_=src_ps, Pt=Pt: act.op(lambda e: e.activation(out=dst, in_=s_, func=AF.Copy), reads=[Pt], writes=[hT.tiles[t]]))
                        else:
                            S_[4].append(lambda dst=dst, s_=src_ps, Pt=Pt: dve.op(lambda e: e.tensor_copy(out=dst, in_=s_), reads=[Pt], writes=[hT.tiles[t]]))
                for t in range(NT):
                    tile(t)
                pipe.run()
                fw.barrier()

        def phase_inproj(L, hT):
            KGi, VGi = KGi_l[L], VGi_l[L]
            with ExitStack() as ps:
                wb = Rot([fw.sbuf(ps, f"b_w{i}", [128, 16, 512], BF16) for i in range(2)])
                tabb = fw.sbuf(ps, "b_tab", [128, NT, 256], F32)
                Xs = Rot([fw.sbuf(ps, f"b_X{i}", [128, 768], F32) for i in range(3)])
                tAs = Rot([fw.sbuf(ps, f"b_tA{i}", [128, 512], F32) for i in range(3)])
                tBs = Rot([fw.sbuf(ps, f"b_tB{i}", [128, 512], F32) for i in range(3)])
                Os = Rot([fw.sbuf(ps, f"b_O{i}", [128, 768], BF16) for i in range(3)])
                sts = Rot([fw.sbuf(ps, f"b_st{i}", [128, 16], F32) for i in range(6)])
                cT_all = fw.sbuf(ps, "b_cTall", [128, 5, T], BF16)
                kr_all = fw.sbuf(ps, "b_krall", [128, NT, 64], F32)
                cTq = [fw.buf(f"cTq{i}") for i in range(NT)]
                cTk = [fw.buf(f"cTk{i}") for i in range(NT)]
                kr_tiles = [fw.buf(f"krt{i}") for i in range(NT)]
                STa = Rot([fw.sbuf(ps, f"b_STa{i}", [128, 4, 512], BF16) for i in range(2)])
                STb = Rot([fw.sbuf(ps, f"b_STb{i}", [128, 4, 512], BF16) for i in range(2)])
                Vb = Rot([fw.sbuf(ps, f"b_V{i}", [128, 512], BF16) for i in range(3)])
                Gb = Rot([fw.sbuf(ps, f"b_G{i}", [128, 512], F32) for i in range(2)])
                junk = fw.sbuf(ps, "b_junk", [128, 384], BF16)
                wqa = fw.sbuf(ps, "wqa", [128, 384], F32)
                wkva = fw.sbuf(ps, "wkva", [128, 256], F32)
                wmq = fw.sbuf(ps, "wmq", [128, 192], F32)
                wmk = fw.sbuf(ps, "wmk", [128, 192], F32)
                wgq = fw.sbuf(ps, "wgq", [128, 128], F32)
                wgk = fw.sbuf(ps, "wgk", [128, 128], F32)
                wuq = fw.sbuf(ps, "wuq", [128, 3, 768], BF16)
                wukv = fw.sbuf(ps, "wukv", [128, 2, 1024], BF16)
                sp.dma(wqa[:], qa_n.ap()[L, :].partition_broadcast(128), writes=[wqa])
                sp.dma(wkva[:], kva_n.ap()[L, :].partition_broadcast(128), writes=[wkva])
                sp.dma(wmq[:], mq_n.ap()[L, :].partition_broadcast(128), writes=[wmq])
                sp.dma(wmk[:], mk_n.ap()[L, :].partition_broadcast(128), writes=[wmk])
                sp.dma(wgq[:], gq_n.ap()[L, :].partition_broadcast(128), writes=[wgq])
                sp.dma(wgk[:], gk_n.ap()[L, :].partition_broadcast(128), writes=[wgk])
                pool.dma(wuq[:], w_uq.ap()[L].rearrange("(k p) c -> p k c", p=128), writes=[wuq])
                pool.dma(wukv[:], w_ukv.ap()[L].rearrange("(k p) c -> p k c", p=128), writes=[wukv])
                wsrc = w_in.ap()[L].rearrange("(k p) c -> p k c", p=128)
                Wt_of = {}

                def load_w(g):
                    c0, ncol = GROUPS[g]
                    Wt = wb.next()
                    Wt_of[g] = Wt
                    pool.dma([Wt[:, 4 * q:4 * q + 4, 0:ncol] for q in range(4)],
                             [wsrc[:, 4 * q:4 * q + 4, c0:c0 + ncol] for q in range(4)], writes=[Wt])

                def load_tab(src, width):
                    sp.dma(tabb[:, :, 0:width], src.ap().rearrange("(n p) c -> p n c", p=128), writes=[tabb])

                def sq_stats(stg, src_of_h, H, D, st, reads):
                    for h in range(H):
                        stg.append(lambda h=h: act.op(lambda e: e.activation(out=junk[:, 0:D], in_=src_of_h(h), func=AF.Square, accum_out=st[:, h:h + 1]),
                                                      reads=reads, writes=[junk, st]))
                    stg.append(lambda: act.op(lambda e: e.activation(out=st[:, 8:8 + H], in_=st[:, 0:H], func=AF.Sqrt, scale=1.0 / D, bias=EPS),
                                              reads=[st], writes=[st]))

                def sq_stats_sb(stg, X2, X2v, sqA, sqB, st):
                    for half, sq in enumerate((sqA, sqB)):
                        sqv = sq[:, 0:384].rearrange("p (h d) -> p h d", h=2)
                        stg.append(lambda half=half, sq=sq, sqv=sqv: pool.op(
                            lambda e: e.tensor_tensor(out=sqv, in0=X2v[:, 2 * half:2 * half + 2, :], in1=X2v[:, 2 * half:2 * half + 2, :], op=ALU.mult),
                            reads=[X2], writes=[sq]))
                        stg.append(lambda half=half, sq=sq, sqv=sqv: dve.op(
                            lambda e: e.tensor_reduce(out=st[:, 2 * half:2 * half + 2], in_=sqv, axis=AX.X, op=ALU.add), reads=[sq], writes=[st]))
                    stg.append(lambda: act.op(lambda e: e.activation(out=st[:, 8:12], in_=st[:, 0:4], func=AF.Sqrt, scale=1.0 / 192, bias=EPS),
                                              reads=[st], writes=[st]))

                def norm_rope(stg_m, stg_a, X, H, D, st, normw, tap, roff, Dr, O, tA, tB, w_eng=None):
                    w_eng = w_eng or pool
                    Xv = X[:, 0:H * D].rearrange("p (h d) -> p h d", h=H)
                    Ov = O[:, 0:H * D].rearrange("p (h d) -> p h d", h=H)
                    Av = tA[:, 0:H * Dr].rearrange("p (h d) -> p h d", h=H)
                    Bv = tB[:, 0:H * Dr].rearrange("p (h d) -> p h d", h=H)
                    if normw is not None:
                        stg_m.append(lambda: dve.op(lambda e: e.reciprocal(out=st[:, 8:8 + H], in_=st[:, 8:8 + H]), reads=[st], writes=[st]))
                        stg_m.append(lambda: dve.op(lambda e: e.tensor_tensor(out=Xv, in0=Xv, in1=st[:, 8:8 + H].unsqueeze(2).to_broadcast([128, H, D]), op=ALU.mult),
                                                    reads=[X, st], writes=[X]))
                        stg_m.append(lambda: w_eng.op(lambda e: e.tensor_tensor(out=Xv, in0=Xv, in1=normw[:, 0:D].unsqueeze(1).to_broadcast([128, H, D]), op=ALU.mult),
                                                      reads=[X, normw], writes=[X]))
                    hf = Dr // 2
                    C2 = tap[:, 0:Dr].unsqueeze(1).to_broadcast([128, H, Dr])
                    S2a = tap[:, Dr:Dr + hf].unsqueeze(1).to_broadcast([128, H, hf])
                    S2b = tap[:, Dr + hf:2 * Dr].unsqueeze(1).to_broadcast([128, H, hf])
                    stg_m.append(lambda: dve.op(lambda e: e.tensor_tensor(out=Av, in0=Xv[:, :, roff:roff + Dr], in1=C2, op=ALU.mult),
                                                reads=[X, tabb], writes=[tA]))
                    stg_m.append(lambda: pool.op(lambda e: e.tensor_tensor(out=Bv[:, :, 0:hf], in0=Xv[:, :, roff + hf:roff + Dr], in1=S2a, op=ALU.mult),
                                                 reads=[X, tabb], writes=[tB]))
                    stg_m.append(lambda: pool.op(lambda e: e.tensor_tensor(out=Bv[:, :, hf:Dr], in0=Xv[:, :, roff:roff + hf], in1=S2b, op=ALU.mult),
                                                 reads=[X, tabb, tB], writes=[tB]))
                    stg_a.append(lambda: dve.op(lambda e: e.tensor_tensor(out=Ov[:, :, roff:roff + Dr], in0=Av, in1=Bv, op=ALU.add),
                                                reads=[tA, tB], writes=[O]))
                    if roff > 0:
                        stg_a.append(lambda: act.op(lambda e: e.activation(out=Ov[:, :, 0:roff], in_=Xv[:, :, 0:roff], func=AF.Copy), reads=[X, O], writes=[O]))

                def main_mm(stg, g, t, Pm):
                    c0, ncol = GROUPS[g]

                    def f():
                        Wt = Wt_of[g]

                        def mm(e):
                            for k in range(16):
                                i = e.matmul(Pm[:, 0:ncol], lhsT=hT.t[:, k, t * 128:(t + 1) * 128], rhs=Wt[:, k, 0:ncol],
                                             start=(k == 0), stop=(k == 15))
                            return i
                        pe.op(mm, reads=[hT.tiles[t], Wt], writes=[Pm])
                    stg.append(f)

                def tr_op(stg, O, H, D, Pt, parts):
                    Ptv = Pt[:, :].rearrange("p (s c) -> p s c", s=8)

                    def f():
                        def tr(e):
                            i = None
                            for h in range(H):
                                for (d0, dn, s0) in parts:
                                    i = e.transpose(out=Ptv[0:dn, s0 + h, :], in_=O[:, h * D + d0:h * D + d0 + dn], identity=ident[:])
                            return i
                        pe.op(tr, reads=[O, ident], writes=[Pt])
                    stg.append(f)
                    return Ptv

                def block_store(g, tb, sta, stb):
                    tc = slice(tb * 512, (tb + 1) * 512)
                    if g == 0:
                        sp.dma(RQT.ap().rearrange("h d t -> d h t")[:, :, tc], sta[:], reads=[sta])
                    elif g == 1:
                        sp.dma(RKT.ap().rearrange("h d t -> d h t")[:, :, tc], sta[:], reads=[sta])
                    elif g == 4:
                        mv = MQT.ap().rearrange("h d t -> d h t")
                        sp.dma(mv[0:128, :, tc], sta[:], reads=[sta])
                        sp.dma(mv[128:192, :, tc], stb[64:128, :, :], reads=[stb])
                    elif g == 5:
                        for pc in range(2):
                            kv_ = KGi[pc].ap().rearrange("(h d) t -> d h t", d=192)
                            sp.dma(kv_[0:128, :, tc], sta[:, 2 * pc:2 * pc + 2, :], reads=[sta])
                            sp.dma(kv_[128:192, :, tc], stb[64:128, 2 * pc:2 * pc + 2, :], reads=[stb])
                    elif g in (6, 7):
                        gv_ = GQT.ap().rearrange("h d t -> d h t")
                        sp.dma(gv_[:, (g - 6) * 4:(g - 6) * 4 + 4, tc], sta[:], reads=[sta])
                    elif g == 8:
                        kg = KGi[2].ap().rearrange("(h d) t -> d h t", d=128)
                        sp.dma(kg[:, :, tc], sta[:, 0:2, :], reads=[sta])

                def simple_tile(pipe, g, t, pmm, pT, sta):
                    j = t % 4
                    tb = t // 4
                    S = pipe.tile(6)
                    rows = slice(t * 128, (t + 1) * 128)
                    cols = slice(j * 128, (j + 1) * 128)
                    Pm = pmm.next()
                    if t == 0:
                        if g + 1 < len(GROUPS):
                            S[0].append(lambda: load_w(g + 1))
                        if g == 1:
                            S[2].append(lambda: load_tab(rt_rk, 256))
                        if g == 6:
                            S[2].append(lambda: load_tab(rt_g, 256))
                    main_mm(S[0], g, t, Pm)
                    if g == 2:
                        V = Vb.next()
                        S[1].append(lambda: act.op(lambda e: e.activation(out=V[:], in_=Pm[:, 0:512], func=AF.Copy), reads=[Pm], writes=[V]))
                        S[1].append(lambda: sp.dma(RV.ap()[rows, :], V[:], reads=[V]))
                        return
                    if g == 3:
                        G = Gb.next()
                        S[1].append(lambda: act.op(lambda e: e.activation(out=G[:], in_=Pm[:, 0:512], func=AF.Silu), reads=[Pm], writes=[G]))
                        S[1].append(lambda: sp.dma(RG.ap()[rows, :], G[:], reads=[G]))
                        return
                    H = 2 if g == 8 else 4
                    X = Xs.next(); O = Os.next(); tA = tAs.next(); tB = tBs.next(); st = sts.next()
                    normw = {0: None, 1: None, 6: wgq, 7: wgq, 8: wgk}[g]
                    S[1].append(lambda: act.op(lambda e: e.activation(out=X[:, 0:H * 128], in_=Pm[:, 0:H * 128], func=AF.Copy), reads=[Pm], writes=[X]))
                    if g == 8:
                        V = Vb.next()
                        S[1].append(lambda: act.op(lambda e: e.activation(out=V[:, 0:256], in_=Pm[:, 256:512], func=AF.Copy), reads=[Pm], writes=[V]))
                        S[1].append(lambda: sp.dma(VGi[t // 8].ap()[(t % 8) * 128:(t % 8 + 1) * 128, 512:768], V[:, 0:256], reads=[V]))
                    if normw is not None:
                        sq_stats(S[1], lambda h: Pm[:, h * 128:(h + 1) * 128], H, 128, st, [Pm])
                    norm_rope(S[2], S[3], X, H, 128, st, normw, tabb[:, t, 0:256], 0, 128, O, tA, tB)
                    if g == 1:
                        S[3].append(lambda: sp.dma(RK.ap()[rows, :], O[:, 0:512], reads=[O]))
                    Pt = pT.next()
                    Ptv = tr_op(S[4], O, H, 128, Pt, [(0, 128, 0)])
                    S[5].append(lambda: dve.op(lambda e: e.tensor_copy(out=sta[:, 0:H, cols], in_=Ptv[:, 0:H, :]), reads=[Pt], writes=[sta]))
                    if j == 3:
                        S[5].append(lambda: block_store(g, tb, sta, None))

                def mla_a_tile(pipe, g, t, pmm, pT):
                    S = pipe.tile(6)
                    Dc = 384 if g == 4 else 256
                    nk = Dc // 128
                    c0 = 0 if g == 4 else 3
                    Pm = pmm.next()
                    if t == 0:
                        S[0].append(lambda: load_w(g + 1))
                    main_mm(S[0], g, t, Pm)
                    X = Xs.next(); O = Os.next(); st = sts.next()
                    S[1].append(lambda: act.op(lambda e: e.activation(out=X[:, 0:Dc], in_=Pm[:, 0:Dc], func=AF.Copy), reads=[Pm], writes=[X]))
                    if g == 5:
                        S[1].append(lambda: act.op(lambda e: e.activation(out=kr_all[:, t, :], in_=Pm[:, 256:320], func=AF.Copy), reads=[Pm], writes=[kr_tiles[t]]))
                    sq_stats(S[1], lambda h: Pm[:, 0:Dc], 1, Dc, st, [Pm])
                    wn = wqa if g == 4 else wkva
                    S[2].append(lambda: dve.op(lambda e: e.reciprocal(out=st[:, 8:9], in_=st[:, 8:9]), reads=[st], writes=[st]))
                    S[2].append(lambda: dve.op(lambda e: e.scalar_tensor_tensor(out=O[:, 0:Dc], in0=X[:, 0:Dc], scalar=st[:, 8:9], in1=wn[:, 0:Dc],
                                                                                op0=ALU.mult, op1=ALU.mult), reads=[X, st, wn], writes=[O]))
                    Pt = pT.next()
                    Ptv = Pt[:, :].rearrange("p (s c) -> p s c", s=8)

                    def trf():
                        def tr(e):
                            for k in range(nk):
                                i = e.transpose(out=Ptv[:, k, :], in_=O[:, k * 128:(k + 1) * 128], identity=ident[:])
                            return i
                        pe.op(tr, reads=[O, ident], writes=[Pt])
                    S[4].append(trf)
                    cb = cTq[t] if g == 4 else cTk[t]
                    S[5].append(lambda: dve.op(lambda e: e.tensor_copy(out=cT_all[:, c0:c0 + nk, t * 128:(t + 1) * 128], in_=Ptv[:, 0:nk, :]), reads=[Pt], writes=[cb]))

                def mla_b_tile(pipe, g, t, pmm, pT, sta, stb):
                    j = t % 4
                    tb = t // 4
                    S = pipe.tile(6)
                    cols = slice(j * 128, (j + 1) * 128)
                    nk = 3 if g == 4 else 2
                    c0 = 0 if g == 4 else 3
                    wsec = wuq if g == 4 else wukv
                    hw = 384 if g == 4 else 512
                    p2 = [pmm.next(), pmm.next()]
                    cb = cTq[t] if g == 4 else cTk[t]
                    if g == 5 and t == 0:
                        S[2].append(lambda: load_tab(rt_m, 128))

                    def mm2f():
                        def mm2(e):
                            for half in range(2):
                                for k in range(nk):
                                    i = e.matmul(p2[half][:, 0:hw], lhsT=cT_all[:, c0 + k, t * 128:(t + 1) * 128], rhs=wsec[:, k, half * hw:(half + 1) * hw],
                                                 start=(k == 0), stop=(k == nk - 1))
                            return i
                        pe.op(mm2, reads=[cb, wsec], writes=[p2[0], p2[1]])
                    S[0].append(mm2f)
                    X2 = Xs.next(); O2 = Os.next(); tA2 = tAs.next(); tB2 = tBs.next(); st2 = sts.next()
                    X2v = X2[:, 0:768].rearrange("p (h d) -> p h d", h=4)
                    if g == 4:
                        for half in range(2):
                            S[1].append(lambda half=half: dve.op(lambda e: e.tensor_copy(
                                out=X2v[:, 2 * half:2 * half + 2, :], in_=p2[half][:, 0:384].rearrange("p (h d) -> p h d", h=2)),
                                reads=[p2[half]], writes=[X2]))
                        wn2 = wmq
                    else:
                        V = Vb.next()
                        Vv = V[:, :].rearrange("p (h e) -> p h e", h=4)
                        for half in range(2):
                            pv = p2[half][:, 0:512].rearrange("p (h two e) -> p h two e", h=2, two=2)
                            S[1].append(lambda half=half, pv=pv: act.op(lambda e: e.activation(out=Vv[:, 2 * half:2 * half + 2, :], in_=pv[:, :, 1, :], func=AF.Copy),
                                                                        reads=[p2[half]], writes=[V]))
                            S[1].append(lambda half=half, pv=pv: dve.op(lambda e: e.tensor_copy(out=X2v[:, 2 * half:2 * half + 2, 0:128], in_=pv[:, :, 0, :]),
                                                                        reads=[p2[half]], writes=[X2]))
                        S[1].append(lambda: sp.dma(VGi[t // 8].ap()[(t % 8) * 128:(t % 8 + 1) * 128, 0:512], V[:], reads=[V]))
                        S[1].append(lambda: pool.op(lambda e: e.tensor_copy(out=X2v[:, :, 128:192], in_=kr_all[:, t, :].unsqueeze(1).to_broadcast([128, 4, 64])),
                                                    reads=[kr_tiles[t], X2], writes=[X2]))
                        wn2 = wmk
                    sq_stats(S[1], lambda h: X2v[:, h, :], 4, 192, st2, [X2])
                    norm_rope(S[2], S[3], X2, 4, 192, st2, wn2, tabb[:, t, 0:128], 128, 64, O2, tA2, tB2, w_eng=dve)
                    Pt2 = pT.next()
                    Ptv2 = tr_op(S[4], O2, 4, 192, Pt2, [(0, 128, 0), (64, 128, 4)])
                    S[5].append(lambda: dve.op(lambda e: e.tensor_copy(out=sta[:, :, cols], in_=Ptv2[:, 0:4, :]), reads=[Pt2], writes=[sta]))
                    S[5].append(lambda: act.op(lambda e: e.activation(out=stb[64:128, :, cols], in_=Ptv2[64:128, 4:8, :], func=AF.Copy), reads=[Pt2], writes=[stb]))
                    if j == 3:
                        S[5].append(lambda: block_store(g, tb, sta, stb))

                load_w(0)
                load_tab(rt_r, 256)
                with ExitStack() as pss:
                    pipe = Pipe()
                    pmm = Rot([fw.psum(pss, f"b_pm{i}", [128, 512], F32) for i in range(4)])
                    pT = Rot([fw.psum(pss, f"b_pT{i}", [128, 1024], BF16) for i in range(3)])
                    for g in range(len(GROUPS)):
                        for tb in range(4):
                            sta = STa.next() if g not in (2, 3, 4, 5) else None
                            for j in range(4):
                                t = tb * 4 + j
                                if g in (4, 5):
                                    mla_a_tile(pipe, g, t, pmm, pT)
                                else:
                                    simple_tile(pipe, g, t, pmm, pT, sta)
                    def issue_gathers():
                        for s_ in fw.sems:
                            if s_.name.startswith("d_sp_") and s_.val > pool.waited.get(s_, 0):
                                pool.e.wait_ge(s_.h, s_.val)
                                pool.waited[s_] = s_.val
                        for p_ in range(3):
                            gather(KGi[p_], KGo_l[L][p_])
                        for p_ in range(2):
                            gather(VGi[p_], VGo_l[L][p_])
                    for g in (5, 4):
                        for tb in range(4):
                            sta = STa.next()
                            stb = STb.next()
                            for j in range(4):
                                mla_b_tile(pipe, g, tb * 4 + j, pmm, pT, sta, stb)
                        if g == 5:
                            pipe.tile(6)[5].append(issue_gathers)
                    pipe.run()
                    fw.barrier()

        def gather(src, dst):
            pool.e.collective_compute("AllGather", ALU.bypass, replica_groups=RG_PAIRS,
                                      ins=[src.ap().opt()], outs=[dst.ap().opt()]).then_inc(cc_sem.h, 1)
            cc_sem.val += 1

        def wait_gathers():
            for e in fw.engs:
                e.e.wait_ge(cc_sem.h, cc_sem.val)
                e.waited[cc_sem] = cc_sem.val

        def phase_ret_kv(L, Sf, gct):
            SG_in, SG_out = SG_in_l[L], SG_out_l[L]
            with ExitStack() as ps:
                Kb = Rot([fw.sbuf(ps, f"r_K{i}", [128, 512], BF16) for i in range(8)])
                Vb = Rot([fw.sbuf(ps, f"r_V{i}", [128, 512], BF16) for i in range(8)])
                Kz = Rot([fw.sbuf(ps, f"r_Kz{i}", [128, 512], BF16) for i in range(6)])
                KVt = Rot([fw.sbuf(ps, f"r_KVt{i}", [128, 512], F32) for i in range(6)])
                pkv = Rot([fw.psum(ps, f"r_pkv{i}", [128, 512], F32) for i in range(4)])
                tmp = fw.sbuf(ps, "r_tmp", [128, 8], F32)
                act.op(lambda e: e.activation(out=tmp[:], in_=lg[:], func=AF.Exp, scale=128.0), reads=[lg], writes=[tmp])
                for d in range(2):
                    dve.op(lambda e, d=d: e.tensor_copy(
                        out=gct[:, d, :].rearrange("p (h e) -> p h e", h=4),
                        in_=tmp[:, 4 * d:4 * d + 4].unsqueeze(2).to_broadcast([128, 4, 128])),
                        reads=[tmp], writes=[gct])
                for h in range(4):
                    act.op(lambda e, h=h: e.activation(out=zfb[:, h:h + 1], in_=czc[:, 0:1], func=AF.Exp, scale=lg[:, h:h + 1]),
                           reads=[lg, czc], writes=[zfb])
                    act.op(lambda e, h=h: e.activation(out=zfb[:, 4 + h:5 + h], in_=czc[:, 1:2], func=AF.Exp, scale=lg[:, 4 + h:5 + h]),
                           reads=[lg, czc], writes=[zfb])
                dve.op(lambda e: e.memset(Sf[:], 0.0), writes=[Sf])
                pipe = Pipe()

                def kvtile(i, d):
                    n = i if d == 0 else NT - 1 - i
                    S_ = pipe.tile(5)
                    K = Kb.next(); V = Vb.next(); Z = Kz.next(); KV = KVt.next(); Pk = pkv.next()
                    rows = slice(n * 128, (n + 1) * 128)
                    S_[0].append(lambda: sp.dma(K[:], RK.ap()[rows, :], writes=[K]))
                    S_[0].append(lambda: sp.dma(V[:], RV.ap()[rows, :], writes=[V]))
                    eng = dve if d == 0 else pool
                    S_[1].append(lambda: eng.op(lambda e: e.tensor_tensor(
                        out=Z[:, :].rearrange("p (h d) -> p h d", h=4), in0=K[:, :].rearrange("p (h d) -> p h d", h=4),
                        in1=zfb[:, 4 * d:4 * d + 4].unsqueeze(2).to_broadcast([128, 4, 128]), op=ALU.mult),
                        reads=[K, zfb], writes=[Z]))

                    def mmf():
                        def mm(e):
                            for h in range(4):
                                i_ = e.matmul(Pk[:, h * 128:(h + 1) * 128], lhsT=Z[:, h * 128:(h + 1) * 128], rhs=V[:, h * 128:(h + 1) * 128],
                                              start=True, stop=True)
                            return i_
                        pe.op(mm, reads=[Z, V], writes=[Pk])
                    S_[2].append(mmf)
                    S_[3].append(lambda: act.op(lambda e: e.activation(out=KV[:], in_=Pk[:], func=AF.Copy), reads=[Pk], writes=[KV]))
                    S_[4].append(lambda: sp.dma(KVD.ap()[n, d], KV[:], reads=[KV]))
                    S_[4].append(lambda: dve.op(lambda e: e.tensor_tensor(out=Sf[:, d, :], in0=Sf[:, d, :], in1=gct[:, d, :], op=ALU.mult), reads=[Sf, gct], writes=[Sf]))
                    S_[4].append(lambda: dve.op(lambda e: e.tensor_tensor(out=Sf[:, d, :], in0=Sf[:, d, :], in1=KV[:], op=ALU.add), reads=[Sf, KV], writes=[Sf]))
                for i in range(NT):
                    for d in range(2):
                        kvtile(i, d)
                pipe.run()
                sp.dma(SG_in.ap().rearrange("(d p) c -> p d c", p=128), Sf[:], reads=[Sf])
                fw.barrier()
                gather(SG_in, SG_out)

        def phase_attn(L, mixT, PV, Sf, gct):
            KGo, VGo = KGo_l[L], VGo_l[L]
            SG_out = SG_out_l[L]
            with ExitStack() as ps:
                KTa = Rot([fw.sbuf(ps, f"a_KTa{i}", [128, 2, T], BF16) for i in range(2)])
                KTb = Rot([fw.sbuf(ps, f"a_KTb{i}", [128, 2, T], BF16) for i in range(2)])
                Vt = Rot([fw.sbuf(ps, f"a_V{i}", [128, 32, 128], BF16) for i in range(2)])
                QTa = Rot([fw.sbuf(ps, f"a_QTa{i}", [128, 512], BF16) for i in range(3)])
                QTb = Rot([fw.sbuf(ps, f"a_QTb{i}", [128, 512], BF16) for i in range(3)])
                Pb = Rot([fw.sbuf(ps, f"a_P{i}", [128, 512], BF16) for i in range(6)])
                rsb = Rot([fw.sbuf(ps, f"a_rs{i}", [128, 512], F32) for i in range(2)])
                psc = Rot([fw.psum(ps, f"a_ps{i}", [128, 512], F32) for i in range(4)])
                pob = Rot([fw.psum(ps, f"a_po{i}", [128, 512], F32) for i in range(2)])
                psb = Rot([fw.psum(ps, f"a_pz{i}", [128, 512], F32) for i in range(2)])
                kgo = [k_.ap().rearrange("(b r) t -> r b t", b=2) for k_ in KGo]

                def load_v(V, c0):
                    outs, ins = [], []
                    for b_ in range(2):
                        for i_ in range(2):
                            outs.append(V[:, b_ * 16 + i_ * 8:b_ * 16 + i_ * 8 + 8, :])
                            ins.append(VGo[i_].ap()[b_ * 1024:(b_ + 1) * 1024, c0:c0 + 128].rearrange("(n p) c -> p n c", p=128))
                    sp.dma(outs, ins, writes=[V])
                jobs = []
                for h in range(4):
                    jobs.append(dict(kind="mla", kv=h, qs=[h], scale=192 ** -0.5))
                for kvh in range(2):
                    jobs.append(dict(kind="gqa", kv=kvh, qs=[kvh * 4 + i for i in range(4)], scale=128 ** -0.5))
                def load_kv(ji):
                    jb = jobs[ji]
                    mla = jb["kind"] == "mla"
                    Ka = KTa.next(); V = Vt.next()
                    jb["Ka"], jb["V"] = Ka, V
                    if mla:
                        Kb_ = KTb.next()
                        jb["Kb"] = Kb_
                        kg_ = kgo[jb["kv"] // 2]
                        r0 = (jb["kv"] % 2) * 192
                        sp.dma(Ka[:], kg_[r0:r0 + 128, :, :], writes=[Ka])
                        sp.dma(Kb_[64:128, :, :], kg_[r0 + 128:r0 + 192, :, :], writes=[Kb_])
                        load_v(V, jb["kv"] * 128)
                    else:
                        r0 = jb["kv"] * 128
                        sp.dma(Ka[:], kgo[2][r0:r0 + 128, :, :], writes=[Ka])
                        load_v(V, 512 + jb["kv"] * 128)

                items = [(ji, qh, qb) for ji, jb in enumerate(jobs) for qh in jb["qs"] for qb in range(4)]
                Qof = {}

                def load_q(i):
                    ji, qh, qb = items[i]
                    mla = jobs[ji]["kind"] == "mla"
                    qc = slice(qb * 512, (qb + 1) * 512)
                    Qa = QTa.next()
                    Qb_ = None
                    if mla:
                        Qb_ = QTb.next()
                        sp.dma(Qa[:], MQT.ap()[qh, 0:128, qc], writes=[Qa])
                        sp.dma(Qb_[64:128, :], MQT.ap()[qh, 128:192, qc], writes=[Qb_])
                    else:
                        sp.dma(Qa[:], GQT.ap()[qh, :, qc], writes=[Qa])
                    Qof[i] = (Qa, Qb_)

                Sin = fw.sbuf(ps, "a_Sin", [128, 4, 512], F32)
                kvts = Rot([fw.sbuf(ps, f"a_kvt{i}", [128, 2, 512], F32) for i in range(3)])
                kvof = {}

                def scan_load(i_):
                    kvt = kvts.next()
                    kvof[i_] = kvt
                    sp.dma([kvt[:, 0, :], kvt[:, 1, :]], [KVD.ap()[i_, 0], KVD.ap()[NT - 1 - i_, 1]], writes=[kvt])

                def scan_init():
                    sp.dma(Sin[:], SG_out.ap().rearrange("(b p) c -> p b c", p=128), writes=[Sin])
                    for d in range(2):
                        dve.op(lambda e, d=d: e.tensor_scalar(out=Sf[:, d, :], in0=Sin[:, d, :], scalar1=ssc[:, 2 * d:2 * d + 1], scalar2=None, op0=ALU.mult),
                               reads=[Sin, ssc, Sf], writes=[Sf])
                        dve.op(lambda e, d=d: e.scalar_tensor_tensor(out=Sf[:, d, :], in0=Sin[:, 2 + d, :], scalar=ssc[:, 2 * d + 1:2 * d + 2], in1=Sf[:, d, :],
                                                                     op0=ALU.mult, op1=ALU.add),
                               reads=[Sin, ssc, Sf], writes=[Sf])
                    scan_load(0)

                def scan_step(i_):
                    if i_ + 1 < NT:
                        scan_load(i_ + 1)
                    kvt = kvof.pop(i_)
                    pool.op(lambda e: e.tensor_copy(out=PV.t[:, i_, 0, :], in_=Sf[:, 0, :]), reads=[Sf], writes=[PV.tiles[i_]])
                    pool.op(lambda e: e.tensor_copy(out=PV.t[:, NT - 1 - i_, 1, :], in_=Sf[:, 1, :]), reads=[Sf], writes=[PV.tiles[NT - 1 - i_]])
                    dve.op(lambda e: e.tensor_tensor(out=Sf[:], in0=Sf[:], in1=gct[:], op=ALU.mult), reads=[Sf, gct], writes=[Sf])
                    dve.op(lambda e: e.tensor_tensor(out=Sf[:], in0=Sf[:], in1=kvt[:], op=ALU.add), reads=[Sf, kvt], writes=[Sf])

                assert len(items) == 3 * NT
                load_kv(0)
                load_q(0)
                scan_init()
                for i, (ji, qh, qb) in enumerate(items):
                    if i % 3 == 2 and i // 3 < NT:
                        scan_step(i // 3)
                    jb = jobs[ji]
                    mla = jb["kind"] == "mla"
                    first_of_job = (i == 0) or (items[i - 1][0] != ji)
                    if first_of_job and ji + 1 < len(jobs):
                        load_kv(ji + 1)
                    if i + 1 < len(items):
                        load_q(i + 1)
                    Ka, V = jb["Ka"], jb["V"]
                    Kb_ = jb.get("Kb")
                    Qa, Qb_ = Qof.pop(i)
                    chunk = (4 + qh) if mla else (8 + qh)
                    qc = slice(qb * 512, (qb + 1) * 512)
                    po = pob.next(); pz = psb.next()

                    def qk_mm(e, Ps, kt, Ka=Ka, Kb_=Kb_, Qa=Qa, Qb_=Qb_, mla=mla):
                        blk, off = kt // 16, (kt % 16) * 128
                        i_ = e.matmul(Ps[:], lhsT=Ka[:, blk, off:off + 128], rhs=Qa[:], start=True, stop=not mla)
                        if mla:
                            i_ = e.matmul(Ps[:], lhsT=Kb_[64:128, blk, off:off + 128], rhs=Qb_[64:128, :], start=False, stop=True)
                        return i_
                    qk_reads = [Ka, Qa] + ([Kb_, Qb_] if mla else [])

                    def qk(kt):
                        Ps_ = psc.next()
                        pe.op(lambda e: qk_mm(e, Ps_, kt), reads=qk_reads, writes=[Ps_])
                        return Ps_
                    pend = [qk(0), qk(1)]
                    for kt in range(32):
                        Ps = pend.pop(0)
                        if kt + 2 < 32:
                            pend.append(qk(kt + 2))
                        Pt_ = Pb.next()
                        blk = kt // 16
                        act.op(lambda e, Ps=Ps, Pt_=Pt_, blk=blk: e.activation(out=Pt_[:], in_=Ps[:], func=AF.Exp, scale=jb["scale"], bias=kb[:, blk:blk + 1]),
                               reads=[Ps, kb], writes=[Pt_])

                        def pvf(e, kt=kt, Pt_=Pt_, V=V, po=po, pz=pz):
                            e.matmul(po[:], lhsT=V[:, kt, :], rhs=Pt_[:], start=(kt == 0), stop=(kt == 31))
                            return e.matmul(pz[:], lhsT=ones[:], rhs=Pt_[:], start=(kt == 0), stop=(kt == 31))
                        pe.op(pvf, reads=[V, Pt_, ones], writes=[po, pz])
                    rs = rsb.next()
                    dve.op(lambda e: e.reciprocal(out=rs[:], in_=pz[:]), reads=[pz], writes=[rs])
                    dve.op(lambda e: e.tensor_tensor(out=mixT.t[:, chunk, qc], in0=po[:], in1=rs[:], op=ALU.mult),
                           reads=[po, rs], writes=[mixT.chunks[chunk]])
                fw.barrier()

        def phase_ret_out(L, PV, mixT):
            with ExitStack() as ps:
                QTb = Rot([fw.sbuf(ps, f"o_QT{i}", [128, 4, 512], BF16) for i in range(2)])
                KTb = Rot([fw.sbuf(ps, f"o_KT{i}", [128, 4, 512], BF16) for i in range(2)])
                Vb = Rot([fw.sbuf(ps, f"o_V{i}", [128, 512], BF16) for i in range(5)])
                Gb = Rot([fw.sbuf(ps, f"o_G{i}", [128, 512], F32) for i in range(5)])
                Pm = Rot([fw.sbuf(ps, f"o_P{i}", [128, 512], BF16) for i in range(3)])
                Qf = Rot([fw.sbuf(ps, f"o_Qf{i}", [128, 2, 512], BF16) for i in range(3)])
                Yb = Rot([fw.sbuf(ps, f"o_Y{i}", [128, 512], F32) for i in range(4)])
                Sq = Rot([fw.sbuf(ps, f"o_Sq{i}", [128, 512], F32) for i in range(2)])
                Ro = Rot([fw.sbuf(ps, f"o_R{i}", [128, 512], BF16) for i in range(3)])
                stt = Rot([fw.sbuf(ps, f"o_st{i}", [128, 16], F32) for i in range(4)])
                psc = Rot([fw.psum(ps, f"o_ps{i}", [128, 512], F32) for i in range(2)])
                pyb = Rot([fw.psum(ps, f"o_py{i}", [128, 512], F32) for i in range(2)])
                pT = Rot([fw.psum(ps, f"o_pT{i}", [128, 1024], BF16) for i in range(2)])
                cret = fw.sbuf(ps, "o_cret", [128, 6, 128], F32)
                inn = fw.sbuf(ps, "o_inn", [128, 2, 512], F32)
                dtot = fw.sbuf(ps, "o_dtot", [128, 512], F32)
                gnw = fw.sbuf(ps, "o_gnw", [128, 512], F32)
                tm2 = fw.sbuf(ps, "o_tm2", [128, 128], F32)
                sp.dma(cret[:], c_ret.ap(), writes=[cret])
                sp.dma(gnw[:], gn_w.ap()[L, :].partition_broadcast(128), writes=[gnw])
                for h in range(4):
                    act.op(lambda e, h=h: e.activation(out=inn[:, 0, h * 128:(h + 1) * 128], in_=cret[:, 4, :], func=AF.Exp, scale=lg[:, h:h + 1]),
                           reads=[lg, cret], writes=[inn])
                    act.op(lambda e, h=h: e.activation(out=inn[:, 1, h * 128:(h + 1) * 128], in_=cret[:, 5, :], func=AF.Exp, scale=lg[:, 4 + h:5 + h]),
                           reads=[lg, cret], writes=[inn])
                for h in range(4):
                    dsl = dtot[:, h * 128:(h + 1) * 128]
                    act.op(lambda e, h=h, dsl=dsl: e.activation(out=dsl, in_=cret[:, 0, :], func=AF.Exp, scale=lg[:, h:h + 1]),
                           reads=[lg, cret], writes=[dtot])
                    dve.op(lambda e, dsl=dsl: e.tensor_tensor(out=dsl, in0=dsl, in1=cret[:, 1, :], op=ALU.mult), reads=[dtot, cret], writes=[dtot])
                    act.op(lambda e, h=h: e.activation(out=tm2[:], in_=cret[:, 2, :], func=AF.Exp, scale=lg[:, 4 + h:5 + h]),
                           reads=[lg, cret], writes=[tm2])
                    dve.op(lambda e: e.tensor_tensor(out=tm2[:], in0=tm2[:], in1=cret[:, 3, :], op=ALU.mult), reads=[tm2, cret], writes=[tm2])
                    dve.op(lambda e, dsl=dsl: e.tensor_tensor(out=dsl, in0=dsl, in1=tm2[:], op=ALU.add), reads=[dtot, tm2], writes=[dtot])
                pipe = Pipe()

                def tile(n, QT, KT, first):
                    S_ = pipe.tile(9)
                    tb, j = n // 4, n % 4
                    tc = slice(tb * 512, (tb + 1) * 512)
                    rows = slice(n * 128, (n + 1) * 128)
                    cols = slice(j * 128, (j + 1) * 128)
                    V = Vb.next(); G = Gb.next()
                    if first:
                        S_[0].append(lambda: sp.dma(QT[:], RQT.ap().rearrange("h d t -> d h t")[:, :, tc], writes=[QT]))
                        S_[0].append(lambda: sp.dma(KT[:], RKT.ap().rearrange("h d t -> d h t")[:, :, tc], writes=[KT]))
                    S_[0].append(lambda: sp.dma(V[:], RV.ap()[rows, :], writes=[V]))
                    S_[3].append(lambda: sp.dma(G[:], RG.ap()[rows, :], writes=[G]))
                    Ps = psc.next()

                    def scf():
                        def sc(e):
                            for h in range(4):
                                i = e.matmul(Ps[:, h * 128:(h + 1) * 128], lhsT=KT[:, h, cols], rhs=QT[:, h, cols], start=True, stop=True)
                            return i
                        pe.op(sc, reads=[KT, QT], writes=[Ps])
                    S_[1].append(scf)
                    Pmt = Pm.next()
                    S_[2].append(lambda: dve.op(lambda e: e.tensor_tensor(out=Pmt[:], in0=Ps[:], in1=dtot[:], op=ALU.mult), reads=[Ps, dtot], writes=[Pmt]))
                    Q2 = Qf.next()
                    for d in range(2):
                        S_[2].append(lambda d=d: pool.op(lambda e: e.tensor_tensor(
                            out=Q2[:, d, :].rearrange("p (h c) -> p h c", h=4), in0=QT[:, :, cols],
                            in1=inn[:, d, :].rearrange("p (h c) -> p h c", h=4), op=ALU.mult),
                            reads=[QT, inn], writes=[Q2]))
                    Py = pyb.next()

                    def ymf():
                        def ym(e):
                            for h in range(4):
                                hs = slice(h * 128, (h + 1) * 128)
                                e.matmul(Py[:, hs], lhsT=Pmt[:, hs], rhs=V[:, hs], start=True, stop=False)
                                e.matmul(Py[:, hs], lhsT=Q2[:, 0, hs], rhs=PV.t[:, n, 0, hs], start=False, stop=False)
                                i = e.matmul(Py[:, hs], lhsT=Q2[:, 1, hs], rhs=PV.t[:, n, 1, hs], start=False, stop=True)
                            return i
                        pe.op(ym, reads=[Pmt, V, Q2, PV.tiles[n]], writes=[Py])
                    S_[3].append(ymf)
                    Y = Yb.next(); S2 = Sq.next(); st = stt.next(); R = Ro.next()
                    Yv = Y[:, :].rearrange("p (h e) -> p h e", h=4)
                    S2v = S2[:, :].rearrange("p (h e) -> p h e", h=4)
                    S_[4].append(lambda: act.op(lambda e: e.activation(out=Y[:], in_=Py[:], func=AF.Copy), reads=[Py], writes=[Y]))
                    S_[4].append(lambda: dve.op(lambda e: e.tensor_reduce(out=st[:, 0:4], in_=Yv, axis=AX.X, op=ALU.add), reads=[Y], writes=[st]))
                    S_[4].append(lambda: dve.op(lambda e: e.tensor_scalar(out=st[:, 0:4], in0=st[:, 0:4], scalar1=-1.0 / 128, scalar2=None, op0=ALU.mult), reads=[st], writes=[st]))
                    S_[4].append(lambda: dve.op(lambda e: e.tensor_tensor(out=Yv, in0=Yv, in1=st[:, 0:4].unsqueeze(2).to_broadcast([128, 4, 128]), op=ALU.add),
                                                reads=[Y, st], writes=[Y]))
                    S_[5].append(lambda: pool.op(lambda e: e.tensor_tensor(out=S2[:], in0=Y[:], in1=Y[:], op=ALU.mult), reads=[Y], writes=[S2]))
                    S_[5].append(lambda: dve.op(lambda e: e.tensor_reduce(out=st[:, 4:8], in_=S2v, axis=AX.X, op=ALU.add), reads=[S2], writes=[st]))
                    S_[5].append(lambda: act.op(lambda e: e.activation(out=st[:, 8:12], in_=st[:, 4:8], func=AF.Sqrt, scale=1.0 / 128, bias=EPS), reads=[st], writes=[st]))
                    S_[5].append(lambda: dve.op(lambda e: e.reciprocal(out=st[:, 8:12], in_=st[:, 8:12]), reads=[st], writes=[st]))
                    S_[6].append(lambda: dve.op(lambda e: e.tensor_tensor(out=Yv, in0=Yv, in1=st[:, 8:12].unsqueeze(2).to_broadcast([128, 4, 128]), op=ALU.mult),
                                                reads=[Y, st], writes=[Y]))
                    S_[6].append(lambda: pool.op(lambda e: e.tensor_tensor(out=Y[:], in0=Y[:], in1=gnw[:], op=ALU.mult), reads=[Y, gnw], writes=[Y]))
                    S_[6].append(lambda: dve.op(lambda e: e.tensor_tensor(out=R[:], in0=Y[:], in1=G[:], op=ALU.mult), reads=[Y, G], writes=[R]))
                    Pt = pT.next()

                    def trf():
                        def tr(e):
                            for h in range(4):
                                i = e.transpose(out=Pt[:, h * 128:(h + 1) * 128], in_=R[:, h * 128:(h + 1) * 128], identity=ident[:])
                            return i
                        pe.op(tr, reads=[R, ident], writes=[Pt])
                    S_[7].append(trf)
                    S_[8].append(lambda: act.op(lambda e: e.activation(out=mixT.t[:, 0:4, n * 128:(n + 1) * 128],
                                                                       in_=Pt[:, 0:512].rearrange("p (h c) -> p h c", h=4), func=AF.Copy),
                                                reads=[Pt], writes=[mixT.chunks[0]]))
                for tb in range(4):
                    QT = QTb.next(); KT = KTb.next()
                    for j in range(4):
                        tile(tb * 4 + j, QT, KT, j == 0)
                pipe.run()
                fw.barrier()

        def phase_outproj(L, mixT, src, dst):
            with ExitStack() as ps:
                wb = Rot([fw.sbuf(ps, f"d_w{i}", [128, 16, 512], BF16) for i in range(2)])
                xs = Rot([fw.sbuf(ps, f"d_x{i}", [128, 512], F32) for i in range(4)])
                pm = Rot([fw.psum(ps, f"d_pm{i}", [128, 512], F32) for i in range(4)])
                wsrc = w_out.ap()[L].rearrange("(k p) c -> p k c", p=128)
                Wts = {}

                def load_w(cg):
                    Wt = wb.next()
                    Wts[cg] = Wt
                    cc = slice(cg * 512, (cg + 1) * 512)
                    pool.dma([Wt[:, 4 * q:4 * q + 4, :] for q in range(4)], [wsrc[:, 4 * q:4 * q + 4, cc] for q in range(4)], writes=[Wt])
                items = [(cg, t) for cg in range(4) for t in range(NT)]
                Xof = {}

                def load_x(i):
                    cg, t = items[i]
                    X = xs.next()
                    Xof[i] = X
                    sp.dma(X[:], src[t * 128:(t + 1) * 128, cg * 512:(cg + 1) * 512], writes=[X])
                load_w(0)
                load_x(0)
                load_x(1)
                for i, (cg, t) in enumerate(items):
                    if t == 0 and cg + 1 < 4:
                        load_w(cg + 1)
                    if i + 2 < len(items):
                        load_x(i + 2)
                    Wt = Wts[cg]
                    X = Xof.pop(i)
                    Pm_ = pm.next()

                    def mm(e, t=t, Pm_=Pm_, Wt=Wt):
                        for k in range(16):
                            i_ = e.matmul(Pm_[:], lhsT=mixT.t[:, k, t * 128:(t + 1) * 128], rhs=Wt[:, k, :], start=(k == 0), stop=(k == 15))
                        return i_
                    pe.op(mm, reads=list(mixT.chunks) + [Wt], writes=[Pm_])
                    dve.op(lambda e, X=X, Pm_=Pm_: e.tensor_tensor(out=X[:], in0=Pm_[:], in1=X[:], op=ALU.add), reads=[Pm_, X], writes=[X])
                    sp.dma(dst[t * 128:(t + 1) * 128, cg * 512:(cg + 1) * 512], X[:], reads=[X])
                fw.barrier()

        def phase_mlp(L, hT, uT, srcs, dsts):
            with ExitStack() as ps:
                wu = Rot([fw.sbuf(ps, f"f_wu{i}", [128, 16, 128], BF16) for i in range(4)])
                wd = Rot([fw.sbuf(ps, f"f_wd{i}", [128, 16, 512], BF16) for i in range(2)])
                xs = Rot([fw.sbuf(ps, f"f_x{i}", [128, 512], F32) for i in range(4)])
                rl = Rot([fw.sbuf(ps, f"f_rl{i}", [128, 512], F32) for i in range(4)])
                pu = Rot([[fw.psum(ps, f"f_pu{i}{h}", [128, 512], F32) for h in range(2)] for i in range(3)])
                pd = Rot([fw.psum(ps, f"f_pd{i}", [128, 512], F32) for i in range(2)])
                wus = w_up.ap()[L].rearrange("(k p) f -> p k f", p=128)
                wds = w_down.ap()[L].rearrange("(c p) d -> p c d", p=128)
                Wus, Wds = {}, {}

                def load_wu(q):
                    if q >= 64:
                        return
                    Wu = wu.next()
                    Wus[q] = Wu
                    pool.dma(Wu[:], wus[:, :, q * 128:(q + 1) * 128], writes=[Wu])

                def load_wd(q):
                    if q >= 16:
                        return
                    fq, cg = q // 4, q % 4
                    Wd = wd.next()
                    Wds[q] = Wd
                    cc = slice(cg * 512, (cg + 1) * 512)
                    pool.dma([Wd[:, 4 * r:4 * r + 4, :] for r in range(4)],
                             [wds[:, fq * 16 + 4 * r:fq * 16 + 4 * r + 4, cc] for r in range(4)], writes=[Wd])
                load_wu(0)
                load_wu(1)
                load_wu(2)
                for fq in range(4):
                    src, dst = srcs[fq], dsts[fq]
                    for fc in range(16):
                        q = fq * 16 + fc
                        load_wu(q + 3)
                        if fc == 8:
                            load_wd(fq * 4)
                        Wu = Wus.pop(q)
                        for pair in range(2):
                            Pus = pu.next()

                            def mm(e, Wu=Wu, Pus=Pus, pair=pair):
                                for k in range(16):
                                    for h in range(2):
                                        tb = 2 * pair + h
                                        i = e.matmul(Pus[h][:], lhsT=Wu[:, k, :], rhs=hT.t[:, k, tb * 512:(tb + 1) * 512], start=(k == 0), stop=(k == 15))
                                return i
                            pe.op(mm, reads=list(hT.tiles) + [Wu], writes=Pus)
                            for h in range(2):
                                tb = 2 * pair + h
                                dstu = uT.t[:, fc, tb * 512:(tb + 1) * 512]
                                R = rl.next()
                                act.op(lambda e, h=h, R=R: e.activation(out=R[:], in_=Pus[h][:], func=AF.Relu), reads=[Pus[h]], writes=[R])
                                dve.op(lambda e, h=h, R=R, dstu=dstu: e.tensor_tensor(out=dstu, in0=Pus[h][:], in1=R[:], op=ALU.mult),
                                       reads=[Pus[h], R], writes=[uT.chunks[fc]])
                    items = [(cg, t) for cg in range(4) for t in range(NT)]
                    Xof = {}

                    def load_x(i):
                        cg, t = items[i]
                        X = xs.next()
                        Xof[i] = X
                        sp.dma(X[:], src[t * 128:(t + 1) * 128, cg * 512:(cg + 1) * 512], writes=[X])
                    load_x(0)
                    load_x(1)
                    for i, (cg, t) in enumerate(items):
                        if t == 0:
                            load_wd(fq * 4 + cg + 1) if cg + 1 < 4 else None
                        if i + 2 < len(items):
                            load_x(i + 2)
                        Wd = Wds[fq * 4 + cg]
                        X = Xof.pop(i)
                        Pd = pd.next()

                        def mm2(e, t=t, Pd=Pd, Wd=Wd):
                            for c in range(16):
                                i_ = e.matmul(Pd[:], lhsT=uT.t[:, c, t * 128:(t + 1) * 128], rhs=Wd[:, c, :], start=(c == 0), stop=(c == 15))
                            return i_
                        pe.op(mm2, reads=list(uT.chunks) + [Wd], writes=[Pd])
                        dve.op(lambda e, X=X, Pd=Pd: e.tensor_tensor(out=X[:], in0=Pd[:], in1=X[:], op=ALU.add), reads=[Pd, X], writes=[X])
                        sp.dma(dst[t * 128:(t + 1) * 128, cg * 512:(cg + 1) * 512], X[:], reads=[X])
                    fw.barrier()

        def big(stack, name, shape, dtype, nsub, attr):
            b = fw.sbuf(stack, name, shape, dtype)
            setattr(b, attr, [fw.buf(f"{name}_{i}") for i in range(nsub)])
            return b

        def run():
            if stop_here("consts"):
                return
            for L in range(DEPTH):
                x_src = x_in.ap() if L == 0 else XB.ap()
                layer_consts(L)
                if stop_here(f"lconsts{L}"):
                    return
                with ExitStack() as s1:
                    hT = big(s1, "hT", [128, 16, T], BF16, NT, "tiles")
                    phase_norm(x_src, ln1.ap()[L, :], hT)
                    if stop_here(f"norm{L}"):
                        return
                    phase_inproj(L, hT)
                if stop_here(f"inproj{L}") or (STOP_AFTER or "").startswith("inprojG") or (STOP_AFTER or "").startswith("inprojO"):
                    return
                with ExitStack() as s_pv:
                    PV = big(s_pv, "PV", [128, NT, 2, 512], BF16, NT, "tiles")
                    Sf = fw.sbuf(s_pv, "Sf", [128, 2, 512], F32)
                    gct = fw.sbuf(s_pv, "gct", [128, 2, 512], F32)
                    mixT = big(s_pv, "mixT", [128, 16, T], BF16, 16, "chunks")
                    phase_ret_kv(L, Sf, gct)
                    if stop_here(f"retkv{L}"):
                        return
                    wait_gathers()
                    phase_attn(L, mixT, PV, Sf, gct)
                    if stop_here(f"attn{L}"):
                        return
                    phase_ret_out(L, PV, mixT)
                    if stop_here(f"retout{L}"):
                        return
                    phase_outproj(L, mixT, x_src, XA.ap())
                if stop_here(f"outproj{L}"):
                    return
                with ExitStack() as s5:
                    hT = big(s5, "h2T", [128, 16, T], BF16, NT, "tiles")
                    uT = big(s5, "uT", [128, 16, T], BF16, 16, "chunks")
                    phase_norm(XA.ap(), ln2.ap()[L, :], hT)
                    last = y_out.ap() if L == DEPTH - 1 else XB.ap()
                    phase_mlp(L, hT, uT, [XA.ap(), XB.ap(), XB.ap(), XB.ap()], [XB.ap(), XB.ap(), XB.ap(), last])
                if stop_here(f"mlp{L}"):
                    return

        run()
        finish()
    return nc


def _rope_tables(pos):
    theta = 10000.0
    pos = pos.astype(np.float32)
    inv = (1.0 / (theta ** (np.arange(0, 128, 2, dtype=np.float32) / 128))).astype(np.float32)
    ang = pos[:, None] * inv[None, :]
    c, s = np.cos(ang).astype(np.float32), np.sin(ang).astype(np.float32)
    rt_r = np.concatenate([c, c, -s, s], axis=1).astype(np.float32)
    rt_rk = (rt_r * np.float32(128 ** -0.5)).astype(np.float32)

    def axial(dim):
        half = dim // 2
        inv = (1.0 / (theta ** (np.arange(0, half, 2, dtype=np.float32) / half))).astype(np.float32)
        row = np.floor(pos / 64).astype(np.float32)
        col = (pos - row * 64).astype(np.float32)
        ang = np.concatenate([row[:, None] * inv[None, :], col[:, None] * inv[None, :]], axis=1)
        c, s = np.cos(ang).astype(np.float32), np.sin(ang).astype(np.float32)
        return np.concatenate([c, c, -s, s], axis=1).astype(np.float32)
    return rt_r, rt_rk, axial(128), axial(64)


def _consts():
    m = np.arange(128, dtype=np.float32)[:, None]
    c = np.arange(128, dtype=np.float32)[None, :]
    A1 = np.maximum(c - m, 0.0)
    M1 = (m <= c).astype(np.float32)
    A2 = np.maximum(m - c, 0.0)
    M2 = (m > c).astype(np.float32)
    io1 = np.broadcast_to(c + 1.0, (128, 128))
    io2 = np.broadcast_to(128.0 - c, (128, 128))
    cret = np.stack([A1, M1, A2, M2, io1, io2], axis=1).astype(np.float32)
    p = np.arange(128, dtype=np.float32)
    czc = np.stack([127.0 - p, p], axis=1).astype(np.float32)
    return cret, czc


_NC_CACHE = {}


def kernel(x_prompt, x_sample, ln1_w, w_in, ret_decay_fwd, ret_decay_bwd, ret_gn_w,
           mla_q_a_norm, mla_w_uq, mla_kv_a_norm, mla_w_ukv, mla_q_norm, mla_k_norm,
           gqa_q_norm, gqa_k_norm, w_out, ln2_w, w_up, w_down):
    f = lambda a: np.ascontiguousarray(np.asarray(a, dtype=np.float32))
    x_prompt, x_sample = f(x_prompt), f(x_sample)
    key = (STOP_AFTER, tuple(DEBUG_OUT), tuple(DEBUG_CORES or []))
    if key not in _NC_CACHE:
        _NC_CACHE[key] = build_program()
    nc = _NC_CACHE[key]
    cret, czc = _consts()
    shared = {
        "w_in": f(w_in), "w_out": f(w_out), "w_up": f(w_up), "w_down": f(w_down),
        "w_uq": f(mla_w_uq), "w_ukv": f(mla_w_ukv), "ln1_w": f(ln1_w), "ln2_w": f(ln2_w),
        "dec_f": f(ret_decay_fwd), "dec_b": f(ret_decay_bwd), "gn_w": f(ret_gn_w),
        "qa_n": f(mla_q_a_norm), "kva_n": f(mla_kv_a_norm), "mq_n": f(mla_q_norm), "mk_n": f(mla_k_norm),
        "gq_n": f(gqa_q_norm), "gk_n": f(gqa_k_norm),
        "c_ident": np.eye(128, dtype=np.float32), "c_ret": cret, "c_zc": czc,
    }
    early = STOP_AFTER is not None and STOP_AFTER.endswith("0") and not STOP_AFTER.startswith("mlp")
    if early:
        shared.pop("w_up"); shared.pop("w_down")
    in_maps = []
    cores = list(range(NCORES)) if DEBUG_CORES is None else list(DEBUG_CORES)
    for c in cores:
        if c < 4:
            xs = x_prompt[c]
            pos0 = 0
            rank = c % 2
            kbias = np.array([[0.0 if rank == 0 else NEG, 0.0 if rank == 1 else NEG]], dtype=np.float32)
            ssc = np.zeros((1, 4), dtype=np.float32)
        else:
            seq = (c - 4) // 2
            rank = c % 2
            pos0 = rank * T
            xs = x_sample[seq, pos0:pos0 + T]
            kbias = np.zeros((1, 2), dtype=np.float32)
            ssc = np.array([[0, 0, 0, 1]] if rank == 0 else [[1, 0, 0, 0]], dtype=np.float32)
        rt_r, rt_rk, rt_g, rt_m = _rope_tables(np.arange(pos0, pos0 + T))
        m = dict(shared)
        m.update({"x": np.ascontiguousarray(xs), "rt_r": rt_r, "rt_rk": rt_rk, "rt_g": rt_g, "rt_m": rt_m,
                  "kbias": kbias, "sscale": ssc})
        in_maps.append(m)
    res = run_bass_kernel_spmd(nc, in_maps, core_ids=list(range(len(cores))))
    kernel.last_results = res.results
    if DEBUG_CORES is not None:
        return None
    ys = [np.asarray(r["y"], dtype=np.float32) for r in res.results]
    y_prompt = np.stack(ys[0:4], axis=0)
    y_sample = np.stack([np.concatenate([ys[4], ys[5]], axis=0), np.concatenate([ys[6], ys[7]], axis=0)], axis=0)
    return (y_prompt, y_sample)
```

```python
import numpy as np
from contextlib import ExitStack
import concourse.bass as bass
import concourse.mybir as mybir
from concourse.bass_utils import run_bass_kernel_spmd

F32 = mybir.dt.float32
BF16 = mybir.dt.bfloat16
AF = mybir.ActivationFunctionType
ALU = mybir.AluOpType
AX = mybir.AxisListType

NCORES = 8
T = 2048
NT = T // 128
DM = 2048
DEPTH = 2
INW = 4288
DFF = 8192
EPS = 1e-6
NEG = -30000.0

GROUPS = [(0, 512), (512, 512), (1024, 512), (1536, 512), (2048, 384), (2432, 320),
          (2752, 512), (3264, 512), (3776, 512)]

import os
KDBG = int(os.environ.get('KDBG', '0'))
STOP_AFTER = None
DEBUG_OUT = []
DEBUG_CORES = None


class Sem:
    def __init__(self, handle, name):
        self.h = handle
        self.name = name
        self.val = 0


class Buf:
    def __init__(self, name, t=None):
        self.name = name
        self.t = t
        self.last_w = None
        self.readers = []
        self.dsem = None
        self.is_psum = False

    def __getitem__(self, k):
        return self.t[k]


class Eng:
    def __init__(self, fw, name, eng):
        self.fw = fw
        self.name = name
        self.e = eng
        self.sem = fw.new_sem("e_" + name)
        self.waited = {}

    def _need(self, ev):
        if ev is None:
            return
        s, v = ev
        if self.waited.get(s, 0) >= v:
            return
        self.e.wait_ge(s.h, v)
        self.waited[s] = v

    def sync(self, reads, writes):
        need = {}

        def add(ev):
            if ev is None:
                return
            s_, v_ = ev
            if need.get(s_, 0) < v_:
                need[s_] = v_
        for b in reads:
            add(b.last_w)
            if b.is_psum:
                for ev in b.readers:
                    if ev[0] is not self.sem:
                        add(ev)
        for b in writes:
            add(b.last_w)
            for ev in b.readers:
                if ev[0] is self.sem:
                    continue
                add(ev)
        for s_, v_ in need.items():
            self._need((s_, v_))

    def _commit(self, ev, reads, writes):
        for b in writes:
            b.last_w = ev
            b.readers = []
        for b in reads:
            if b not in writes:
                b.readers.append(ev)
                if len(b.readers) > 16:
                    best = {}
                    for s, v in b.readers:
                        if best.get(s, 0) < v:
                            best[s] = v
                    b.readers = list(best.items())

    def op(self, fn, reads=(), writes=()):
        self.sync(reads, writes)
        ins = fn(self.e)
        ins.then_inc(self.sem.h, 1)
        self.sem.val += 1
        self._commit((self.sem, self.sem.val), reads, writes)

    def dma(self, out, in_, reads=(), writes=(), sem=None):
        self.sync(reads, writes)
        if sem is None:
            b = (list(writes) + list(reads))[0]
            if b.dsem is None:
                b.dsem = {}
            if self.name not in b.dsem:
                b.dsem[self.name] = self.fw.get_dsem(b.name, self.name)
            sem = b.dsem[self.name]
        outs = out if isinstance(out, (list, tuple)) else [out]
        ins = in_ if isinstance(in_, (list, tuple)) else [in_]
        for o, i in zip(outs, ins):
            self.e.dma_start(out=o, in_=i).then_inc(sem.h, 16)
            sem.val += 16
        self._commit((sem, sem.val), reads, writes)


class FW:
    def __init__(self, nc, stack):
        self.nc = nc
        self.stack = stack
        self.sems = []
        self.bufs = []
        self.uid = 0
        self.free_dsems = {}
        self.pe = Eng(self, "pe", nc.tensor)
        self.act = Eng(self, "act", nc.scalar)
        self.dve = Eng(self, "dve", nc.vector)
        self.pool = Eng(self, "pool", nc.gpsimd)
        self.sp = Eng(self, "sp", nc.sync)
        self.engs = [self.pe, self.act, self.dve, self.pool, self.sp]

    def new_sem(self, name):
        self.uid += 1
        name = f"{name}_{self.uid}"
        h = self.stack.enter_context(self.nc.semaphore(name))
        s = Sem(h, name)
        self.sems.append(s)
        return s

    def get_dsem(self, name, qname):
        fl = self.free_dsems.setdefault(qname, [])
        if fl:
            return fl.pop()
        return self.new_sem("d_" + qname + "_" + name)

    def buf(self, name, t=None):
        b = Buf(name, t)
        self.bufs.append(b)
        return b

    def sbuf(self, stack, name, shape, dtype):
        self.uid += 1
        t = stack.enter_context(self.nc.sbuf_tensor(f"{name}_{self.uid}", list(shape), dtype))
        return self.buf(name, t)

    def psum(self, stack, name, shape, dtype):
        self.uid += 1
        t = stack.enter_context(self.nc.psum_tensor(f"{name}_{self.uid}", list(shape), dtype))
        b = self.buf(name, t)
        b.is_psum = True
        return b

    def barrier(self):
        for e in self.engs:
            for s in self.sems:
                if getattr(s, "nobarrier", False):
                    continue
                if s.val > 0 and e.waited.get(s, 0) < s.val:
                    e.e.wait_ge(s.h, s.val)
                    e.waited[s] = s.val
        for b in self.bufs:
            b.last_w = None
            b.readers = []
        keep = []
        for b in self.bufs:
            if getattr(b, "persist", False):
                keep.append(b)
            elif b.dsem:
                for qn, sm in b.dsem.items():
                    self.free_dsems.setdefault(qn, []).append(sm)
                b.dsem = None
        self.bufs = keep


class Pipe:
    def __init__(self):
        self.items = []

    def tile(self, nstages):
        st = [[] for _ in range(nstages)]
        self.items.append(st)
        return st

    def run(self):
        n = len(self.items)
        if n == 0:
            return
        K = max(len(s) for s in self.items)
        for it in range(n + K - 1):
            for s in range(K - 1, -1, -1):
                j = it - s
                if 0 <= j < n and s < len(self.items[j]):
                    for th in self.items[j][s]:
                        th()


class Rot:
    def __init__(self, items):
        self.items = items
        self.i = 0

    def next(self):
        r = self.items[self.i % len(self.items)]
        self.i += 1
        return r


def build_program():
    nc = bass.Bass("TRN2", target_bir_lowering=False)
    dbg = set(DEBUG_OUT)

    def din(name, shape, dt=F32):
        return nc.dram_tensor(name, list(shape), dt, kind="ExternalInput")

    def dscr(name, shape, dt):
        if name in dbg:
            return nc.dram_tensor(name, list(shape), dt, kind="ExternalOutput")
        return nc.dram_tensor(name, list(shape), dt)

    x_in = din("x", [T, DM])
    w_in = din("w_in", [DEPTH, DM, INW])
    w_out = din("w_out", [DEPTH, DM, DM])
    early = STOP_AFTER is not None and STOP_AFTER.endswith("0") and not STOP_AFTER.startswith("mlp")
    w_up = None if early else din("w_up", [DEPTH, DM, DFF])
    w_down = None if early else din("w_down", [DEPTH, DFF, DM])
    w_uq = din("w_uq", [DEPTH, 384, 768])
    w_ukv = din("w_ukv", [DEPTH, 256, 1024])
    ln1 = din("ln1_w", [DEPTH, DM])
    ln2 = din("ln2_w", [DEPTH, DM])
    dec_f = din("dec_f", [DEPTH, 4])
    dec_b = din("dec_b", [DEPTH, 4])
    gn_w = din("gn_w", [DEPTH, 512])
    qa_n = din("qa_n", [DEPTH, 384])
    kva_n = din("kva_n", [DEPTH, 256])
    mq_n = din("mq_n", [DEPTH, 192])
    mk_n = din("mk_n", [DEPTH, 192])
    gq_n = din("gq_n", [DEPTH, 128])
    gk_n = din("gk_n", [DEPTH, 128])
    rt_r = din("rt_r", [T, 256])
    rt_rk = din("rt_rk", [T, 256])
    rt_g = din("rt_g", [T, 256])
    rt_m = din("rt_m", [T, 128])
    kbias = din("kbias", [1, 2])
    sscale = din("sscale", [1, 4])
    c_ident = din("c_ident", [128, 128])
    c_ret = din("c_ret", [128, 6, 128])
    c_zc = din("c_zc", [128, 2])
    y_out = nc.dram_tensor("y", [T, DM], F32, kind="ExternalOutput")

    XA = dscr("XA", [T, DM], F32)
    XB = dscr("XB", [T, DM], F32)
    RQT = dscr("RQT", [4, 128, T], BF16)
    RKT = dscr("RKT", [4, 128, T], BF16)
    RK = dscr("RK", [T, 512], BF16)
    RV = dscr("RV", [T, 512], BF16)
    RG = dscr("RG", [T, 512], F32)
    MQT = dscr("MQT", [4, 192, T], BF16)
    GQT = dscr("GQT", [8, 128, T], BF16)
    KROWS = [384, 384, 256]
    KGi_l = [[nc.dram_tensor(f"KGi{l}_{p}", [KROWS[p], T], BF16) for p in range(3)] for l in range(DEPTH)]
    KGo_l = [[nc.dram_tensor(f"KGo{l}_{p}", [2 * KROWS[p], T], BF16) for p in range(3)] for l in range(DEPTH)]
    VGi_l = [[nc.dram_tensor(f"VGi{l}_{p}", [T // 2, 768], BF16) for p in range(2)] for l in range(DEPTH)]
    VGo_l = [[nc.dram_tensor(f"VGo{l}_{p}", [T, 768], BF16) for p in range(2)] for l in range(DEPTH)]
    SG_in_l = [nc.dram_tensor(f"SG_in{l}", [256, 512], F32) for l in range(DEPTH)]
    SG_out_l = [nc.dram_tensor(f"SG_out{l}", [512, 512], F32) for l in range(DEPTH)]

    ncr = NCORES if DEBUG_CORES is None else len(DEBUG_CORES)
    RG_PAIRS = [[2 * i, 2 * i + 1] for i in range(ncr // 2)]
    KVD = nc.dram_tensor("KVD", [NT, 2, 128, 512], F32)

    with ExitStack() as top:
        fw = FW(nc, top)
        pe, act, dve, pool, sp = fw.pe, fw.act, fw.dve, fw.pool, fw.sp
        cc_sem = fw.new_sem("cc")
        cc_sem.nobarrier = True

        def P(b):
            b.persist = True
            return b

        ident = P(fw.sbuf(top, "ident", [128, 128], BF16))
        ones = P(fw.sbuf(top, "ones", [128, 128], BF16))
        czc = P(fw.sbuf(top, "czc", [128, 2], F32))
        kb = P(fw.sbuf(top, "kb", [128, 2], F32))
        ssc = P(fw.sbuf(top, "ssc", [128, 4], F32))
        lg = P(fw.sbuf(top, "lg", [128, 8], F32))
        zfb = P(fw.sbuf(top, "zfb", [128, 8], F32))
        pool.dma(ident[:], c_ident.ap(), writes=[ident])
        sp.dma(czc[:], c_zc.ap(), writes=[czc])
        sp.dma(kb[:], kbias.ap()[0, :].partition_broadcast(128), writes=[kb])
        sp.dma(ssc[:], sscale.ap()[0, :].partition_broadcast(128), writes=[ssc])
        dve.op(lambda e: e.memset(ones[:], 1.0), writes=[ones])

        def stop_here(name):
            return STOP_AFTER == name

        def finish():
            cc_sem.nobarrier = False
            fw.barrier()

        def layer_consts(L):
            with ExitStack() as ps:
                tmp = fw.sbuf(ps, "lc_tmp", [128, 8], F32)
                sp.dma([tmp[:, 0:4], tmp[:, 4:8]],
                       [dec_f.ap()[L, :].partition_broadcast(128), dec_b.ap()[L, :].partition_broadcast(128)],
                       writes=[tmp])
                act.op(lambda e: e.activation(out=tmp[:], in_=tmp[:], func=AF.Exp, scale=-1.0), reads=[tmp], writes=[tmp])
                act.op(lambda e: e.activation(out=tmp[:], in_=tmp[:], func=AF.Ln, bias=1.0), reads=[tmp], writes=[tmp])
                dve.op(lambda e: e.tensor_scalar(out=lg[:], in0=tmp[:], scalar1=-1.0, scalar2=None, op0=ALU.mult),
                       reads=[tmp], writes=[lg])
                fw.barrier()

        def phase_norm(src, lnvec, hT):
            with ExitStack() as ps:
                xt = Rot([fw.sbuf(ps, f"n_x{i}", [128, DM], F32) for i in range(3)])
                hn = Rot([fw.sbuf(ps, f"n_hn{i}", [128, DM], BF16) for i in range(3)])
                junk = fw.sbuf(ps, "n_junk", [128, DM], BF16)
                ss = Rot([fw.sbuf(ps, f"n_ss{i}", [128, 2], F32) for i in range(3)])
                pT = Rot([fw.psum(ps, f"n_pT{i}", [128, 1024], BF16) for i in range(6)])
                lnw = fw.sbuf(ps, "n_lnw", [128, DM], F32)
                sp.dma(lnw[:], lnvec.partition_broadcast(128), writes=[lnw])
                pipe = Pipe()

                def tile(t):
                    S_ = pipe.tile(5)
                    X = xt.next(); H = hn.next(); S = ss.next()
                    S_[0].append(lambda: sp.dma(X[:], src[t * 128:(t + 1) * 128, :], writes=[X]))
                    S_[1].append(lambda: act.op(lambda e: e.activation(out=junk[:], in_=X[:], func=AF.Square, accum_out=S[:, 0:1]),
                                                reads=[X], writes=[junk, S]))
                    S_[1].append(lambda: act.op(lambda e: e.activation(out=S[:, 1:2], in_=S[:, 0:1], func=AF.Sqrt, scale=1.0 / DM, bias=EPS),
                                                reads=[S], writes=[S]))
                    S_[2].append(lambda: dve.op(lambda e: e.reciprocal(out=S[:, 1:2], in_=S[:, 1:2]), reads=[S], writes=[S]))
                    S_[2].append(lambda: dve.op(lambda e: e.scalar_tensor_tensor(out=H[:], in0=X[:], scalar=S[:, 1:2], in1=lnw[:],
                                                                                 op0=ALU.mult, op1=ALU.mult),
                                                reads=[X, S, lnw], writes=[H]))
                    for half in range(2):
                        Pt = pT.next()

                        def trf(half=half, Pt=Pt):
                            def tr(e):
                                for j in range(8):
                                    k = half * 8 + j
                                    i = e.transpose(out=Pt[:, j * 128:(j + 1) * 128], in_=H[:, k * 128:(k + 1) * 128], identity=ident[:])
                                return i
                            pe.op(tr, reads=[H, ident], writes=[Pt])
                        S_[3].append(trf)
                        dst = hT.t[:, half * 8:(half + 1) * 8, t * 128:(t + 1) * 128]
                        src_ps = Pt[:, :].rearrange("p (k c) -> p k c", k=8)
                        if half == 0:
                            S_[4].append(lambda dst=dst, s_=src_ps, Pt=Pt: act.op(lambda e: e.activation(out=dst, in_=s_, func=AF.Copy), reads=[Pt], writes=[hT.tiles[t]]))
                        else:
                            S_[4].append(lambda dst=dst, s_=src_ps, Pt=Pt: dve.op(lambda e: e.tensor_copy(out=dst, in_=s_), reads=[Pt], writes=[hT.tiles[t]]))
                for t in range(NT):
                    tile(t)
                pipe.run()
                fw.barrier()

        def phase_inproj(L, hT):
            KGi, VGi = KGi_l[L], VGi_l[L]
            with ExitStack() as ps:
                wb = Rot([fw.sbuf(ps, f"b_w{i}", [128, 16, 512], BF16) for i in range(2)])
                tabb = fw.sbuf(ps, "b_tab", [128, NT, 256], F32)
                Xs = Rot([fw.sbuf(ps, f"b_X{i}", [128, 768], F32) for i in range(3)])
                tAs = Rot([fw.sbuf(ps, f"b_tA{i}", [128, 512], F32) for i in range(3)])
                tBs = Rot([fw.sbuf(ps, f"b_tB{i}", [128, 512], F32) for i in range(3)])
                Os = Rot([fw.sbuf(ps, f"b_O{i}", [128, 768], BF16) for i in range(3)])
                sts = Rot([fw.sbuf(ps, f"b_st{i}", [128, 16], F32) for i in range(6)])
                cT_all = fw.sbuf(ps, "b_cTall", [128, 5, T], BF16)
                kr_all = fw.sbuf(ps, "b_krall", [128, NT, 64], F32)
                cTq = [fw.buf(f"cTq{i}") for i in range(NT)]
                cTk = [fw.buf(f"cTk{i}") for i in range(NT)]
                kr_tiles = [fw.buf(f"krt{i}") for i in range(NT)]
                STa = Rot([fw.sbuf(ps, f"b_STa{i}", [128, 4, 512], BF16) for i in range(2)])
                STb = Rot([fw.sbuf(ps, f"b_STb{i}", [128, 4, 512], BF16) for i in range(2)])
                Vb = Rot([fw.sbuf(ps, f"b_V{i}", [128, 512], BF16) for i in range(3)])
                Gb = Rot([fw.sbuf(ps, f"b_G{i}", [128, 512], F32) for i in range(2)])
                junk = fw.sbuf(ps, "b_junk", [128, 384], BF16)
                wqa = fw.sbuf(ps, "wqa", [128, 384], F32)
                wkva = fw.sbuf(ps, "wkva", [128, 256], F32)
                wmq = fw.sbuf(ps, "wmq", [128, 192], F32)
                wmk = fw.sbuf(ps, "wmk", [128, 192], F32)
                wgq = fw.sbuf(ps, "wgq", [128, 128], F32)
                wgk = fw.sbuf(ps, "wgk", [128, 128], F32)
                wuq = fw.sbuf(ps, "wuq", [128, 3, 768], BF16)
                wukv = fw.sbuf(ps, "wukv", [128, 2, 1024], BF16)
                sp.dma(wqa[:], qa_n.ap()[L, :].partition_broadcast(128), writes=[wqa])
                sp.dma(wkva[:], kva_n.ap()[L, :].partition_broadcast(128), writes=[wkva])
                sp.dma(wmq[:], mq_n.ap()[L, :].partition_broadcast(128), writes=[wmq])
                sp.dma(wmk[:], mk_n.ap()[L, :].partition_broadcast(128), writes=[wmk])
                sp.dma(wgq[:], gq_n.ap()[L, :].partition_broadcast(128), writes=[wgq])
                sp.dma(wgk[:], gk_n.ap()[L, :].partition_broadcast(128), writes=[wgk])
                pool.dma(wuq[:], w_uq.ap()[L].rearrange("(k p) c -> p k c", p=128), writes=[wuq])
                pool.dma(wukv[:], w_ukv.ap()[L].rearrange("(k p) c -> p k c", p=128), writes=[wukv])
                wsrc = w_in.ap()[L].rearrange("(k p) c -> p k c", p=128)
                Wt_of = {}

                def load_w(g):
                    c0, ncol = GROUPS[g]
                    Wt = wb.next()
                    Wt_of[g] = Wt
                    pool.dma([Wt[:, 4 * q:4 * q + 4, 0:ncol] for q in range(4)],
                             [wsrc[:, 4 * q:4 * q + 4, c0:c0 + ncol] for q in range(4)], writes=[Wt])

                def load_tab(src, width):
                    sp.dma(tabb[:, :, 0:width], src.ap().rearrange("(n p) c -> p n c", p=128), writes=[tabb])

                def sq_stats(stg, src_of_h, H, D, st, reads):
                    for h in range(H):
                        stg.append(lambda h=h: act.op(lambda e: e.activation(out=junk[:, 0:D], in_=src_of_h(h), func=AF.Square, accum_out=st[:, h:h + 1]),
                                                      reads=reads, writes=[junk, st]))
                    stg.append(lambda: act.op(lambda e: e.activation(out=st[:, 8:8 + H], in_=st[:, 0:H], func=AF.Sqrt, scale=1.0 / D, bias=EPS),
                                              reads=[st], writes=[st]))

                def sq_stats_sb(stg, X2, X2v, sqA, sqB, st):
                    for half, sq in enumerate((sqA, sqB)):
                        sqv = sq[:, 0:384].rearrange("p (h d) -> p h d", h=2)
                        stg.append(lambda half=half, sq=sq, sqv=sqv: pool.op(
                            lambda e: e.tensor_tensor(out=sqv, in0=X2v[:, 2 * half:2 * half + 2, :], in1=X2v[:, 2 * half:2 * half + 2, :], op=ALU.mult),
                            reads=[X2], writes=[sq]))
                        stg.append(lambda half=half, sq=sq, sqv=sqv: dve.op(
                            lambda e: e.tensor_reduce(out=st[:, 2 * half:2 * half + 2], in_=sqv, axis=AX.X, op=ALU.add), reads=[sq], writes=[st]))
                    stg.append(lambda: act.op(lambda e: e.activation(out=st[:, 8:12], in_=st[:, 0:4], func=AF.Sqrt, scale=1.0 / 192, bias=EPS),
                                              reads=[st], writes=[st]))

                def norm_rope(stg_m, stg_a, X, H, D, st, normw, tap, roff, Dr, O, tA, tB, w_eng=None, t1_eng=None):
                    w_eng = w_eng or pool
                    t1_eng = t1_eng or dve
                    Xv = X[:, 0:H * D].rearrange("p (h d) -> p h d", h=H)
                    Ov = O[:, 0:H * D].rearrange("p (h d) -> p h d", h=H)
                    Av = tA[:, 0:H * Dr].rearrange("p (h d) -> p h d", h=H)
                    Bv = tB[:, 0:H * Dr].rearrange("p (h d) -> p h d", h=H)
                    if normw is not None:
                        stg_m.append(lambda: dve.op(lambda e: e.reciprocal(out=st[:, 8:8 + H], in_=st[:, 8:8 + H]), reads=[st], writes=[st]))
                        stg_m.append(lambda: dve.op(lambda e: e.tensor_tensor(out=Xv, in0=Xv, in1=st[:, 8:8 + H].unsqueeze(2).to_broadcast([128, H, D]), op=ALU.mult),
                                                    reads=[X, st], writes=[X]))
                        stg_m.append(lambda: w_eng.op(lambda e: e.tensor_tensor(out=Xv, in0=Xv, in1=normw[:, 0:D].unsqueeze(1).to_broadcast([128, H, D]), op=ALU.mult),
                                                      reads=[X, normw], writes=[X]))
                    hf = Dr // 2
                    C2 = tap[:, 0:Dr].unsqueeze(1).to_broadcast([128, H, Dr])
                    S2a = tap[:, Dr:Dr + hf].unsqueeze(1).to_broadcast([128, H, hf])
                    S2b = tap[:, Dr + hf:2 * Dr].unsqueeze(1).to_broadcast([128, H, hf])
                    stg_m.append(lambda: t1_eng.op(lambda e: e.tensor_tensor(out=Av, in0=Xv[:, :, roff:roff + Dr], in1=C2, op=ALU.mult),
                                                   reads=[X, tabb], writes=[tA]))
                    stg_m.append(lambda: pool.op(lambda e: e.tensor_tensor(out=Bv[:, :, 0:hf], in0=Xv[:, :, roff + hf:roff + Dr], in1=S2a, op=ALU.mult),
                                                 reads=[X, tabb], writes=[tB]))
                    stg_m.append(lambda: pool.op(lambda e: e.tensor_tensor(out=Bv[:, :, hf:Dr], in0=Xv[:, :, roff:roff + hf], in1=S2b, op=ALU.mult),
                                                 reads=[X, tabb, tB], writes=[tB]))
                    stg_a.append(lambda: dve.op(lambda e: e.tensor_tensor(out=Ov[:, :, roff:roff + Dr], in0=Av, in1=Bv, op=ALU.add),
                                                reads=[tA, tB], writes=[O]))
                    if roff > 0:
                        stg_a.append(lambda: act.op(lambda e: e.activation(out=Ov[:, :, 0:roff], in_=Xv[:, :, 0:roff], func=AF.Copy), reads=[X, O], writes=[O]))

                def main_mm(stg, g, t, Pm):
                    c0, ncol = GROUPS[g]

                    def f():
                        Wt = Wt_of[g]

                        def mm(e):
                            for k in range(16):
                                i = e.matmul(Pm[:, 0:ncol], lhsT=hT.t[:, k, t * 128:(t + 1) * 128], rhs=Wt[:, k, 0:ncol],
                                             start=(k == 0), stop=(k == 15))
                            return i
                        pe.op(mm, reads=[hT.tiles[t], Wt], writes=[Pm])
                    stg.append(f)

                def tr_op(stg, O, H, D, Pt, parts):
                    Ptv = Pt[:, :].rearrange("p (s c) -> p s c", s=8)

                    def f():
                        def tr(e):
                            i = None
                            for h in range(H):
                                for (d0, dn, s0) in parts:
                                    i = e.transpose(out=Ptv[0:dn, s0 + h, :], in_=O[:, h * D + d0:h * D + d0 + dn], identity=ident[:])
                            return i
                        pe.op(tr, reads=[O, ident], writes=[Pt])
                    stg.append(f)
                    return Ptv

                def block_store(g, tb, sta, stb):
                    tc = slice(tb * 512, (tb + 1) * 512)
                    if g == 0:
                        sp.dma(RQT.ap().rearrange("h d t -> d h t")[:, :, tc], sta[:], reads=[sta])
                    elif g == 1:
                        sp.dma(RKT.ap().rearrange("h d t -> d h t")[:, :, tc], sta[:], reads=[sta])
                    elif g == 4:
                        mv = MQT.ap().rearrange("h d t -> d h t")
                        sp.dma(mv[0:128, :, tc], sta[:], reads=[sta])
                        sp.dma(mv[128:192, :, tc], stb[64:128, :, :], reads=[stb])
                    elif g == 5:
                        for pc in range(2):
                            kv_ = KGi[pc].ap().rearrange("(h d) t -> d h t", d=192)
                            sp.dma(kv_[0:128, :, tc], sta[:, 2 * pc:2 * pc + 2, :], reads=[sta])
                            sp.dma(kv_[128:192, :, tc], stb[64:128, 2 * pc:2 * pc + 2, :], reads=[stb])
                    elif g in (6, 7):
                        gv_ = GQT.ap().rearrange("h d t -> d h t")
                        sp.dma(gv_[:, (g - 6) * 4:(g - 6) * 4 + 4, tc], sta[:], reads=[sta])
                    elif g == 8:
                        kg = KGi[2].ap().rearrange("(h d) t -> d h t", d=128)
                        sp.dma(kg[:, :, tc], sta[:, 0:2, :], reads=[sta])

                def simple_tile(pipe, g, t, pmm, pT, sta):
                    j = t % 4
                    tb = t // 4
                    S = pipe.tile(6)
                    rows = slice(t * 128, (t + 1) * 128)
                    cols = slice(j * 128, (j + 1) * 128)
                    Pm = pmm.next()
                    if t == 0:
                        if g + 1 < len(GROUPS):
                            S[0].append(lambda: load_w(g + 1))
                        if g == 1:
                            S[2].append(lambda: load_tab(rt_rk, 256))
                        if g == 6:
                            S[2].append(lambda: load_tab(rt_g, 256))
                    main_mm(S[0], g, t, Pm)
                    if g == 2:
                        V = Vb.next()
                        S[1].append(lambda: act.op(lambda e: e.activation(out=V[:], in_=Pm[:, 0:512], func=AF.Copy), reads=[Pm], writes=[V]))
                        S[1].append(lambda: sp.dma(RV.ap()[rows, :], V[:], reads=[V]))
                        return
                    if g == 3:
                        G = Gb.next()
                        S[1].append(lambda: act.op(lambda e: e.activation(out=G[:], in_=Pm[:, 0:512], func=AF.Silu), reads=[Pm], writes=[G]))
                        S[1].append(lambda: sp.dma(RG.ap()[rows, :], G[:], reads=[G]))
                        return
                    H = 2 if g == 8 else 4
                    X = Xs.next(); O = Os.next(); tA = tAs.next(); tB = tBs.next(); st = sts.next()
                    normw = {0: None, 1: None, 6: wgq, 7: wgq, 8: wgk}[g]
                    S[1].append(lambda: act.op(lambda e: e.activation(out=X[:, 0:H * 128], in_=Pm[:, 0:H * 128], func=AF.Copy), reads=[Pm], writes=[X]))
                    if g == 8:
                        V = Vb.next()
                        S[1].append(lambda: act.op(lambda e: e.activation(out=V[:, 0:256], in_=Pm[:, 256:512], func=AF.Copy), reads=[Pm], writes=[V]))
                        S[1].append(lambda: sp.dma(VGi[t // 8].ap()[(t % 8) * 128:(t % 8 + 1) * 128, 512:768], V[:, 0:256], reads=[V]))
                    if normw is not None:
                        sq_stats(S[1], lambda h: Pm[:, h * 128:(h + 1) * 128], H, 128, st, [Pm])
                    norm_rope(S[2], S[3], X, H, 128, st, normw, tabb[:, t, 0:256], 0, 128, O, tA, tB)
                    if g == 1:
                        S[3].append(lambda: sp.dma(RK.ap()[rows, :], O[:, 0:512], reads=[O]))
                    Pt = pT.next()
                    Ptv = tr_op(S[4], O, H, 128, Pt, [(0, 128, 0)])
                    S[5].append(lambda: dve.op(lambda e: e.tensor_copy(out=sta[:, 0:H, cols], in_=Ptv[:, 0:H, :]), reads=[Pt], writes=[sta]))
                    if j == 3:
                        S[5].append(lambda: block_store(g, tb, sta, None))

                def mla_a_tile(pipe, g, t, pmm, pT):
                    S = pipe.tile(6)
                    Dc = 384 if g == 4 else 256
                    nk = Dc // 128
                    c0 = 0 if g == 4 else 3
                    Pm = pmm.next()
                    if t == 0:
                        S[0].append(lambda: load_w(g + 1))
                    main_mm(S[0], g, t, Pm)
                    X = Xs.next(); O = Os.next(); st = sts.next()
                    S[1].append(lambda: act.op(lambda e: e.activation(out=X[:, 0:Dc], in_=Pm[:, 0:Dc], func=AF.Copy), reads=[Pm], writes=[X]))
                    if g == 5:
                        S[1].append(lambda: act.op(lambda e: e.activation(out=kr_all[:, t, :], in_=Pm[:, 256:320], func=AF.Copy), reads=[Pm], writes=[kr_tiles[t]]))
                    sq_stats(S[1], lambda h: Pm[:, 0:Dc], 1, Dc, st, [Pm])
                    wn = wqa if g == 4 else wkva
                    S[2].append(lambda: dve.op(lambda e: e.reciprocal(out=st[:, 8:9], in_=st[:, 8:9]), reads=[st], writes=[st]))
                    S[2].append(lambda: dve.op(lambda e: e.scalar_tensor_tensor(out=O[:, 0:Dc], in0=X[:, 0:Dc], scalar=st[:, 8:9], in1=wn[:, 0:Dc],
                                                                                op0=ALU.mult, op1=ALU.mult), reads=[X, st, wn], writes=[O]))
                    Pt = pT.next()
                    Ptv = Pt[:, :].rearrange("p (s c) -> p s c", s=8)

                    def trf():
                        def tr(e):
                            for k in range(nk):
                                i = e.transpose(out=Ptv[:, k, :], in_=O[:, k * 128:(k + 1) * 128], identity=ident[:])
                            return i
                        pe.op(tr, reads=[O, ident], writes=[Pt])
                    S[4].append(trf)
                    cb = cTq[t] if g == 4 else cTk[t]
                    S[5].append(lambda: dve.op(lambda e: e.tensor_copy(out=cT_all[:, c0:c0 + nk, t * 128:(t + 1) * 128], in_=Ptv[:, 0:nk, :]), reads=[Pt], writes=[cb]))

                def mla_b_tile(pipe, g, t, pmm, pT, sta, stb):
                    j = t % 4
                    tb = t // 4
                    S = pipe.tile(6)
                    cols = slice(j * 128, (j + 1) * 128)
                    nk = 3 if g == 4 else 2
                    c0 = 0 if g == 4 else 3
                    wsec = wuq if g == 4 else wukv
                    hw = 384 if g == 4 else 512
                    p2 = [pmm.next(), pmm.next()]
                    cb = cTq[t] if g == 4 else cTk[t]
                    if g == 5 and t == 0:
                        S[2].append(lambda: load_tab(rt_m, 128))

                    def mm2f():
                        def mm2(e):
                            for half in range(2):
                                for k in range(nk):
                                    i = e.matmul(p2[half][:, 0:hw], lhsT=cT_all[:, c0 + k, t * 128:(t + 1) * 128], rhs=wsec[:, k, half * hw:(half + 1) * hw],
                                                 start=(k == 0), stop=(k == nk - 1))
                            return i
                        pe.op(mm2, reads=[cb, wsec], writes=[p2[0], p2[1]])
                    S[0].append(mm2f)
                    X2 = Xs.next(); O2 = Os.next(); tA2 = tAs.next(); tB2 = tBs.next(); st2 = sts.next()
                    X2v = X2[:, 0:768].rearrange("p (h d) -> p h d", h=4)
                    if g == 4:
                        for half in range(2):
                            S[1].append(lambda half=half: dve.op(lambda e: e.tensor_copy(
                                out=X2v[:, 2 * half:2 * half + 2, :], in_=p2[half][:, 0:384].rearrange("p (h d) -> p h d", h=2)),
                                reads=[p2[half]], writes=[X2]))
                        wn2 = wmq
                    else:
                        V = Vb.next()
                        Vv = V[:, :].rearrange("p (h e) -> p h e", h=4)
                        for half in range(2):
                            pv = p2[half][:, 0:512].rearrange("p (h two e) -> p h two e", h=2, two=2)
                            S[1].append(lambda half=half, pv=pv: act.op(lambda e: e.activation(out=Vv[:, 2 * half:2 * half + 2, :], in_=pv[:, :, 1, :], func=AF.Copy),
                                                                        reads=[p2[half]], writes=[V]))
                            S[1].append(lambda half=half, pv=pv: dve.op(lambda e: e.tensor_copy(out=X2v[:, 2 * half:2 * half + 2, 0:128], in_=pv[:, :, 0, :]),
                                                                        reads=[p2[half]], writes=[X2]))
                        S[1].append(lambda: sp.dma(VGi[t // 8].ap()[(t % 8) * 128:(t % 8 + 1) * 128, 0:512], V[:], reads=[V]))
                        S[1].append(lambda: pool.op(lambda e: e.tensor_copy(out=X2v[:, :, 128:192], in_=kr_all[:, t, :].unsqueeze(1).to_broadcast([128, 4, 64])),
                                                    reads=[kr_tiles[t], X2], writes=[X2]))
                        wn2 = wmk
                    sq_stats(S[1], lambda h: X2v[:, h, :], 4, 192, st2, [X2])
                    norm_rope(S[2], S[3], X2, 4, 192, st2, wn2, tabb[:, t, 0:128], 128, 64, O2, tA2, tB2, w_eng=pool, t1_eng=pool)
                    Pt2 = pT.next()
                    Ptv2 = tr_op(S[4], O2, 4, 192, Pt2, [(0, 128, 0), (64, 128, 4)])
                    S[5].append(lambda: dve.op(lambda e: e.tensor_copy(out=sta[:, :, cols], in_=Ptv2[:, 0:4, :]), reads=[Pt2], writes=[sta]))
                    S[5].append(lambda: act.op(lambda e: e.activation(out=stb[64:128, :, cols], in_=Ptv2[64:128, 4:8, :], func=AF.Copy), reads=[Pt2], writes=[stb]))
                    if j == 3:
                        S[5].append(lambda: block_store(g, tb, sta, stb))

                load_w(0)
                load_tab(rt_r, 256)
                with ExitStack() as pss:
                    pipe = Pipe()
                    pmm = Rot([fw.psum(pss, f"b_pm{i}", [128, 512], F32) for i in range(4)])
                    pT = Rot([fw.psum(pss, f"b_pT{i}", [128, 1024], BF16) for i in range(3)])
                    for g in range(len(GROUPS)):
                        for tb in range(4):
                            sta = STa.next() if g not in (2, 3, 4, 5) else None
                            for j in range(4):
                                t = tb * 4 + j
                                if g in (4, 5):
                                    mla_a_tile(pipe, g, t, pmm, pT)
                                else:
                                    simple_tile(pipe, g, t, pmm, pT, sta)
                    def issue_gathers():
                        for s_ in fw.sems:
                            if s_.name.startswith("d_sp_") and s_.val > pool.waited.get(s_, 0):
                                pool.e.wait_ge(s_.h, s_.val)
                                pool.waited[s_] = s_.val
                        for p_ in range(3):
                            gather(KGi[p_], KGo_l[L][p_])
                        for p_ in range(2):
                            gather(VGi[p_], VGo_l[L][p_])
                    for g in (5, 4):
                        for tb in range(4):
                            sta = STa.next()
                            stb = STb.next()
                            for j in range(4):
                                mla_b_tile(pipe, g, tb * 4 + j, pmm, pT, sta, stb)
                        if g == 5:
                            pipe.tile(6)[5].append(issue_gathers)
                    pipe.run()
                    fw.barrier()

        def gather(src, dst):
            pool.e.collective_compute("AllGather", ALU.bypass, replica_groups=RG_PAIRS,
                                      ins=[src.ap().opt()], outs=[dst.ap().opt()]).then_inc(cc_sem.h, 1)
            cc_sem.val += 1

        def wait_gathers(upto=None):
            v = cc_sem.val if upto is None else upto
            if sp.waited.get(cc_sem, 0) < v:
                sp.e.wait_ge(cc_sem.h, v)
                sp.waited[cc_sem] = v

        def phase_ret_kv(L, Sf, gct):
            SG_in, SG_out = SG_in_l[L], SG_out_l[L]
            with ExitStack() as ps:
                Kb = Rot([fw.sbuf(ps, f"r_K{i}", [128, 512], BF16) for i in range(8)])
                Vb = Rot([fw.sbuf(ps, f"r_V{i}", [128, 512], BF16) for i in range(8)])
                Kz = Rot([fw.sbuf(ps, f"r_Kz{i}", [128, 512], BF16) for i in range(6)])
                KVt = Rot([fw.sbuf(ps, f"r_KVt{i}", [128, 512], F32) for i in range(6)])
                pkv = Rot([fw.psum(ps, f"r_pkv{i}", [128, 512], F32) for i in range(4)])
                tmp = fw.sbuf(ps, "r_tmp", [128, 8], F32)
                act.op(lambda e: e.activation(out=tmp[:], in_=lg[:], func=AF.Exp, scale=128.0), reads=[lg], writes=[tmp])
                for d in range(2):
                    dve.op(lambda e, d=d: e.tensor_copy(
                        out=gct[:, d, :].rearrange("p (h e) -> p h e", h=4),
                        in_=tmp[:, 4 * d:4 * d + 4].unsqueeze(2).to_broadcast([128, 4, 128])),
                        reads=[tmp], writes=[gct])
                for h in range(4):
                    act.op(lambda e, h=h: e.activation(out=zfb[:, h:h + 1], in_=czc[:, 0:1], func=AF.Exp, scale=lg[:, h:h + 1]),
                           reads=[lg, czc], writes=[zfb])
                    act.op(lambda e, h=h: e.activation(out=zfb[:, 4 + h:5 + h], in_=czc[:, 1:2], func=AF.Exp, scale=lg[:, 4 + h:5 + h]),
                           reads=[lg, czc], writes=[zfb])
                dve.op(lambda e: e.memset(Sf[:], 0.0), writes=[Sf])
                pipe = Pipe()

                def kvtile(i, d):
                    n = i if d == 0 else NT - 1 - i
                    S_ = pipe.tile(5)
                    K = Kb.next(); V = Vb.next(); Z = Kz.next(); KV = KVt.next(); Pk = pkv.next()
                    rows = slice(n * 128, (n + 1) * 128)
                    S_[0].append(lambda: sp.dma(K[:], RK.ap()[rows, :], writes=[K]))
                    S_[0].append(lambda: sp.dma(V[:], RV.ap()[rows, :], writes=[V]))
                    eng = dve if d == 0 else pool
                    S_[1].append(lambda: eng.op(lambda e: e.tensor_tensor(
                        out=Z[:, :].rearrange("p (h d) -> p h d", h=4), in0=K[:, :].rearrange("p (h d) -> p h d", h=4),
                        in1=zfb[:, 4 * d:4 * d + 4].unsqueeze(2).to_broadcast([128, 4, 128]), op=ALU.mult),
                        reads=[K, zfb], writes=[Z]))

                    def mmf():
                        def mm(e):
                            for h in range(4):
                                i_ = e.matmul(Pk[:, h * 128:(h + 1) * 128], lhsT=Z[:, h * 128:(h + 1) * 128], rhs=V[:, h * 128:(h + 1) * 128],
                                              start=True, stop=True)
                            return i_
                        pe.op(mm, reads=[Z, V], writes=[Pk])
                    S_[2].append(mmf)
                    S_[3].append(lambda: act.op(lambda e: e.activation(out=KV[:], in_=Pk[:], func=AF.Copy), reads=[Pk], writes=[KV]))
                    S_[4].append(lambda: sp.dma(KVD.ap()[n, d], KV[:], reads=[KV]))
                    S_[4].append(lambda: dve.op(lambda e: e.tensor_tensor(out=Sf[:, d, :], in0=Sf[:, d, :], in1=gct[:, d, :], op=ALU.mult), reads=[Sf, gct], writes=[Sf]))
                    S_[4].append(lambda: dve.op(lambda e: e.tensor_tensor(out=Sf[:, d, :], in0=Sf[:, d, :], in1=KV[:], op=ALU.add), reads=[Sf, KV], writes=[Sf]))
                for i in range(NT):
                    for d in range(2):
                        kvtile(i, d)
                pipe.run()
                sp.dma(SG_in.ap().rearrange("(d p) c -> p d c", p=128), Sf[:], reads=[Sf])
                fw.barrier()
                gather(SG_in, SG_out)

        def phase_attn(L, mixT, PV, Sf, gct):
            KGo, VGo = KGo_l[L], VGo_l[L]
            SG_out = SG_out_l[L]
            with ExitStack() as ps:
                KTa = Rot([fw.sbuf(ps, f"a_KTa{i}", [128, 2, T], BF16) for i in range(2)])
                KTb = Rot([fw.sbuf(ps, f"a_KTb{i}", [128, 2, T], BF16) for i in range(2)])
                Vt = Rot([fw.sbuf(ps, f"a_V{i}", [128, 32, 128], BF16) for i in range(2)])
                QTa = Rot([fw.sbuf(ps, f"a_QTa{i}", [128, 512], BF16) for i in range(3)])
                QTb = Rot([fw.sbuf(ps, f"a_QTb{i}", [128, 512], BF16) for i in range(3)])
                Pb = Rot([fw.sbuf(ps, f"a_P{i}", [128, 512], BF16) for i in range(6)])
                rsb = Rot([fw.sbuf(ps, f"a_rs{i}", [128, 512], F32) for i in range(2)])
                psc = Rot([fw.psum(ps, f"a_ps{i}", [128, 512], F32) for i in range(4)])
                pob = Rot([fw.psum(ps, f"a_po{i}", [128, 512], F32) for i in range(2)])
                psb = Rot([fw.psum(ps, f"a_pz{i}", [128, 512], F32) for i in range(2)])
                kgo = [k_.ap().rearrange("(b r) t -> r b t", b=2) for k_ in KGo]

                def load_v(V, c0):
                    outs, ins = [], []
                    for b_ in range(2):
                        for i_ in range(2):
                            outs.append(V[:, b_ * 16 + i_ * 8:b_ * 16 + i_ * 8 + 8, :])
                            ins.append(VGo[i_].ap()[b_ * 1024:(b_ + 1) * 1024, c0:c0 + 128].rearrange("(n p) c -> p n c", p=128))
                    sp.dma(outs, ins, writes=[V])
                jobs = []
                for h in range(4):
                    jobs.append(dict(kind="mla", kv=h, qs=[h], scale=192 ** -0.5))
                for kvh in range(2):
                    jobs.append(dict(kind="gqa", kv=kvh, qs=[kvh * 4 + i for i in range(4)], scale=128 ** -0.5))
                def load_kv(ji):
                    jb = jobs[ji]
                    mla = jb["kind"] == "mla"
                    Ka = KTa.next(); V = Vt.next()
                    jb["Ka"], jb["V"] = Ka, V
                    if mla:
                        Kb_ = KTb.next()
                        jb["Kb"] = Kb_
                        kg_ = kgo[jb["kv"] // 2]
                        r0 = (jb["kv"] % 2) * 192
                        sp.dma(Ka[:], kg_[r0:r0 + 128, :, :], writes=[Ka])
                        sp.dma(Kb_[64:128, :, :], kg_[r0 + 128:r0 + 192, :, :], writes=[Kb_])
                        load_v(V, jb["kv"] * 128)
                    else:
                        r0 = jb["kv"] * 128
                        sp.dma(Ka[:], kgo[2][r0:r0 + 128, :, :], writes=[Ka])
                        load_v(V, 512 + jb["kv"] * 128)

                items = [(ji, qh, qb) for ji, jb in enumerate(jobs) for qh in jb["qs"] for qb in range(4)]
                Qof = {}

                def load_q(i):
                    ji, qh, qb = items[i]
                    mla = jobs[ji]["kind"] == "mla"
                    qc = slice(qb * 512, (qb + 1) * 512)
                    Qa = QTa.next()
                    Qb_ = None
                    if mla:
                        Qb_ = QTb.next()
                        sp.dma(Qa[:], MQT.ap()[qh, 0:128, qc], writes=[Qa])
                        sp.dma(Qb_[64:128, :], MQT.ap()[qh, 128:192, qc], writes=[Qb_])
                    else:
                        sp.dma(Qa[:], GQT.ap()[qh, :, qc], writes=[Qa])
                    Qof[i] = (Qa, Qb_)

                Sin = fw.sbuf(ps, "a_Sin", [128, 4, 512], F32)
                kvts = Rot([fw.sbuf(ps, f"a_kvt{i}", [128, 2, 512], F32) for i in range(3)])
                kvof = {}

                def scan_load(i_):
                    kvt = kvts.next()
                    kvof[i_] = kvt
                    sp.dma([kvt[:, 0, :], kvt[:, 1, :]], [KVD.ap()[i_, 0], KVD.ap()[NT - 1 - i_, 1]], writes=[kvt])

                def scan_init():
                    sp.dma(Sin[:], SG_out.ap().rearrange("(b p) c -> p b c", p=128), writes=[Sin])
                    for d in range(2):
                        dve.op(lambda e, d=d: e.tensor_scalar(out=Sf[:, d, :], in0=Sin[:, d, :], scalar1=ssc[:, 2 * d:2 * d + 1], scalar2=None, op0=ALU.mult),
                               reads=[Sin, ssc, Sf], writes=[Sf])
                        dve.op(lambda e, d=d: e.scalar_tensor_tensor(out=Sf[:, d, :], in0=Sin[:, 2 + d, :], scalar=ssc[:, 2 * d + 1:2 * d + 2], in1=Sf[:, d, :],
                                                                     op0=ALU.mult, op1=ALU.add),
                               reads=[Sin, ssc, Sf], writes=[Sf])
                    scan_load(0)

                def scan_step(i_):
                    if i_ + 1 < NT:
                        scan_load(i_ + 1)
                    kvt = kvof.pop(i_)
                    pool.op(lambda e: e.tensor_copy(out=PV.t[:, i_, 0, :], in_=Sf[:, 0, :]), reads=[Sf], writes=[PV.tiles[i_]])
                    pool.op(lambda e: e.tensor_copy(out=PV.t[:, NT - 1 - i_, 1, :], in_=Sf[:, 1, :]), reads=[Sf], writes=[PV.tiles[NT - 1 - i_]])
                    dve.op(lambda e: e.tensor_tensor(out=Sf[:], in0=Sf[:], in1=gct[:], op=ALU.mult), reads=[Sf, gct], writes=[Sf])
                    dve.op(lambda e: e.tensor_tensor(out=Sf[:], in0=Sf[:], in1=kvt[:], op=ALU.add), reads=[Sf, kvt], writes=[Sf])

                assert len(items) == 3 * NT
                load_kv(0)
                load_q(0)
                for i, (ji, qh, qb) in enumerate(items):
                    if i % 3 == 2 and i // 3 < NT:
                        scan_step(i // 3)
                    jb = jobs[ji]
                    mla = jb["kind"] == "mla"
                    first_of_job = (i == 0) or (items[i - 1][0] != ji)
                    if first_of_job and ji + 1 < len(jobs):
                        load_kv(ji + 1)
                    if i + 1 < len(items):
                        load_q(i + 1)
                    if i == 0:
                        wait_gathers()
                        scan_init()
                    Ka, V = jb["Ka"], jb["V"]
                    Kb_ = jb.get("Kb")
                    Qa, Qb_ = Qof.pop(i)
                    chunk = (4 + qh) if mla else (8 + qh)
                    qc = slice(qb * 512, (qb + 1) * 512)
                    po = pob.next(); pz = psb.next()

                    def qk_mm(e, Ps, kt, Ka=Ka, Kb_=Kb_, Qa=Qa, Qb_=Qb_, mla=mla):
                        blk, off = kt // 16, (kt % 16) * 128
                        i_ = e.matmul(Ps[:], lhsT=Ka[:, blk, off:off + 128], rhs=Qa[:], start=True, stop=not mla)
                        if mla:
                            i_ = e.matmul(Ps[:], lhsT=Kb_[64:128, blk, off:off + 128], rhs=Qb_[64:128, :], start=False, stop=True)
                        return i_
                    qk_reads = [Ka, Qa] + ([Kb_, Qb_] if mla else [])

                    def qk(kt):
                        Ps_ = psc.next()
                        pe.op(lambda e: qk_mm(e, Ps_, kt), reads=qk_reads, writes=[Ps_])
                        return Ps_
                    pend = [qk(0), qk(1)]
                    for kt in range(32):
                        Ps = pend.pop(0)
                        if kt + 2 < 32:
                            pend.append(qk(kt + 2))
                        Pt_ = Pb.next()
                        blk = kt // 16
                        act.op(lambda e, Ps=Ps, Pt_=Pt_, blk=blk: e.activation(out=Pt_[:], in_=Ps[:], func=AF.Exp, scale=jb["scale"], bias=kb[:, blk:blk + 1]),
                               reads=[Ps, kb], writes=[Pt_])

                        def pvf(e, kt=kt, Pt_=Pt_, V=V, po=po, pz=pz):
                            e.matmul(po[:], lhsT=V[:, kt, :], rhs=Pt_[:], start=(kt == 0), stop=(kt == 31))
                            return e.matmul(pz[:], lhsT=ones[:], rhs=Pt_[:], start=(kt == 0), stop=(kt == 31))
                        pe.op(pvf, reads=[V, Pt_, ones], writes=[po, pz])
                    rs = rsb.next()
                    dve.op(lambda e: e.reciprocal(out=rs[:], in_=pz[:]), reads=[pz], writes=[rs])
                    dve.op(lambda e: e.tensor_tensor(out=mixT.t[:, chunk, qc], in0=po[:], in1=rs[:], op=ALU.mult),
                           reads=[po, rs], writes=[mixT.chunks[chunk]])
                fw.barrier()

        def phase_ret_out(L, PV, mixT):
            with ExitStack() as ps:
                QTb = Rot([fw.sbuf(ps, f"o_QT{i}", [128, 4, 512], BF16) for i in range(2)])
                KTb = Rot([fw.sbuf(ps, f"o_KT{i}", [128, 4, 512], BF16) for i in range(2)])
                Vb = Rot([fw.sbuf(ps, f"o_V{i}", [128, 512], BF16) for i in range(5)])
                Gb = Rot([fw.sbuf(ps, f"o_G{i}", [128, 512], F32) for i in range(5)])
                Pm = Rot([fw.sbuf(ps, f"o_P{i}", [128, 512], BF16) for i in range(3)])
                Qf = Rot([fw.sbuf(ps, f"o_Qf{i}", [128, 2, 512], BF16) for i in range(3)])
                Yb = Rot([fw.sbuf(ps, f"o_Y{i}", [128, 512], F32) for i in range(4)])
                Sq = Rot([fw.sbuf(ps, f"o_Sq{i}", [128, 512], F32) for i in range(2)])
                Ro = Rot([fw.sbuf(ps, f"o_R{i}", [128, 512], BF16) for i in range(3)])
                stt = Rot([fw.sbuf(ps, f"o_st{i}", [128, 16], F32) for i in range(4)])
                psc = Rot([fw.psum(ps, f"o_ps{i}", [128, 512], F32) for i in range(2)])
                pyb = Rot([fw.psum(ps, f"o_py{i}", [128, 512], F32) for i in range(2)])
                pT = Rot([fw.psum(ps, f"o_pT{i}", [128, 1024], BF16) for i in range(2)])
                cret = fw.sbuf(ps, "o_cret", [128, 6, 128], F32)
                inn = fw.sbuf(ps, "o_inn", [128, 2, 512], F32)
                dtot = fw.sbuf(ps, "o_dtot", [128, 512], F32)
                gnw = fw.sbuf(ps, "o_gnw", [128, 512], F32)
                tm2 = fw.sbuf(ps, "o_tm2", [128, 128], F32)
                sp.dma(cret[:], c_ret.ap(), writes=[cret])
                sp.dma(gnw[:], gn_w.ap()[L, :].partition_broadcast(128), writes=[gnw])
                for h in range(4):
                    act.op(lambda e, h=h: e.activation(out=inn[:, 0, h * 128:(h + 1) * 128], in_=cret[:, 4, :], func=AF.Exp, scale=lg[:, h:h + 1]),
                           reads=[lg, cret], writes=[inn])
                    act.op(lambda e, h=h: e.activation(out=inn[:, 1, h * 128:(h + 1) * 128], in_=cret[:, 5, :], func=AF.Exp, scale=lg[:, 4 + h:5 + h]),
                           reads=[lg, cret], writes=[inn])
                for h in range(4):
                    dsl = dtot[:, h * 128:(h + 1) * 128]
                    act.op(lambda e, h=h, dsl=dsl: e.activation(out=dsl, in_=cret[:, 0, :], func=AF.Exp, scale=lg[:, h:h + 1]),
                           reads=[lg, cret], writes=[dtot])
                    dve.op(lambda e, dsl=dsl: e.tensor_tensor(out=dsl, in0=dsl, in1=cret[:, 1, :], op=ALU.mult), reads=[dtot, cret], writes=[dtot])
                    act.op(lambda e, h=h: e.activation(out=tm2[:], in_=cret[:, 2, :], func=AF.Exp, scale=lg[:, 4 + h:5 + h]),
                           reads=[lg, cret], writes=[tm2])
                    dve.op(lambda e: e.tensor_tensor(out=tm2[:], in0=tm2[:], in1=cret[:, 3, :], op=ALU.mult), reads=[tm2, cret], writes=[tm2])
                    dve.op(lambda e, dsl=dsl: e.tensor_tensor(out=dsl, in0=dsl, in1=tm2[:], op=ALU.add), reads=[dtot, tm2], writes=[dtot])
                pipe = Pipe()

                def tile(n, QT, KT, first):
                    S_ = pipe.tile(9)
                    tb, j = n // 4, n % 4
                    tc = slice(tb * 512, (tb + 1) * 512)
                    rows = slice(n * 128, (n + 1) * 128)
                    cols = slice(j * 128, (j + 1) * 128)
                    V = Vb.next(); G = Gb.next()
                    if first:
                        S_[0].append(lambda: sp.dma(QT[:], RQT.ap().rearrange("h d t -> d h t")[:, :, tc], writes=[QT]))
                        S_[0].append(lambda: sp.dma(KT[:], RKT.ap().rearrange("h d t -> d h t")[:, :, tc], writes=[KT]))
                    S_[0].append(lambda: sp.dma(V[:], RV.ap()[rows, :], writes=[V]))
                    S_[3].append(lambda: sp.dma(G[:], RG.ap()[rows, :], writes=[G]))
                    Ps = psc.next()

                    def scf():
                        def sc(e):
                            for h in range(4):
                                i = e.matmul(Ps[:, h * 128:(h + 1) * 128], lhsT=KT[:, h, cols], rhs=QT[:, h, cols], start=True, stop=True)
                            return i
                        pe.op(sc, reads=[KT, QT], writes=[Ps])
                    S_[1].append(scf)
                    Pmt = Pm.next()
                    S_[2].append(lambda: dve.op(lambda e: e.tensor_tensor(out=Pmt[:], in0=Ps[:], in1=dtot[:], op=ALU.mult), reads=[Ps, dtot], writes=[Pmt]))
                    Q2 = Qf.next()
                    for d in range(2):
                        S_[2].append(lambda d=d: pool.op(lambda e: e.tensor_tensor(
                            out=Q2[:, d, :].rearrange("p (h c) -> p h c", h=4), in0=QT[:, :, cols],
                            in1=inn[:, d, :].rearrange("p (h c) -> p h c", h=4), op=ALU.mult),
                            reads=[QT, inn], writes=[Q2]))
                    Py = pyb.next()

                    def ymf():
                        def ym(e):
                            for h in range(4):
                                hs = slice(h * 128, (h + 1) * 128)
                                e.matmul(Py[:, hs], lhsT=Pmt[:, hs], rhs=V[:, hs], start=True, stop=False)
                                e.matmul(Py[:, hs], lhsT=Q2[:, 0, hs], rhs=PV.t[:, n, 0, hs], start=False, stop=False)
                                i = e.matmul(Py[:, hs], lhsT=Q2[:, 1, hs], rhs=PV.t[:, n, 1, hs], start=False, stop=True)
                            return i
                        pe.op(ym, reads=[Pmt, V, Q2, PV.tiles[n]], writes=[Py])
                    S_[3].append(ymf)
                    Y = Yb.next(); S2 = Sq.next(); st = stt.next(); R = Ro.next()
                    Yv = Y[:, :].rearrange("p (h e) -> p h e", h=4)
                    S2v = S2[:, :].rearrange("p (h e) -> p h e", h=4)
                    S_[4].append(lambda: act.op(lambda e: e.activation(out=Y[:], in_=Py[:], func=AF.Copy), reads=[Py], writes=[Y]))
                    S_[4].append(lambda: dve.op(lambda e: e.tensor_reduce(out=st[:, 0:4], in_=Yv, axis=AX.X, op=ALU.add), reads=[Y], writes=[st]))
                    S_[4].append(lambda: dve.op(lambda e: e.tensor_scalar(out=st[:, 0:4], in0=st[:, 0:4], scalar1=-1.0 / 128, scalar2=None, op0=ALU.mult), reads=[st], writes=[st]))
                    S_[4].append(lambda: dve.op(lambda e: e.tensor_tensor(out=Yv, in0=Yv, in1=st[:, 0:4].unsqueeze(2).to_broadcast([128, 4, 128]), op=ALU.add),
                                                reads=[Y, st], writes=[Y]))
                    S_[5].append(lambda: pool.op(lambda e: e.tensor_tensor(out=S2[:], in0=Y[:], in1=Y[:], op=ALU.mult), reads=[Y], writes=[S2]))
                    S_[5].append(lambda: dve.op(lambda e: e.tensor_reduce(out=st[:, 4:8], in_=S2v, axis=AX.X, op=ALU.add), reads=[S2], writes=[st]))
                    S_[5].append(lambda: act.op(lambda e: e.activation(out=st[:, 8:12], in_=st[:, 4:8], func=AF.Sqrt, scale=1.0 / 128, bias=EPS), reads=[st], writes=[st]))
                    S_[5].append(lambda: dve.op(lambda e: e.reciprocal(out=st[:, 8:12], in_=st[:, 8:12]), reads=[st], writes=[st]))
                    S_[6].append(lambda: dve.op(lambda e: e.tensor_tensor(out=Yv, in0=Yv, in1=st[:, 8:12].unsqueeze(2).to_broadcast([128, 4, 128]), op=ALU.mult),
                                                reads=[Y, st], writes=[Y]))
                    S_[6].append(lambda: pool.op(lambda e: e.tensor_tensor(out=Y[:], in0=Y[:], in1=gnw[:], op=ALU.mult), reads=[Y, gnw], writes=[Y]))
                    S_[6].append(lambda: dve.op(lambda e: e.tensor_tensor(out=R[:], in0=Y[:], in1=G[:], op=ALU.mult), reads=[Y, G], writes=[R]))
                    Pt = pT.next()

                    def trf():
                        def tr(e):
                            for h in range(4):
                                i = e.transpose(out=Pt[:, h * 128:(h + 1) * 128], in_=R[:, h * 128:(h + 1) * 128], identity=ident[:])
                            return i
                        pe.op(tr, reads=[R, ident], writes=[Pt])
                    S_[7].append(trf)
                    S_[8].append(lambda: act.op(lambda e: e.activation(out=mixT.t[:, 0:4, n * 128:(n + 1) * 128],
                                                                       in_=Pt[:, 0:512].rearrange("p (h c) -> p h c", h=4), func=AF.Copy),
                                                reads=[Pt], writes=[mixT.chunks[0]]))
                for tb in range(4):
                    QT = QTb.next(); KT = KTb.next()
                    for j in range(4):
                        tile(tb * 4 + j, QT, KT, j == 0)
                pipe.run()
                fw.barrier()

        def phase_outproj(L, mixT, src, dst):
            with ExitStack() as ps:
                wb = Rot([fw.sbuf(ps, f"d_w{i}", [128, 16, 512], BF16) for i in range(2)])
                xs = Rot([fw.sbuf(ps, f"d_x{i}", [128, 512], F32) for i in range(4)])
                pm = Rot([fw.psum(ps, f"d_pm{i}", [128, 512], F32) for i in range(4)])
                wsrc = w_out.ap()[L].rearrange("(k p) c -> p k c", p=128)
                Wts = {}

                def load_w(cg):
                    Wt = wb.next()
                    Wts[cg] = Wt
                    cc = slice(cg * 512, (cg + 1) * 512)
                    pool.dma([Wt[:, 4 * q:4 * q + 4, :] for q in range(4)], [wsrc[:, 4 * q:4 * q + 4, cc] for q in range(4)], writes=[Wt])
                items = [(cg, t) for cg in range(4) for t in range(NT)]
                Xof = {}

                def load_x(i):
                    cg, t = items[i]
                    X = xs.next()
                    Xof[i] = X
                    sp.dma(X[:], src[t * 128:(t + 1) * 128, cg * 512:(cg + 1) * 512], writes=[X])
                load_w(0)
                load_x(0)
                load_x(1)
                for i, (cg, t) in enumerate(items):
                    if t == 0 and cg + 1 < 4:
                        load_w(cg + 1)
                    if i + 2 < len(items):
                        load_x(i + 2)
                    Wt = Wts[cg]
                    X = Xof.pop(i)
                    Pm_ = pm.next()

                    def mm(e, t=t, Pm_=Pm_, Wt=Wt):
                        for k in range(16):
                            i_ = e.matmul(Pm_[:], lhsT=mixT.t[:, k, t * 128:(t + 1) * 128], rhs=Wt[:, k, :], start=(k == 0), stop=(k == 15))
                        return i_
                    pe.op(mm, reads=list(mixT.chunks) + [Wt], writes=[Pm_])
                    dve.op(lambda e, X=X, Pm_=Pm_: e.tensor_tensor(out=X[:], in0=Pm_[:], in1=X[:], op=ALU.add), reads=[Pm_, X], writes=[X])
                    sp.dma(dst[t * 128:(t + 1) * 128, cg * 512:(cg + 1) * 512], X[:], reads=[X])
                fw.barrier()

        def phase_mlp(L, hT, uT, srcs, dsts):
            with ExitStack() as ps:
                wu = Rot([fw.sbuf(ps, f"f_wu{i}", [128, 16, 128], BF16) for i in range(4)])
                wd = Rot([fw.sbuf(ps, f"f_wd{i}", [128, 16, 512], BF16) for i in range(2)])
                xs = Rot([fw.sbuf(ps, f"f_x{i}", [128, 512], F32) for i in range(4)])
                rl = Rot([fw.sbuf(ps, f"f_rl{i}", [128, 512], F32) for i in range(4)])
                pu = Rot([[fw.psum(ps, f"f_pu{i}{h}", [128, 512], F32) for h in range(2)] for i in range(3)])
                pd = Rot([fw.psum(ps, f"f_pd{i}", [128, 512], F32) for i in range(2)])
                wus = w_up.ap()[L].rearrange("(k p) f -> p k f", p=128)
                wds = w_down.ap()[L].rearrange("(c p) d -> p c d", p=128)
                Wus, Wds = {}, {}

                def load_wu(q):
                    if q >= 64:
                        return
                    Wu = wu.next()
                    Wus[q] = Wu
                    pool.dma(Wu[:], wus[:, :, q * 128:(q + 1) * 128], writes=[Wu])

                def load_wd(q):
                    if q >= 16:
                        return
                    fq, cg = q // 4, q % 4
                    Wd = wd.next()
                    Wds[q] = Wd
                    cc = slice(cg * 512, (cg + 1) * 512)
                    pool.dma([Wd[:, 4 * r:4 * r + 4, :] for r in range(4)],
                             [wds[:, fq * 16 + 4 * r:fq * 16 + 4 * r + 4, cc] for r in range(4)], writes=[Wd])
                load_wu(0)
                load_wu(1)
                load_wu(2)
                for fq in range(4):
                    src, dst = srcs[fq], dsts[fq]
                    for fc in range(16):
                        q = fq * 16 + fc
                        load_wu(q + 3)
                        if fc == 8:
                            load_wd(fq * 4)
                        Wu = Wus.pop(q)
                        for pair in range(2):
                            Pus = pu.next()

                            def mm(e, Wu=Wu, Pus=Pus, pair=pair):
                                for k in range(16):
                                    for h in range(2):
                                        tb = 2 * pair + h
                                        i = e.matmul(Pus[h][:], lhsT=Wu[:, k, :], rhs=hT.t[:, k, tb * 512:(tb + 1) * 512], start=(k == 0), stop=(k == 15))
                                return i
                            pe.op(mm, reads=list(hT.tiles) + [Wu], writes=Pus)
                            for h in range(2):
                                tb = 2 * pair + h
                                dstu = uT.t[:, fc, tb * 512:(tb + 1) * 512]
                                R = rl.next()
                                act.op(lambda e, h=h, R=R: e.activation(out=R[:], in_=Pus[h][:], func=AF.Relu), reads=[Pus[h]], writes=[R])
                                dve.op(lambda e, h=h, R=R, dstu=dstu: e.tensor_tensor(out=dstu, in0=Pus[h][:], in1=R[:], op=ALU.mult),
                                       reads=[Pus[h], R], writes=[uT.chunks[fc]])
                    items = [(cg, t) for cg in range(4) for t in range(NT)]
                    Xof = {}

                    def load_x(i):
                        cg, t = items[i]
                        X = xs.next()
                        Xof[i] = X
                        sp.dma(X[:], src[t * 128:(t + 1) * 128, cg * 512:(cg + 1) * 512], writes=[X])
                    load_x(0)
                    load_x(1)
                    for i, (cg, t) in enumerate(items):
                        if t == 0:
                            load_wd(fq * 4 + cg + 1) if cg + 1 < 4 else None
                        if i + 2 < len(items):
                            load_x(i + 2)
                        Wd = Wds[fq * 4 + cg]
                        X = Xof.pop(i)
                        Pd = pd.next()

                        def mm2(e, t=t, Pd=Pd, Wd=Wd):
                            for c in range(16):
                                i_ = e.matmul(Pd[:], lhsT=uT.t[:, c, t * 128:(t + 1) * 128], rhs=Wd[:, c, :], start=(c == 0), stop=(c == 15))
                            return i_
                        pe.op(mm2, reads=list(uT.chunks) + [Wd], writes=[Pd])
                        dve.op(lambda e, X=X, Pd=Pd: e.tensor_tensor(out=X[:], in0=Pd[:], in1=X[:], op=ALU.add), reads=[Pd, X], writes=[X])
                        sp.dma(dst[t * 128:(t + 1) * 128, cg * 512:(cg + 1) * 512], X[:], reads=[X])
                    fw.barrier()

        def big(stack, name, shape, dtype, nsub, attr):
            b = fw.sbuf(stack, name, shape, dtype)
            setattr(b, attr, [fw.buf(f"{name}_{i}") for i in range(nsub)])
            return b

        def run():
            if stop_here("consts"):
                return
            for L in range(DEPTH):
                x_src = x_in.ap() if L == 0 else XB.ap()
                layer_consts(L)
                if stop_here(f"lconsts{L}"):
                    return
                with ExitStack() as s1:
                    hT = big(s1, "hT", [128, 16, T], BF16, NT, "tiles")
                    phase_norm(x_src, ln1.ap()[L, :], hT)
                    if stop_here(f"norm{L}"):
                        return
                    phase_inproj(L, hT)
                if stop_here(f"inproj{L}") or (STOP_AFTER or "").startswith("inprojG") or (STOP_AFTER or "").startswith("inprojO"):
                    return
                with ExitStack() as s_pv:
                    PV = big(s_pv, "PV", [128, NT, 2, 512], BF16, NT, "tiles")
                    Sf = fw.sbuf(s_pv, "Sf", [128, 2, 512], F32)
                    gct = fw.sbuf(s_pv, "gct", [128, 2, 512], F32)
                    mixT = big(s_pv, "mixT", [128, 16, T], BF16, 16, "chunks")
                    phase_ret_kv(L, Sf, gct)
                    if stop_here(f"retkv{L}"):
                        return
                    wait_gathers(cc_sem.val - 1)
                    phase_attn(L, mixT, PV, Sf, gct)
                    if stop_here(f"attn{L}"):
                        return
                    phase_ret_out(L, PV, mixT)
                    if stop_here(f"retout{L}"):
                        return
                    phase_outproj(L, mixT, x_src, XA.ap())
                if stop_here(f"outproj{L}"):
                    return
                with ExitStack() as s5:
                    hT = big(s5, "h2T", [128, 16, T], BF16, NT, "tiles")
                    uT = big(s5, "uT", [128, 16, T], BF16, 16, "chunks")
                    phase_norm(XA.ap(), ln2.ap()[L, :], hT)
                    last = y_out.ap() if L == DEPTH - 1 else XB.ap()
                    phase_mlp(L, hT, uT, [XA.ap(), XB.ap(), XB.ap(), XB.ap()], [XB.ap(), XB.ap(), XB.ap(), last])
                if stop_here(f"mlp{L}"):
                    return

        run()
        finish()
    return nc


def _rope_tables(pos):
    theta = 10000.0
    pos = pos.astype(np.float32)
    inv = (1.0 / (theta ** (np.arange(0, 128, 2, dtype=np.float32) / 128))).astype(np.float32)
    ang = pos[:, None] * inv[None, :]
    c, s = np.cos(ang).astype(np.float32), np.sin(ang).astype(np.float32)
    rt_r = np.concatenate([c, c, -s, s], axis=1).astype(np.float32)
    rt_rk = (rt_r * np.float32(128 ** -0.5)).astype(np.float32)

    def axial(dim):
        half = dim // 2
        inv = (1.0 / (theta ** (np.arange(0, half, 2, dtype=np.float32) / half))).astype(np.float32)
        row = np.floor(pos / 64).astype(np.float32)
        col = (pos - row * 64).astype(np.float32)
        ang = np.concatenate([row[:, None] * inv[None, :], col[:, None] * inv[None, :]], axis=1)
        c, s = np.cos(ang).astype(np.float32), np.sin(ang).astype(np.float32)
        return np.concatenate([c, c, -s, s], axis=1).astype(np.float32)
    return rt_r, rt_rk, axial(128), axial(64)


def _consts():
    m = np.arange(128, dtype=np.float32)[:, None]
    c = np.arange(128, dtype=np.float32)[None, :]
    A1 = np.maximum(c - m, 0.0)
    M1 = (m <= c).astype(np.float32)
    A2 = np.maximum(m - c, 0.0)
    M2 = (m > c).astype(np.float32)
    io1 = np.broadcast_to(c + 1.0, (128, 128))
    io2 = np.broadcast_to(128.0 - c, (128, 128))
    cret = np.stack([A1, M1, A2, M2, io1, io2], axis=1).astype(np.float32)
    p = np.arange(128, dtype=np.float32)
    czc = np.stack([127.0 - p, p], axis=1).astype(np.float32)
    return cret, czc


_NC_CACHE = {}


def kernel(x_prompt, x_sample, ln1_w, w_in, ret_decay_fwd, ret_decay_bwd, ret_gn_w,
           mla_q_a_norm, mla_w_uq, mla_kv_a_norm, mla_w_ukv, mla_q_norm, mla_k_norm,
           gqa_q_norm, gqa_k_norm, w_out, ln2_w, w_up, w_down):
    f = lambda a: np.ascontiguousarray(np.asarray(a, dtype=np.float32))
    x_prompt, x_sample = f(x_prompt), f(x_sample)
    key = (STOP_AFTER, tuple(DEBUG_OUT), tuple(DEBUG_CORES or []))
    if key not in _NC_CACHE:
        _NC_CACHE[key] = build_program()
    nc = _NC_CACHE[key]
    cret, czc = _consts()
    shared = {
        "w_in": f(w_in), "w_out": f(w_out), "w_up": f(w_up), "w_down": f(w_down),
        "w_uq": f(mla_w_uq), "w_ukv": f(mla_w_ukv), "ln1_w": f(ln1_w), "ln2_w": f(ln2_w),
        "dec_f": f(ret_decay_fwd), "dec_b": f(ret_decay_bwd), "gn_w": f(ret_gn_w),
        "qa_n": f(mla_q_a_norm), "kva_n": f(mla_kv_a_norm), "mq_n": f(mla_q_norm), "mk_n": f(mla_k_norm),
        "gq_n": f(gqa_q_norm), "gk_n": f(gqa_k_norm),
        "c_ident": np.eye(128, dtype=np.float32), "c_ret": cret, "c_zc": czc,
    }
    early = STOP_AFTER is not None and STOP_AFTER.endswith("0") and not STOP_AFTER.startswith("mlp")
    if early:
        shared.pop("w_up"); shared.pop("w_down")
    in_maps = []
    cores = list(range(NCORES)) if DEBUG_CORES is None else list(DEBUG_CORES)
    for c in cores:
        if c < 4:
            xs = x_prompt[c]
            pos0 = 0
            rank = c % 2
            kbias = np.array([[0.0 if rank == 0 else NEG, 0.0 if rank == 1 else NEG]], dtype=np.float32)
            ssc = np.zeros((1, 4), dtype=np.float32)
        else:
            seq = (c - 4) // 2
            rank = c % 2
            pos0 = rank * T
            xs = x_sample[seq, pos0:pos0 + T]
            kbias = np.zeros((1, 2), dtype=np.float32)
            ssc = np.array([[0, 0, 0, 1]] if rank == 0 else [[1, 0, 0, 0]], dtype=np.float32)
        rt_r, rt_rk, rt_g, rt_m = _rope_tables(np.arange(pos0, pos0 + T))
        m = dict(shared)
        m.update({"x": np.ascontiguousarray(xs), "rt_r": rt_r, "rt_rk": rt_rk, "rt_g": rt_g, "rt_m": rt_m,
                  "kbias": kbias, "sscale": ssc})
        in_maps.append(m)
    res = run_bass_kernel_spmd(nc, in_maps, core_ids=list(range(len(cores))))
    kernel.last_results = res.results
    if DEBUG_CORES is not None:
        return None
    ys = [np.asarray(r["y"], dtype=np.float32) for r in res.results]
    y_prompt = np.stack(ys[0:4], axis=0)
    y_sample = np.stack([np.concatenate([ys[4], ys[5]], axis=0), np.concatenate([ys[6], ys[7]], axis=0)], axis=0)
    return (y_prompt, y_sample)
```
